# Optimizing a Trainium2 kernel written in Bass

```python
import math
import jax, jax.numpy as jnp
from jax import lax
import numpy as np

D_MODEL = 1024
BATCH = 8
SEQ = 2048
DEPTH = 1
DEC_BATCH = 32
DEC_SEQ = 4
PAST_LEN = 16384
PAGE_SIZE = 128

N_ATT_HEADS = 4
QK_DIM = 64
V_DIM = 2 * QK_DIM
ATT_WIDTH = N_ATT_HEADS * V_DIM
GMLP_WIDTH = D_MODEL - ATT_WIDTH
N_GMLP_GROUPS = 4
GMLP_GROUP_CH = GMLP_WIDTH // N_GMLP_GROUPS
GMLP_CHUNK = 128
MIX_WIDTH = ATT_WIDTH + GMLP_WIDTH
Q_COLS = N_ATT_HEADS * 2 * QK_DIM
IN_COLS = 2 * Q_COLS + ATT_WIDTH + 2 * GMLP_WIDTH
D_FF = 4 * D_MODEL
Q_BLOCK = 128
EPS = 1e-6

kernel_name = "hymba_diffattn_chunkgmlp_decode_step"


def rms_norm(x, g):
    xf = x.astype(jnp.float32)
    y = xf * lax.rsqrt(jnp.mean(xf * xf, axis=-1, keepdims=True) + EPS)
    return (y * g.astype(jnp.float32)).astype(x.dtype)


def split_proj(hn, w_in, q_gain, k_gain, gv_gain):
    B, T, _ = hn.shape
    z = hn @ w_in
    i1 = Q_COLS
    i2 = 2 * Q_COLS
    i3 = i2 + ATT_WIDTH
    i4 = i3 + GMLP_WIDTH
    q = rms_norm(z[..., :i1].reshape(B, T, N_ATT_HEADS, 2, QK_DIM), q_gain)
    k = rms_norm(z[..., i1:i2].reshape(B, T, N_ATT_HEADS, 2, QK_DIM), k_gain)
    v = z[..., i2:i3].reshape(B, T, N_ATT_HEADS, V_DIM)
    u = jax.nn.gelu(z[..., i3:i4])
    gv = rms_norm(jax.nn.gelu(z[..., i4:]).reshape(B, T, N_GMLP_GROUPS, GMLP_GROUP_CH), gv_gain)
    return q, k, v, u, gv


def prompt_diff_attn(q, k, v, lam):
    B, S = q.shape[0], q.shape[1]
    n_blk = S // Q_BLOCK
    kpos = jnp.arange(S)
    scale = QK_DIM ** -0.5

    def block(i):
        q_blk = lax.dynamic_slice_in_dim(q, i * Q_BLOCK, Q_BLOCK, axis=1)
        s = jnp.einsum('bthcd,bshcd->bhcts', q_blk, k,
                       preferred_element_type=jnp.float32) * scale
        qpos = i * Q_BLOCK + jnp.arange(Q_BLOCK)
        mask = kpos[None, :] <= qpos[:, None]
        p = jax.nn.softmax(jnp.where(mask, s, -jnp.inf), axis=-1)
        a = p[:, :, 0] - lam * p[:, :, 1]
        return jnp.einsum('bhts,bshd->bthd', a.astype(v.dtype), v)

    out = lax.map(block, jnp.arange(n_blk))
    return out.transpose(1, 0, 2, 3, 4).reshape(B, S, N_ATT_HEADS, V_DIM)


def sample_diff_attn(q, k_new, v_new, k_past, v_past, lam):
    T = q.shape[1]
    P = k_past.shape[1]
    scale = QK_DIM ** -0.5
    s_past = jnp.einsum('bthcd,bshcd->bhcts', q, k_past,
                        preferred_element_type=jnp.float32) * scale
    s_new = jnp.einsum('bthcd,bshcd->bhcts', q, k_new,
                       preferred_element_type=jnp.float32) * scale
    causal = jnp.tril(jnp.ones((T, T), dtype=bool))
    s_new = jnp.where(causal, s_new, -jnp.inf)
    p = jax.nn.softmax(jnp.concatenate([s_past, s_new], axis=-1), axis=-1)
    a = (p[:, :, 0] - lam * p[:, :, 1]).astype(v_new.dtype)
    return (jnp.einsum('bhts,bshd->bthd', a[..., :P], v_past)
            + jnp.einsum('bhts,bshd->bthd', a[..., P:], v_new))


def chunk_gating(u, gv, w_s, b_s, chunk):
    B, T = u.shape[0], u.shape[1]
    n_c = T // chunk
    tri = jnp.tril(jnp.ones((chunk, chunk), dtype=bool))
    w = jnp.where(tri, w_s[:, :chunk, :chunk], 0)
    gv_c = gv.reshape(B, n_c, chunk, N_GMLP_GROUPS, GMLP_GROUP_CH)
    s = jnp.einsum('gts,bcsgd->bctgd', w, gv_c) + b_s[:, :chunk].T[None, None, :, :, None]
    return u * s.reshape(B, T, GMLP_WIDTH)


def merge_out(att, mix, subln_gain, lam_init, w_out):
    B, T = att.shape[0], att.shape[1]
    o_att = (rms_norm(att, subln_gain) * (1.0 - lam_init)).reshape(B, T, ATT_WIDTH)
    return jnp.concatenate([o_att, mix], axis=-1) @ w_out


def sq_relu_ffn(h, g, w_up, w_down):
    return jnp.square(jax.nn.relu(rms_norm(h, g) @ w_up)) @ w_down


def setup_inputs(seed: int = 0) -> dict:
    key = jax.random.key(seed)
    ks = jax.random.split(key, 24)
    f32 = jnp.float32
    n_pages = PAST_LEN // PAGE_SIZE
    n_used = DEC_BATCH * n_pages
    n_phys = n_used + max(1, n_used // 4)
    nrm = lambda k, shape, s: jax.random.normal(k, shape, f32) * s
    perm = jax.random.permutation(ks[4], n_phys)
    return {
        "x_prompt": nrm(ks[0], (BATCH, SEQ, D_MODEL), 1.0),
        "x_sample": nrm(ks[1], (DEC_BATCH, DEC_SEQ, D_MODEL), 1.0),
        "cache_k": nrm(ks[2], (DEPTH, n_phys, PAGE_SIZE, N_ATT_HEADS, 2, QK_DIM), 1.0),
        "cache_v": nrm(ks[3], (DEPTH, n_phys, PAGE_SIZE, N_ATT_HEADS, V_DIM), 1.0),
        "page_table": perm[:n_used].reshape(DEC_BATCH, n_pages).astype(jnp.int32),
        "norm_mix": 1.0 + nrm(ks[5], (DEPTH, D_MODEL), 0.02),
        "w_in": nrm(ks[6], (DEPTH, D_MODEL, IN_COLS), D_MODEL ** -0.5),
        "q_gain": 1.0 + nrm(ks[7], (DEPTH, QK_DIM), 0.02),
        "k_gain": 1.0 + nrm(ks[8], (DEPTH, QK_DIM), 0.02),
        "lambda_q1": nrm(ks[9], (DEPTH, QK_DIM), 0.1),
        "lambda_k1": nrm(ks[10], (DEPTH, QK_DIM), 0.1),
        "lambda_q2": nrm(ks[11], (DEPTH, QK_DIM), 0.1),
        "lambda_k2": nrm(ks[12], (DEPTH, QK_DIM), 0.1),
        "subln_gain": 1.0 + nrm(ks[13], (DEPTH, V_DIM), 0.02),
        "gv_gain": 1.0 + nrm(ks[14], (DEPTH, N_GMLP_GROUPS, GMLP_GROUP_CH), 0.02),
        "w_spatial": nrm(ks[15], (DEPTH, N_GMLP_GROUPS, GMLP_CHUNK, GMLP_CHUNK), GMLP_CHUNK ** -0.5),
        "b_spatial": 1.0 + nrm(ks[16], (DEPTH, N_GMLP_GROUPS, GMLP_CHUNK), 0.1),
        "w_out": nrm(ks[17], (DEPTH, MIX_WIDTH, D_MODEL), MIX_WIDTH ** -0.5),
        "norm_ffn": 1.0 + nrm(ks[18], (DEPTH, D_MODEL), 0.02),
        "w_up": nrm(ks[19], (DEPTH, D_MODEL, D_FF), D_MODEL ** -0.5),
        "w_down": nrm(ks[20], (DEPTH, D_FF, D_MODEL), D_FF ** -0.5),
    }


def reference(x_prompt, x_sample, cache_k, cache_v, page_table, norm_mix, w_in,
              q_gain, k_gain, lambda_q1, lambda_k1, lambda_q2, lambda_k2,
              subln_gain, gv_gain, w_spatial, b_spatial, w_out, norm_ffn,
              w_up, w_down):
    f32 = jnp.float32
    n_seq_d, n_pages = page_table.shape
    hp, hs = x_prompt, x_sample
    kp_l, vp_l, ks_l, vs_l, gs_l = [], [], [], [], []
    for l in range(DEPTH):
        lam_init = 0.8 - 0.6 * math.exp(-0.3 * l)
        lam = (jnp.exp(jnp.sum(lambda_q1[l].astype(f32) * lambda_k1[l].astype(f32)))
               - jnp.exp(jnp.sum(lambda_q2[l].astype(f32) * lambda_k2[l].astype(f32)))
               + lam_init)

        q, k, v, u, gv = split_proj(rms_norm(hp, norm_mix[l]), w_in[l],
                                    q_gain[l], k_gain[l], gv_gain[l])
        att = prompt_diff_attn(q, k, v, lam)
        mix = chunk_gating(u, gv, w_spatial[l], b_spatial[l], GMLP_CHUNK)
        hp = hp + merge_out(att, mix, subln_gain[l], lam_init, w_out[l])
        hp = hp + sq_relu_ffn(hp, norm_ffn[l], w_up[l], w_down[l])
        kp_l.append(k)
        vp_l.append(v)

        qs, kns, vns, us, gvs = split_proj(rms_norm(hs, norm_mix[l]), w_in[l],
                                           q_gain[l], k_gain[l], gv_gain[l])
        k_past = cache_k[l][page_table].reshape(
            n_seq_d, n_pages * PAGE_SIZE, N_ATT_HEADS, 2, QK_DIM)
        v_past = cache_v[l][page_table].reshape(
            n_seq_d, n_pages * PAGE_SIZE, N_ATT_HEADS, V_DIM)
        att_s = sample_diff_attn(qs, kns, vns, k_past, v_past, lam)
        mix_s = chunk_gating(us, gvs, w_spatial[l], b_spatial[l], hs.shape[1])
        hs = hs + merge_out(att_s, mix_s, subln_gain[l], lam_init, w_out[l])
        hs = hs + sq_relu_ffn(hs, norm_ffn[l], w_up[l], w_down[l])
        ks_l.append(kns)
        vs_l.append(vns)
        gs_l.append(gvs.reshape(hs.shape[0], hs.shape[1], GMLP_WIDTH))

    return (hp, hs, jnp.stack(kp_l), jnp.stack(vp_l), jnp.stack(ks_l),
            jnp.stack(vs_l), jnp.stack(gs_l))
```

```python
import contextlib
import math
import numpy as np
import concourse.bass as bass
import concourse.mybir as mybir
from concourse.bass_utils import run_bass_kernel_spmd

F32 = mybir.dt.float32
BF16 = mybir.dt.bfloat16
I32 = mybir.dt.int32
AF = mybir.ActivationFunctionType
ALU = mybir.AluOpType
AX = mybir.AxisListType

D = 1024
S = 2048
NT = 16
NS = 16
NTOK = S + NS
H = 4
INC = 2560
DFF = 4096
EPS = 1e-6
PAGE = 128
LAM_INIT = 0.8 - 0.6 * math.exp(-0.3 * 0)


class Buf:
    __slots__ = ("name", "w", "r")

    def __init__(self, name):
        self.name = name
        self.w = None
        self.r = []


class Sync:
    def __init__(self, nc, es):
        self.nc = nc
        self.es = es
        self.eng = {"pe": nc.tensor, "act": nc.scalar, "dve": nc.vector, "pool": nc.gpsimd, "sp": nc.sync}
        self.sem = {k: es.enter_context(nc.semaphore("s_" + k)) for k in self.eng}
        self.cnt = {k: 0 for k in self.eng}
        self.pending = {k: False for k in self.eng}
        self.seen = {k: {} for k in self.eng}
        self.dsems = []
        self.nwaits = 0
        self.defer = False
        self.queue = []

    def flush(self):
        q, self.queue = self.queue, []
        for kind, args, kw in q:
            getattr(self, kind)(*args, **kw)

    def dma_sem(self, name):
        s = self.es.enter_context(self.nc.semaphore(name))
        ent = [s, 0]
        self.dsems.append(ent)
        return ent

    def _wait(self, e, dep):
        kind, key, val = dep
        if kind == "e":
            if key == e and e == "pe":
                return
            sem = self.sem[key]
            sk = "e_" + key
            assert val <= self.cnt[key], (e, key, val, self.cnt[key])
        else:
            sem = key[0]
            sk = id(key)
            val = key[1]
        if self.seen[e].get(sk, 0) >= val:
            return
        self.seen[e][sk] = val
        self.eng[e].wait_ge(sem, val)
        self.nwaits += 1

    def _deps(self, e, reads, writes, same_war=False):
        for b in reads:
            if b.w is not None:
                self._wait(e, b.w)
        for b in writes:
            if b.w is not None:
                self._wait(e, b.w)
            for d in b.r:
                if d[0] == "e" and d[1] == e:
                    continue
                self._wait(e, d)

    def op(self, e, fn, reads=(), writes=(), inc=True):
        if self.defer:
            self.queue.append(("op", (e, fn, list(reads), list(writes), inc), {}))
            return None
        self._deps(e, reads, writes)
        ins = fn()
        if inc:
            self.cnt[e] += 1
            ins.then_inc(self.sem[e], 1)
            idx = self.cnt[e]
            self.pending[e] = False
        else:
            idx = self.cnt[e] + 1
            self.pending[e] = True
        tok = ("e", e, idx)
        for b in reads:
            b.r = [d for d in b.r if not (d[0] == "e" and d[1] == e)] + [tok]
        for b in writes:
            b.w = tok
            b.r = []
        return ins

    def dma(self, q, ent, out, in_, reads=(), writes=(), **kw):
        if self.defer:
            self.queue.append(("dma", (q, ent, out, in_, list(reads), list(writes)), kw))
            return None
        self._deps(q, reads, writes)
        ins = self.eng[q].dma_start(out=out, in_=in_, **kw)
        ent[1] += 16
        ins.then_inc(ent[0], 16)
        tok = ("d", ent, ent[1])
        for b in reads:
            b.r = b.r + [tok]
        for b in writes:
            b.w = tok
            b.r = []
        return ins

    def dma_fn(self, q, ent, fn, reads=(), writes=()):
        self._deps(q, reads, writes)
        ins = fn()
        ent[1] += 16
        ins.then_inc(ent[0], 16)
        tok = ("d", ent, ent[1])
        for b in reads:
            b.r = b.r + [tok]
        for b in writes:
            b.w = tok
            b.r = []
        return ins

    def barrier(self):
        for e in self.eng:
            assert not self.pending[e], e
        for e in self.eng:
            for k in self.eng:
                if k != e and self.cnt[k] > 0:
                    self._wait(e, ("e", k, self.cnt[k]))
            for ent in self.dsems:
                if ent[1] > 0:
                    self._wait(e, ("d", ent, ent[1]))

    def finish(self):
        for ent in self.dsems:
            if ent[1] > 0:
                self._wait("sp", ("d", ent, ent[1]))
        for k in self.eng:
            if k != "sp" and self.cnt[k] > 0:
                self._wait("sp", ("e", k, self.cnt[k]))


def build_nc(n_phys=5120, n_pages=128, stop_after=99):
    nc = bass.Bass("TRN2", target_bir_lowering=False)
    dt = nc.dram_tensor
    xp = dt("x_prompt", [S, D], F32, kind="ExternalInput").ap()
    xs = dt("x_sample", [NS, D], F32, kind="ExternalInput").ap()
    cache_k = dt("cache_k", [n_phys, PAGE, 512], F32, kind="ExternalInput").ap()
    cache_v = dt("cache_v", [n_phys, PAGE, 512], F32, kind="ExternalInput").ap()
    ptab = dt("page_table", [1, 4 * n_pages], I32, kind="ExternalInput").ap()
    norm_mix = dt("norm_mix", [1, D], F32, kind="ExternalInput").ap()
    w_in = dt("w_in", [D, INC], F32, kind="ExternalInput").ap()
    q_gain = dt("q_gain", [1, 64], F32, kind="ExternalInput").ap()
    k_gain = dt("k_gain", [1, 64], F32, kind="ExternalInput").ap()
    lq1 = dt("lambda_q1", [1, 64], F32, kind="ExternalInput").ap()
    lk1 = dt("lambda_k1", [1, 64], F32, kind="ExternalInput").ap()
    lq2 = dt("lambda_q2", [1, 64], F32, kind="ExternalInput").ap()
    lk2 = dt("lambda_k2", [1, 64], F32, kind="ExternalInput").ap()
    subln = dt("subln_gain", [1, 128], F32, kind="ExternalInput").ap()
    gv_gain = dt("gv_gain", [1, 512], F32, kind="ExternalInput").ap()
    w_sp = dt("w_spatial", [4, 128, 128], F32, kind="ExternalInput").ap()
    b_sp = dt("b_spatial", [4, 128], F32, kind="ExternalInput").ap()
    w_out = dt("w_out", [D, D], F32, kind="ExternalInput").ap()
    norm_ffn = dt("norm_ffn", [1, D], F32, kind="ExternalInput").ap()
    w_up = dt("w_up", [D, DFF], F32, kind="ExternalInput").ap()
    w_down = dt("w_down", [DFF, D], F32, kind="ExternalInput").ap()

    y_p = dt("y_prompt", [S, D], F32, kind="ExternalOutput").ap()
    y_s = dt("y_sample", [NS, D], F32, kind="ExternalOutput").ap()
    k_p = dt("k_prompt", [S, 512], F32, kind="ExternalOutput").ap()
    v_p = dt("v_prompt", [S, 512], F32, kind="ExternalOutput").ap()
    k_s = dt("k_sample", [NS, 512], F32, kind="ExternalOutput").ap()
    v_s = dt("v_sample", [NS, 512], F32, kind="ExternalOutput").ap()
    g_s = dt("gv_sample", [NS, 512], F32, kind="ExternalOutput").ap()

    def bc(ap1, shape_steps):
        return bass.AP(ap1.tensor, 0, shape_steps)

    es = contextlib.ExitStack()
    with es:
        Sy = Sync(nc, es)
        op, dma = Sy.op, Sy.dma
        TE, ACT, DVE, POOL = nc.tensor, nc.scalar, nc.vector, nc.gpsimd

        cur = [es]

        def sb(name, shape, dtype, stack=None):
            return (stack or cur[0]).enter_context(nc.sbuf_tensor(name, shape, dtype))

        PS = [es.enter_context(nc.psum_tensor(f"ps{i}", [128, 512], F32)) for i in range(8)]
        PSB = [Buf(f"ps{i}") for i in range(8)]

        def ps_bf(i):
            return PS[i][:].bitcast(BF16)

        identf = sb("identf", [128, 128], F32); B_idf = Buf("identf")
        identb = sb("identb", [128, 128], BF16); B_idb = Buf("identb")
        nhalf = sb("nhalf", [128, 16], F32); B_nh = Buf("nhalf")
        a_stack = contextlib.ExitStack()
        cur[0] = a_stack
        c_sem = Sy.dma_sem("c_sem")
        gmix = sb("gmix", [128, D], F32, a_stack); B_gmix = Buf("gmix")
        qkg = sb("qkg", [128, 2, 8, 64], F32, a_stack); B_qkg = Buf("qkg")
        gvg = sb("gvg", [128, 512], F32, a_stack); B_gvg = Buf("gvg")
        sbl = sb("sbl", [128, 4, 128], F32, a_stack); B_sbl = Buf("sbl")
        lam4 = sb("lam4", [128, 4, 64], F32, a_stack); B_lam4 = Buf("lam4")
        wsp = sb("wsp", [128, 4, 128], F32, a_stack); B_wsp = Buf("wsp")
        bsp = sb("bsp", [128, 4], F32); B_bsp = Buf("bsp")
        bspn = sb("bspn", [4, 128], F32); B_bspn = Buf("bspn")
        c2_sem = Sy.dma_sem("c2_sem")
        bsp_s = sb("bsp_s", [16, 4], F32); B_bsps = Buf("bsp_s")
        wblk_f = sb("wblk_f", [16, 4, 16], F32, a_stack); B_wblkf = Buf("wblk_f")
        ptb = sb("ptb", [1, 4 * n_pages], I32); B_ptb = Buf("ptb")
        wT = sb("wT", [128, 4, 128], BF16); B_wT = Buf("wT")
        wblk = sb("wblk", [16, 4, 16], BF16); B_wblk = Buf("wblk")
        lam = sb("lam", [128, 4], F32); B_lam = Buf("lam")
        ljunk = sb("ljunk", [128, 64], F32)
        msk_s = sb("msk_s", [16, 4, 8, 4], F32, a_stack); B_msk = Buf("msk_s")
        coef = sb("coef", [32, 4], F32); B_coef = Buf("coef")
        oneh = sb("oneh", [32, 4, 128], F32, a_stack); B_oneh = Buf("oneh")
        sel = sb("sel", [32, 4, 16], F32); B_sel = Buf("sel")

        Sy.defer = True
        with nc.allow_non_contiguous_dma(reason="tiny constant loads"):
            dma("sp", c_sem, gmix[:], bc(norm_mix, [[0, 128], [1, D]]), writes=[B_gmix])
            dma("sp", c_sem, qkg[:, 0, :, :], bc(q_gain, [[0, 128], [0, 8], [1, 64]]), writes=[B_qkg])
            dma("sp", c_sem, qkg[:, 1, :, :], bc(k_gain, [[0, 128], [0, 8], [1, 64]]), writes=[B_qkg])
            dma("sp", c_sem, gvg[:], bc(gv_gain, [[0, 128], [1, 512]]), writes=[B_gvg])
            dma("sp", c_sem, sbl[:], bc(subln, [[0, 128], [0, 4], [1, 128]]), writes=[B_sbl])
            for i_, l_ in enumerate((lq1, lk1, lq2, lk2)):
                dma("sp", c_sem, lam4[:, i_, :], bc(l_, [[0, 128], [1, 64]]), writes=[B_lam4])
            dma("sp", c_sem, wsp[:], w_sp.rearrange("g t s -> t g s"), writes=[B_wsp])
            dma("sp", c_sem, bspn[:], b_sp, writes=[B_bspn])

        op("pool", lambda: POOL.memset(identf[:], 0.0), writes=[B_idf])
        op("pool", lambda: POOL.affine_select(out=identf[:], in_=identf[:], pattern=[[-1, 128]],
                                              compare_op=ALU.not_equal, fill=1.0, base=0, channel_multiplier=1),
           reads=[B_idf], writes=[B_idf])
        op("pool", lambda: POOL.tensor_copy(out=identb[:], in_=identf[:]), reads=[B_idf], writes=[B_idb])
        op("pool", lambda: POOL.memset(nhalf[:], -0.5), writes=[B_nh])
        op("pe", lambda: TE.transpose(out=PS[1][:, 0:4], in_=bspn[0:4, :], identity=identf[0:4, 0:4]),
           reads=[B_bspn, B_idf], writes=[PSB[1]], inc=True)
        op("act", lambda: ACT.copy(out=bsp[:], in_=PS[1][:, 0:4]), reads=[PSB[1]], writes=[B_bsp])
        for j in range(4):
            dma("sp", c2_sem, bsp_s[4 * j:4 * j + 4, :], bsp[0:4, :], reads=[B_bsp], writes=[B_bsps])

        ldot = sb("ldot", [128, 2], F32); B_ldot = Buf("ldot")
        op("dve", lambda: DVE.tensor_tensor(out=lam4[:, 0, :], in0=lam4[:, 0, :], in1=lam4[:, 1, :], op=ALU.mult),
           reads=[B_lam4], writes=[B_lam4])
        op("dve", lambda: DVE.tensor_tensor(out=lam4[:, 2, :], in0=lam4[:, 2, :], in1=lam4[:, 3, :], op=ALU.mult),
           reads=[B_lam4], writes=[B_lam4])
        op("dve", lambda: DVE.tensor_reduce(out=ldot[:, 0:1], in_=lam4[:, 0, :], axis=AX.X, op=ALU.add),
           reads=[B_lam4], writes=[B_ldot])
        op("dve", lambda: DVE.tensor_reduce(out=ldot[:, 1:2], in_=lam4[:, 2, :], axis=AX.X, op=ALU.add),
           reads=[B_lam4], writes=[B_ldot])
        op("act", lambda: ACT.activation(out=ldot[:], in_=ldot[:], func=AF.Exp), reads=[B_ldot], writes=[B_ldot])
        op("dve", lambda: DVE.tensor_tensor(out=lam[:, 0:1], in0=ldot[:, 0:1], in1=ldot[:, 1:2], op=ALU.subtract),
           reads=[B_ldot], writes=[B_lam])
        op("dve", lambda: DVE.tensor_scalar(out=lam[:, 0:1], in0=lam[:, 0:1], scalar1=float(LAM_INIT), scalar2=None,
                                            op0=ALU.add), reads=[B_lam], writes=[B_lam])
        op("dve", lambda: DVE.tensor_scalar(out=lam[:, 1:2], in0=lam[:, 0:1], scalar1=-1.0, scalar2=None,
                                            op0=ALU.mult), reads=[B_lam], writes=[B_lam])

        op("dve", lambda: DVE.tensor_scalar(out=sbl[:], in0=sbl[:], scalar1=float(1.0 - LAM_INIT), scalar2=None,
                                            op0=ALU.mult), reads=[B_sbl], writes=[B_sbl])

        op("pool", lambda: POOL.affine_select(out=wsp[:], in_=wsp[:], pattern=[[0, 4], [-1, 128]],
                                              compare_op=ALU.is_ge, fill=0.0, base=0, channel_multiplier=1),
           reads=[B_wsp], writes=[B_wsp])
        for g in range(4):
            op("pe", lambda g=g: TE.transpose(out=PS[0][:, g * 128:(g + 1) * 128], in_=wsp[:, g, :], identity=identf[:]),
               reads=[B_wsp, B_idf], writes=[PSB[0]], inc=(g == 3))
        op("act", lambda: ACT.copy(out=wT[:], in_=PS[0][:].rearrange("p (g t) -> p g t", g=4)),
           reads=[PSB[0]], writes=[B_wT])
        op("pool", lambda: POOL.memset(wblk_f[:], 0.0), writes=[B_wblkf])
        wT4 = sb("wT4", [4, 4, 4], F32); B_wT4 = Buf("wT4")
        op("act", lambda: ACT.copy(out=wT4[:], in_=PS[0][0:4, :].rearrange("p (g t) -> p g t", g=4)[:, :, 0:4]),
           reads=[PSB[0]], writes=[B_wT4])
        for j in range(4):
            dma("sp", c2_sem, wblk_f[4 * j:4 * j + 4, :, 4 * j:4 * j + 4], wT4[:], reads=[B_wT4], writes=[B_wblkf])
        op("pool", lambda: POOL.tensor_copy(out=wblk[:], in_=wblk_f[:]), reads=[B_wblkf], writes=[B_wblk])

        op("pool", lambda: POOL.memset(msk_s[:], 1.0), writes=[B_msk])
        op("pool", lambda: POOL.affine_select(out=msk_s[:], in_=msk_s[:], pattern=[[-4, 4], [0, 8], [0, 4]],
                                              compare_op=ALU.is_ge, fill=0.0, base=0, channel_multiplier=1),
           reads=[B_msk], writes=[B_msk])
        op("pool", lambda: POOL.affine_select(out=msk_s[:], in_=msk_s[:], pattern=[[4, 4], [0, 8], [1, 4]],
                                              compare_op=ALU.is_ge, fill=0.0, base=0, channel_multiplier=-1),
           reads=[B_msk], writes=[B_msk])
        csel = sb("csel", [32, 4, 2], F32); B_csel = Buf("csel")
        op("pool", lambda: POOL.memset(csel[:], 1.0), writes=[B_csel])
        op("pool", lambda: POOL.affine_select(out=csel[:], in_=csel[:], pattern=[[-8, 4], [-4, 2]],
                                              compare_op=ALU.is_ge, fill=0.0, base=0, channel_multiplier=1),
           reads=[B_csel], writes=[B_csel])
        op("pool", lambda: POOL.affine_select(out=csel[:], in_=csel[:], pattern=[[8, 4], [4, 2]],
                                              compare_op=ALU.is_ge, fill=0.0, base=3, channel_multiplier=-1),
           reads=[B_csel], writes=[B_csel])
        cvec = sb("cvec", [32, 2], F32); B_cvec = Buf("cvec")
        op("dve", lambda: DVE.tensor_reduce(out=cvec[:], in_=csel[:].rearrange("p h c -> p c h"), axis=AX.X, op=ALU.add),
           reads=[B_csel], writes=[B_cvec])
        op("dve", lambda: DVE.scalar_tensor_tensor(out=coef[:, 0:1], in0=cvec[:, 1:2], scalar=lam[0:32, 1:2],
                                                   in1=cvec[:, 0:1], op0=ALU.mult, op1=ALU.add),
           reads=[B_cvec, B_lam], writes=[B_coef])
        ohs = sb("ohs", [32, 4], F32); B_ohs = Buf("ohs")
        op("dve", lambda: DVE.tensor_reduce(out=ohs[:], in_=csel[:], axis=AX.X, op=ALU.add), reads=[B_csel], writes=[B_ohs])
        op("dve", lambda: DVE.tensor_copy(out=oneh[:], in_=ohs[:].unsqueeze(2).to_broadcast([32, 4, 128])),
           reads=[B_ohs], writes=[B_oneh])
        op("pool", lambda: POOL.memset(sel[:], 0.0), writes=[B_sel])
        selq = sb("selq", [32, 8, 4, 16], F32, a_stack); B_selq = Buf("selq")
        op("pool", lambda: POOL.memset(selq[:], 1.0), writes=[B_selq])
        op("pool", lambda: POOL.affine_select(out=selq[:], in_=selq[:], pattern=[[4, 8], [-4, 4], [1, 16]],
                                              compare_op=ALU.is_equal, fill=0.0, base=0, channel_multiplier=-1),
           reads=[B_selq], writes=[B_selq])
        op("pool", lambda: POOL.affine_select(out=selq[:], in_=selq[:], pattern=[[-4, 8], [0, 4], [0, 16]],
                                              compare_op=ALU.is_ge, fill=0.0, base=0, channel_multiplier=1),
           reads=[B_selq], writes=[B_selq])
        op("pool", lambda: POOL.affine_select(out=selq[:], in_=selq[:], pattern=[[4, 8], [0, 4], [0, 16]],
                                              compare_op=ALU.is_ge, fill=0.0, base=3, channel_multiplier=-1),
           reads=[B_selq], writes=[B_selq])
        op("dve", lambda: DVE.tensor_reduce(out=sel[:].rearrange("p j m -> p (j m)"),
                                            in_=selq[:].rearrange("p q j m -> p (j m) q"), axis=AX.X, op=ALU.add),
           reads=[B_selq], writes=[B_sel])

        qTz = sb("qTz", [128, 2, H, S], BF16, a_stack); B_qT = [Buf(f"qT{i}") for i in range(NT)]
        op("dve", lambda: DVE.memset(qTz[64:128, 0], 0.0), writes=B_qT)
        op("dve", lambda: DVE.memset(qTz[0:64, 1], 0.0), writes=B_qT)
        kT = sb("kT", [128, H, S], BF16, a_stack); B_kT = [Buf(f"kT{i}") for i in range(NT)]
        qTs = sb("qTs", [128, H, NS], BF16, a_stack); B_qTs = Buf("qTs")
        kTs = sb("kTs", [128, H, NS], BF16, a_stack); B_kTs = Buf("kTs")
        V1 = sb("V1", [128, NT, H, 130], BF16, a_stack); B_V1 = [Buf(f"V1{i}") for i in range(NT)]
        Vs = sb("Vs", [16, H, 130], BF16, a_stack); B_Vs = Buf("Vs")
        mixb = sb("mixb", [128, NT + 1, 512], BF16, a_stack); B_mix = [Buf(f"mix{i}") for i in range(NT + 1)]
        op("pool", lambda: POOL.memset(V1[:, :, :, 128:130], 1.0), writes=B_V1)
        op("pool", lambda: POOL.memset(Vs[:, :, 128:130], 1.0), writes=[B_Vs])

        def tile_rows(i):
            return 128 if i < NT else NS

        def x_src(i):
            return xp[i * 128:(i + 1) * 128, :] if i < NT else xs[:, :]

        def rstd_from_ss(ss_ap, rs_ap, n, width, B_ss, B_rs, ncol):
            op("pool", lambda: POOL.tensor_scalar(out=rs_ap, in0=ss_ap, scalar1=1.0 / width, scalar2=EPS,
                                                  op0=ALU.mult, op1=ALU.add), reads=[B_ss], writes=[B_rs])
            op("pool", lambda: POOL.tensor_tensor(out=rs_ap, in0=rs_ap, in1=nhalf[0:n, 0:ncol], op=ALU.pow),
               reads=[B_rs, B_nh], writes=[B_rs])

        p1 = contextlib.ExitStack()
        with p1:
            win = sb("win", [128, 8, INC], BF16, p1); B_win = [Buf(f"win{k}") for k in range(8)]
            w_sem = Sy.dma_sem("w_sem")
            Sy.defer = False
            for k in range(8):
                for hh in range(2):
                    Sy.dma("pool", w_sem, win[:, k, hh * 1280:(hh + 1) * 1280],
                           w_in[k * 128:(k + 1) * 128, hh * 1280:(hh + 1) * 1280], writes=[B_win[k]])
            with nc.allow_non_contiguous_dma(reason="tiny constant loads"):
                Sy.flush()

            def dbl(name, shape, dtype):
                return ([sb(f"{name}{i}", shape, dtype, p1) for i in range(2)], [Buf(f"{name}{i}") for i in range(2)])

            xt, B_xt = dbl("xt", [128, D], F32)
            x_sem = [Sy.dma_sem(f"x_sem{i}") for i in range(2)]
            junk = sb("junk", [128, D], BF16, p1)
            ss, B_ss = dbl("ss", [128, 1], F32)
            rs, B_rs = dbl("rs", [128, 1], F32)
            hn, B_hn = dbl("hn", [128, D], BF16)
            hnT, B_hnT = dbl("hnT", [128, 8, 128], BF16)
            sq = sb("sq", [128, 2, 512], F32, p1); B_sq = Buf("sq")
            ssq, B_ssq = dbl("ssq", [128, 16], F32)
            rq, B_rq = dbl("rq", [128, 16], F32)
            qkt, B_qkt = dbl("qkt", [128, 2, 8, 64], F32)
            qkb, B_qkb = dbl("qkb", [128, D], BF16)
            vst, B_vst = dbl("vst", [128, 512], F32)
            o_sem = [Sy.dma_sem(f"o_sem{i}") for i in range(2)]
            ug, B_ug = dbl("ug", [128, 512], F32)
            gg, B_gg = dbl("gg", [128, 4, 128], F32)
            gss, B_gss = dbl("gss", [128, 4], F32)
            grs, B_grs = dbl("grs", [128, 4], F32)
            gvb, B_gvb = dbl("gvb", [128, 512], BF16)

            def pre(i):
                n = tile_rows(i); b = i % 2
                if i + 1 <= NT:
                    dma("sp", x_sem[1 - b], xt[1 - b][0:tile_rows(i + 1), :], x_src(i + 1), writes=[B_xt[1 - b]])
                op("act", lambda: ACT.activation(out=junk[0:n, :], in_=xt[b][0:n, :], func=AF.Square,
                                                 accum_out=ss[b][0:n, :]), reads=[B_xt[b]], writes=[B_ss[b]])
                rstd_from_ss(ss[b][0:n, :], rs[b][0:n, :], n, float(D), B_ss[b], B_rs[b], 1)
                op("dve", lambda: DVE.scalar_tensor_tensor(out=hn[b][0:n, :], in0=xt[b][0:n, :], scalar=rs[b][0:n, 0:1],
                                                           in1=gmix[0:n, :], op0=ALU.mult, op1=ALU.mult),
                   reads=[B_xt[b], B_rs[b], B_gmix], writes=[B_hn[b]])

            def TH(i):
                n = tile_rows(i); b = i % 2
                ptr = ps_bf(5).rearrange("p (k t) -> p k t", k=8)
                for k in range(8):
                    op("pe", lambda k=k: TE.transpose(out=ptr[:, k, 0:n], in_=hn[b][0:n, k * 128:(k + 1) * 128],
                                                      identity=identb[0:n, 0:n]),
                       reads=[B_hn[b], B_idb], writes=[PSB[5]], inc=(k == 7))
                op("act", lambda: ACT.copy(out=hnT[b][:, :, 0:n], in_=ptr[:, :, 0:n]), reads=[PSB[5]], writes=[B_hnT[b]])

            def MM(i):
                n = tile_rows(i); b = i % 2
                for cb in (2, 3, 4, 0, 1):
                    for k in range(8):
                        op("pe", lambda cb=cb, k=k: TE.matmul(PS[cb][0:n, :], lhsT=hnT[b][:, k, 0:n],
                                                              rhs=win[:, k, cb * 512:(cb + 1) * 512],
                                                              start=(k == 0), stop=(k == 7)),
                           reads=[B_hnT[b], B_win[k]], writes=[PSB[cb]], inc=(k == 7))

            def back_a(i):
                n = tile_rows(i); b = i % 2
                op("act", lambda: ACT.copy(out=vst[b][0:n, :], in_=PS[2][0:n, :]), reads=[PSB[2]], writes=[B_vst[b]])
                op("act", lambda: ACT.activation(out=ug[b][0:n, :], in_=PS[3][0:n, :], func=AF.Gelu_apprx_tanh),
                   reads=[PSB[3]], writes=[B_ug[b]])
                op("act", lambda: ACT.activation(out=gg[b][0:n].rearrange("p g d -> p (g d)"), in_=PS[4][0:n, :],
                                                 func=AF.Gelu_apprx_tanh), reads=[PSB[4]], writes=[B_gg[b]])
                for cb in range(2):
                    op("act", lambda cb=cb: ACT.activation(out=sq[0:n, cb, :], in_=PS[cb][0:n, :], func=AF.Square),
                       reads=[PSB[cb]], writes=[B_sq])
                for g in range(4):
                    op("act", lambda g=g: ACT.activation(out=junk[0:n, g * 128:(g + 1) * 128], in_=gg[b][0:n, g, :],
                                                         func=AF.Square, accum_out=gss[b][0:n, g:g + 1]),
                       reads=[B_gg[b]], writes=[B_gss[b]])
                op("dve", lambda: DVE.tensor_reduce(out=ssq[b][0:n, :], in_=sq[0:n].rearrange("p a (g d) -> p (a g) d", d=64),
                                                    axis=AX.X, op=ALU.add), reads=[B_sq], writes=[B_ssq[b]])
                rstd_from_ss(ssq[b][0:n, :], rq[b][0:n, :], n, 64.0, B_ssq[b], B_rq[b], 16)
                rstd_from_ss(gss[b][0:n, :], grs[b][0:n, :], n, 128.0, B_gss[b], B_grs[b], 4)
                vdst = v_p[i * 128:(i + 1) * 128, :] if i < NT else v_s[:, :]
                dma("sp", o_sem[b], vdst, vst[b][0:n, :], reads=[B_vst[b]])
                if i < NT:
                    op("dve", lambda: DVE.tensor_copy(out=V1[:, i, :, 0:128],
                                                      in_=vst[b][:, :].rearrange("p (h d) -> p h d", h=4)),
                       reads=[B_vst[b]], writes=[B_V1[i]])
                else:
                    op("dve", lambda: DVE.tensor_copy(out=Vs[:, :, 0:128],
                                                      in_=vst[b][0:n, :].rearrange("p (h d) -> p h d", h=4)),
                       reads=[B_vst[b]], writes=[B_Vs])
                for cb in range(2):
                    op("dve", lambda cb=cb: DVE.tensor_tensor(
                        out=qkt[b][0:n, cb], in0=PS[cb][0:n, :].rearrange("p (g d) -> p g d", d=64),
                        in1=rq[b][0:n, cb * 8:(cb + 1) * 8].unsqueeze(2).to_broadcast([n, 8, 64]), op=ALU.mult),
                       reads=[PSB[cb], B_rq[b]], writes=[B_qkt[b]])
                op("dve", lambda: DVE.tensor_tensor(out=gg[b][0:n], in0=gg[b][0:n],
                                                    in1=grs[b][0:n, :].unsqueeze(2).to_broadcast([n, 4, 128]), op=ALU.mult),
                   reads=[B_gg[b], B_grs[b]], writes=[B_gg[b]])

            def back_b(i):
                n = tile_rows(i); b = i % 2
                op("dve", lambda: DVE.tensor_tensor(out=qkt[b][0:n], in0=qkt[b][0:n], in1=qkg[0:n], op=ALU.mult),
                   reads=[B_qkt[b], B_qkg], writes=[B_qkt[b]])
                op("act", lambda: ACT.copy(out=qkb[b][0:n, :], in_=qkt[b][0:n].rearrange("p a g d -> p (a g d)")),
                   reads=[B_qkt[b]], writes=[B_qkb[b]])
                kdst = k_p[i * 128:(i + 1) * 128, :] if i < NT else k_s[:, :]
                dma("sp", o_sem[b], kdst, qkt[b][0:n, 1].rearrange("p g d -> p (g d)"), reads=[B_qkt[b]])
                op("pool", lambda: POOL.tensor_tensor(out=gg[b][0:n].rearrange("p g d -> p (g d)"),
                                                      in0=gg[b][0:n].rearrange("p g d -> p (g d)"), in1=gvg[0:n, :],
                                                      op=ALU.mult), reads=[B_gg[b], B_gvg], writes=[B_gg[b]])
                if i == NT:
                    dma("sp", o_sem[b], g_s[:, :], gg[b][0:n].rearrange("p g d -> p (g d)"), reads=[B_gg[b]])
                op("act", lambda: ACT.copy(out=gvb[b][0:n, :], in_=gg[b][0:n].rearrange("p g d -> p (g d)")),
                   reads=[B_gg[b]], writes=[B_gvb[b]])
                for g in range(4):
                    lw = wT[:, g, :] if i < NT else wblk[:, g, :]
                    op("pe", lambda g=g, lw=lw: TE.matmul(PS[6][0:n, g * 128:(g + 1) * 128], lhsT=lw,
                                                          rhs=gvb[b][0:n, g * 128:(g + 1) * 128], start=True, stop=True),
                       reads=[B_gvb[b], B_wT, B_wblk], writes=[PSB[6]], inc=(g == 3))
                bb = bsp if i < NT else bsp_s
                for g in range(4):
                    op("dve", lambda g=g: DVE.scalar_tensor_tensor(
                        out=mixb[0:n, i, g * 128:(g + 1) * 128], in0=PS[6][0:n, g * 128:(g + 1) * 128],
                        scalar=bb[0:n, g:g + 1], in1=ug[b][0:n, g * 128:(g + 1) * 128], op0=ALU.add, op1=ALU.mult),
                       reads=[PSB[6], B_bsp, B_bsps, B_ug[b]], writes=[B_mix[i]])
                ptq = ps_bf(7).rearrange("p (k t) -> p k t", k=8)
                for k in range(8):
                    op("pe", lambda k=k: TE.transpose(out=ptq[:, k, 0:n], in_=qkb[b][0:n, k * 128:(k + 1) * 128],
                                                      identity=identb[0:n, 0:n]),
                       reads=[B_qkb[b], B_idb], writes=[PSB[7]], inc=(k == 7))
                if i < NT:
                    op("dve", lambda: DVE.tensor_copy(out=qTz[0:64, 0, :, i * 128:(i + 1) * 128], in_=ptq[0:64, 0:4, :]),
                       reads=[PSB[7]], writes=[B_qT[i]])
                    op("dve", lambda: DVE.tensor_copy(out=qTz[64:128, 1, :, i * 128:(i + 1) * 128], in_=ptq[64:128, 0:4, :]),
                       reads=[PSB[7]], writes=[B_qT[i]])
                    op("act", lambda: ACT.copy(out=kT[:, :, i * 128:(i + 1) * 128], in_=ptq[:, 4:8, :]),
                       reads=[PSB[7]], writes=[B_kT[i]])
                else:
                    op("dve", lambda: DVE.tensor_copy(out=qTs[:, :, :], in_=ptq[:, 0:4, 0:n]),
                       reads=[PSB[7]], writes=[B_qTs])
                    op("act", lambda: ACT.copy(out=kTs[:, :, :], in_=ptq[:, 4:8, 0:n]),
                       reads=[PSB[7]], writes=[B_kTs])

            dma("sp", x_sem[0], xt[0][:, :], x_src(0), writes=[B_xt[0]])
            pre(0)
            TH(0)
            for i in range(NT + 1):
                if i + 1 <= NT:
                    pre(i + 1)
                MM(i)
                if i + 1 <= NT:
                    TH(i + 1)
                if i >= 1:
                    back_b(i - 1)
                back_a(i)
            back_b(NT)
            Sy.barrier()
        B_hd = [Buf(f"hd{i}") for i in range(NT + 1)]

        def y_dst(i):
            return y_p[i * 128:(i + 1) * 128, :] if i < NT else y_s[:, :]

        p2 = contextlib.ExitStack()
        with p2:
            wout = sb("wout", [128, 8, D], BF16, p2); B_wout = Buf("wout")
            wo_sem = Sy.dma_sem("wo_sem")
            for k in range(8):
                dma("pool", wo_sem, wout[:, k, :], w_out[k * 128:(k + 1) * 128, :], writes=[B_wout])
            att = sb("att", [128, 4, H, 128], F32, p2); B_att = [Buf(f"att{r}") for r in range(4)]
            eT = [sb(f"eT{i}", [128, 512], BF16, p2) for i in range(4)]; B_eT = [Buf(f"eT{i}") for i in range(4)]
            rz = [sb(f"rz{i}", [128, 4], F32, p2) for i in range(2)]; B_rz = [Buf(f"rz{i}") for i in range(2)]
            junk2 = sb("junk2", [128, 512], BF16, p2)
            tri01 = sb("tri01", [128, 128], BF16, p2); B_tri = Buf("tri01")
            op("pool", lambda: POOL.memset(tri01[:], 1.0), writes=[B_tri])
            op("pool", lambda: POOL.affine_select(out=tri01[:], in_=tri01[:], pattern=[[1, 128]], compare_op=ALU.is_ge,
                                                  fill=0.0, base=0, channel_multiplier=-1), reads=[B_tri], writes=[B_tri])
            ass = sb("ass", [128, 4], F32, p2); B_ass = Buf("ass")
            ars = sb("ars", [128, 4], F32, p2); B_ars = Buf("ars")
            at1 = sb("at1", [128, H, 128], F32, p2); B_at1 = Buf("at1")
            catb = sb("catb", [128, H, 128], BF16, p2); B_catb = Buf("catb")
            catT = sb("catT", [128, 8, 128], BF16, p2); B_catT = Buf("catT")
            xt2 = [sb(f"xt2{i}", [128, D], F32, p2) for i in range(2)]; B_xt2 = [Buf(f"xt2{i}") for i in range(2)]
            x2_sem = [Sy.dma_sem(f"x2_sem{i}") for i in range(2)]
            hst = [sb(f"hst{i}", [128, D], F32, p2) for i in range(2)]; B_hst = [Buf(f"hst{i}") for i in range(2)]
            h_sem = [Sy.dma_sem(f"h_sem{i}") for i in range(2)]

            def epilogue(i, n, att_ap, B_a):
                b = i % 2
                dma("sp", x2_sem[b], xt2[b][0:n, :], x_src(i), writes=[B_xt2[b]])
                for h in range(H):
                    op("act", lambda h=h: ACT.activation(out=junk2[0:n, h * 128:(h + 1) * 128], in_=att_ap[:, h, :],
                                                         func=AF.Square, accum_out=ass[0:n, h:h + 1]),
                       reads=[B_a], writes=[B_ass])
                rstd_from_ss(ass[0:n, :], ars[0:n, :], n, 128.0, B_ass, B_ars, 4)
                op("dve", lambda: DVE.tensor_tensor(out=at1[0:n], in0=att_ap,
                                                    in1=ars[0:n, :].unsqueeze(2).to_broadcast([n, H, 128]), op=ALU.mult),
                   reads=[B_a, B_ars], writes=[B_at1])
                op("pool", lambda: POOL.tensor_tensor(out=catb[0:n], in0=at1[0:n], in1=sbl[0:n], op=ALU.mult),
                   reads=[B_at1, B_sbl], writes=[B_catb])
                ptr = ps_bf(4).rearrange("p (k t) -> p k t", k=8)
                for k in range(4):
                    op("pe", lambda k=k: TE.transpose(out=ptr[:, k, 0:n], in_=catb[0:n, k, :], identity=identb[0:n, 0:n]),
                       reads=[B_catb, B_idb], writes=[PSB[4]], inc=False)
                for k in range(4):
                    op("pe", lambda k=k: TE.transpose(out=ptr[:, 4 + k, 0:n], in_=mixb[0:n, i, k * 128:(k + 1) * 128],
                                                      identity=identb[0:n, 0:n]),
                       reads=[B_mix[i], B_idb], writes=[PSB[4]], inc=(k == 3))
                op("act", lambda: ACT.copy(out=catT[:, :, 0:n], in_=ptr[:, :, 0:n]), reads=[PSB[4]], writes=[B_catT])
                for cb in range(2):
                    for k in range(8):
                        op("pe", lambda cb=cb, k=k: TE.matmul(PS[cb][0:n, :], lhsT=catT[:, k, 0:n],
                                                              rhs=wout[:, k, cb * 512:(cb + 1) * 512],
                                                              start=(k == 0), stop=(k == 7)),
                           reads=[B_catT, B_wout], writes=[PSB[cb]], inc=(k == 7))
                for cb in range(2):
                    op("dve", lambda cb=cb: DVE.tensor_tensor(out=hst[b][0:n, cb * 512:(cb + 1) * 512], in0=PS[cb][0:n, :],
                                                              in1=xt2[b][0:n, cb * 512:(cb + 1) * 512], op=ALU.add),
                       reads=[PSB[cb], B_xt2[b]], writes=[B_hst[b]])
                dma("sp", h_sem[b], y_dst(i), hst[b][0:n, :], reads=[B_hst[b]], writes=[B_hd[i]])

            if True:
                NBt = 3
                n_tot = 4 * n_pages
                n_tiles = n_tot // 4
                tiles_per_seq = n_pages // 4
                Kb = [sb(f"Kb{i}", [128, 4, 512], BF16, p2) for i in range(NBt)]; B_Kb = [Buf(f"Kb{i}") for i in range(NBt)]
                Vb = [sb(f"Vb{i}", [128, 4, 512], BF16, p2) for i in range(NBt)]; B_Vb = [Buf(f"Vb{i}") for i in range(NBt)]
                kd_sem = [Sy.dma_sem(f"kd_sem{i}") for i in range(NBt)]
                vd_sem = [Sy.dma_sem(f"vd_sem{i}") for i in range(NBt)]
                KTb = [sb(f"KTb{i}", [128, 2, H, 128], BF16, p2) for i in range(2)]; B_KTb = [Buf(f"KTb{i}") for i in range(2)]
                eTd = [sb(f"eTd{i}", [128, 4, 32], BF16, p2) for i in range(2)]; B_eTd = [Buf(f"eTd{i}") for i in range(2)]
                qblk = sb("qblk", [128, 4, H, 8], BF16, p2); B_qblk = Buf("qblk")
                ones2 = sb("ones2", [128, 2], BF16, p2); B_ones2 = Buf("ones2")
                Vsb = sb("Vsb", [16, 512], BF16, p2); B_Vsb = Buf("Vsb")
                en = sb("en", [16, 32], F32, p2); B_en = Buf("en")
                enb = sb("enb", [16, 32], BF16, p2); B_enb = Buf("enb")
                Rr = sb("Rr", [32, H, 128], F32, p2); B_Rr = Buf("Rr")
                zt = sb("zt", [32, 8], F32, p2); B_zt = Buf("zt")
                R2 = sb("R2", [32, H, 128], F32, p2); B_R2 = Buf("R2")
                atts = sb("atts", [16, H, 128], F32, p2); B_atts = Buf("atts")
                R2h = sb("R2h", [32, H, 128], BF16, p2); B_R2h = Buf("R2h")
                R2l = sb("R2l", [32, H, 128], BF16, p2); B_R2l = Buf("R2l")
                selb = sb("selb", [32, 4, 16], BF16, p2); B_selb = Buf("selb")
                op("dve", lambda: DVE.tensor_copy(out=selb[:], in_=sel[:]), reads=[B_sel], writes=[B_selb])
                op("pool", lambda: POOL.memset(ones2[:], 1.0), writes=[B_ones2])
                op("dve", lambda: DVE.tensor_copy(out=Vsb[:, :].rearrange("p (h d) -> p h d", h=H), in_=Vs[:, :, 0:128]),
                   reads=[B_Vs], writes=[B_Vsb])
                op("pool", lambda: POOL.memset(qblk[:], 0.0), writes=[B_qblk])
                for c in range(2):
                    op("dve", lambda c=c: DVE.tensor_copy(
                        out=qblk[c * 64:(c + 1) * 64, :, :, c * 4:(c + 1) * 4],
                        in_=qTs[c * 64:(c + 1) * 64, :, :].rearrange("p h (s t) -> p s h t", s=4)),
                       reads=[B_qTs, B_qblk], writes=[B_qblk])
                ptbc = sb("ptbc", [128, n_tot], I32, p2); B_ptbc = Buf("ptbc")
                ptf = sb("ptf", [128, n_tiles, 4], F32, p2); B_ptf = Buf("ptf")
                m4 = sb("m4", [128, 4], F32, p2); B_m4 = Buf("m4")
                pv = sb("pv", [128, 4], F32, p2); B_pv = Buf("pv")
                pm = sb("pm", [128, 1], F32, p2); B_pm = Buf("pm")
                selp = sb("selp", [128, n_tiles], F32, p2); B_selp = Buf("selp")
                idx4 = sb("idx4", [128, n_tiles], I32, p2); B_idx = Buf("idx4")
                pt_sem = Sy.dma_sem("pt_sem")
                dma("sp", pt_sem, ptbc[:], bc(ptab, [[0, 128], [1, n_tot]]), writes=[B_ptbc])
                op("pool", lambda: POOL.memset(m4[:], 1.0), writes=[B_m4])
                op("pool", lambda: POOL.affine_select(out=m4[:], in_=m4[:], pattern=[[-32, 4]], compare_op=ALU.is_ge,
                                                      fill=0.0, base=0, channel_multiplier=1), reads=[B_m4], writes=[B_m4])
                op("pool", lambda: POOL.affine_select(out=m4[:], in_=m4[:], pattern=[[32, 4]], compare_op=ALU.is_ge,
                                                      fill=0.0, base=31, channel_multiplier=-1), reads=[B_m4], writes=[B_m4])
                op("pool", lambda: POOL.iota(pv[:], pattern=[[-32, 4]], base=0, channel_multiplier=1,
                                             allow_small_or_imprecise_dtypes=True), writes=[B_pv])
                op("dve", lambda: DVE.tensor_tensor(out=pv[:], in0=pv[:], in1=m4[:], op=ALU.mult),
                   reads=[B_pv, B_m4], writes=[B_pv])
                op("dve", lambda: DVE.tensor_reduce(out=pm[:], in_=pv[:], axis=AX.X, op=ALU.add), reads=[B_pv], writes=[B_pm])
                op("dve", lambda: DVE.tensor_copy(out=ptf[:].rearrange("p g q -> p (g q)"), in_=ptbc[:]),
                   reads=[B_ptbc], writes=[B_ptf])
                op("dve", lambda: DVE.tensor_tensor(out=ptf[:], in0=ptf[:],
                                                    in1=m4[:].unsqueeze(1).to_broadcast([128, n_tiles, 4]), op=ALU.mult),
                   reads=[B_ptf, B_m4], writes=[B_ptf])
                op("dve", lambda: DVE.tensor_reduce(out=selp[:], in_=ptf[:], axis=AX.X, op=ALU.add),
                   reads=[B_ptf], writes=[B_selp])
                op("dve", lambda: DVE.tensor_scalar(out=idx4[:], in0=selp[:], scalar1=32.0, scalar2=pm[:, 0:1],
                                                    op0=ALU.mult, op1=ALU.add), reads=[B_selp, B_pm], writes=[B_idx])
                ck_rows = cache_k.rearrange("n (r q) c -> (n r) (q c)", q=4)
                cv_rows = cache_v.rearrange("n (r q) c -> (n r) (q c)", q=4)

                def issue(g_):
                    if g_ >= n_tiles:
                        return
                    sl = g_ % NBt
                    Sy.dma_fn("pool", kd_sem[sl], lambda: POOL.indirect_dma_start(
                        out=Kb[sl][:].rearrange("p s c -> p (s c)"), out_offset=None, in_=ck_rows,
                        in_offset=bass.IndirectOffsetOnAxis(ap=idx4[:, g_:g_ + 1], axis=0)),
                        reads=[B_idx], writes=[B_Kb[sl]])
                    Sy.dma_fn("pool", vd_sem[sl], lambda: POOL.indirect_dma_start(
                        out=Vb[sl][:].rearrange("p s c -> p (s c)"), out_offset=None, in_=cv_rows,
                        in_offset=bass.IndirectOffsetOnAxis(ap=idx4[:, g_:g_ + 1], axis=0)),
                        reads=[B_idx], writes=[B_Vb[sl]])

                accO = sb("accO", [32, 512], F32, p2); B_accO = Buf("accO")
                accZs = sb("accZs", [32, 2], F32, p2); B_accZs = Buf("accZs")
                T_BANK, S_BANK, O_BANK = 5, 6, 7

                def do_Th(g_, half):
                    sl = g_ % NBt
                    ptk = ps_bf(T_BANK).rearrange("p (s h t) -> p s h t", s=2, h=H)
                    for s2 in range(2):
                        for h in range(H):
                            op("pe", lambda: TE.transpose(out=ptk[:, s2, h, :],
                                                          in_=Kb[sl][:, 2 * half + s2, h * 128:(h + 1) * 128],
                                                          identity=identb[:]),
                               reads=[B_Kb[sl], B_idb], writes=[PSB[T_BANK]], inc=(s2 == 1 and h == H - 1))
                    if half == 0:
                        op("act", lambda: ACT.copy(out=KTb[half][:], in_=ptk), reads=[PSB[T_BANK]], writes=[B_KTb[half]])
                    else:
                        op("dve", lambda: DVE.tensor_copy(out=KTb[half][:], in_=ptk), reads=[PSB[T_BANK]],
                           writes=[B_KTb[half]])

                def do_S(g_, half):
                    sidx = g_ // tiles_per_seq
                    for s2 in range(2):
                        s_ = 2 * half + s2
                        for h in range(H):
                            op("pe", lambda: TE.matmul(PS[S_BANK][:, s_ * 32 + h * 8:s_ * 32 + h * 8 + 8],
                                                       lhsT=KTb[half][:, s2, h, :], rhs=qblk[:, sidx, h, :],
                                                       start=True, stop=True),
                               reads=[B_KTb[half], B_qblk], writes=[PSB[S_BANK]], inc=(s2 == 1 and h == H - 1))

                def do_E(g_):
                    e = g_ % 2
                    op("act", lambda: ACT.activation(out=eTd[e][:].rearrange("p g x -> p (g x)"),
                                                     in_=PS[S_BANK][:, 0:128], func=AF.Exp, scale=0.125),
                       reads=[PSB[S_BANK]], writes=[B_eTd[e]])

                def acc_add(first):
                    if first:
                        op("dve", lambda: DVE.tensor_copy(out=accO[:, :], in_=PS[O_BANK][0:32, :]),
                           reads=[PSB[O_BANK]], writes=[B_accO])
                        op("dve", lambda: DVE.tensor_copy(out=accZs[:, :], in_=PS[O_BANK + 0][0:32, 0:2]) if False else
                           DVE.tensor_copy(out=accZs[:, :], in_=PS[S_BANK][0:32, 128:130]),
                           reads=[PSB[S_BANK]], writes=[B_accZs])
                    else:
                        op("dve", lambda: DVE.tensor_tensor(out=accO[:, :], in0=PS[O_BANK][0:32, :], in1=accO[:, :],
                                                            op=ALU.add), reads=[PSB[O_BANK], B_accO], writes=[B_accO])
                        op("dve", lambda: DVE.tensor_tensor(out=accZs[:, :], in0=PS[S_BANK][0:32, 128:130],
                                                            in1=accZs[:, :], op=ALU.add),
                           reads=[PSB[S_BANK], B_accZs], writes=[B_accZs])

                def do_AV(g_):
                    sl = g_ % NBt
                    e = g_ % 2
                    for s_ in range(4):
                        op("pe", lambda: TE.matmul(PS[O_BANK][0:32, :], lhsT=eTd[e][:, s_, :], rhs=Vb[sl][:, s_, :],
                                                   start=(s_ == 0), stop=(s_ == 3), skip_group_check=True),
                           reads=[B_eTd[e], B_Vb[sl]], writes=[PSB[O_BANK]], inc=False)
                        op("pe", lambda: TE.matmul(PS[S_BANK][0:32, 128:130], lhsT=eTd[e][:, s_, :], rhs=ones2[:, :],
                                                   start=False, stop=(s_ == 3), skip_group_check=True),
                           reads=[B_eTd[e], B_ones2], writes=[PSB[S_BANK]], inc=(s_ == 3))
                    acc_add(g_ % tiles_per_seq == 0)

                def seq_tail(sidx):
                    for h in range(H):
                        op("pe", lambda h=h: TE.matmul(PS[S_BANK][0:16, h * 8:(h + 1) * 8], lhsT=kTs[:, h, :],
                                                       rhs=qblk[:, sidx, h, :], start=True, stop=True),
                           reads=[B_kTs, B_qblk], writes=[PSB[S_BANK]], inc=(h == 3))
                    op("act", lambda: ACT.activation(out=en[:, :], in_=PS[S_BANK][0:16, 0:32], func=AF.Exp, scale=0.125),
                       reads=[PSB[S_BANK]], writes=[B_en])
                    op("dve", lambda: DVE.tensor_tensor(out=enb[:, :], in0=en[:, :],
                                                        in1=msk_s[:, sidx].rearrange("p a b -> p (a b)"), op=ALU.mult),
                       reads=[B_en, B_msk], writes=[B_enb])
                    op("pe", lambda: TE.matmul(PS[O_BANK][0:32, :], lhsT=enb[:, :], rhs=Vsb[:, :], start=True, stop=True,
                                               skip_group_check=True), reads=[B_enb, B_Vsb], writes=[PSB[O_BANK]], inc=False)
                    op("pe", lambda: TE.matmul(PS[S_BANK][0:32, 128:130], lhsT=enb[:, :], rhs=ones2[0:16, :],
                                               start=False, stop=True, skip_group_check=True),
                       reads=[B_enb, B_ones2], writes=[PSB[S_BANK]], inc=True)
                    acc_add(False)
                    op("dve", lambda: DVE.reciprocal(out=zt[:, 5:6], in_=accZs[:, 0:1]), reads=[B_accZs], writes=[B_zt])
                    op("dve", lambda: DVE.tensor_tensor(out=zt[:, 6:7], in0=zt[:, 5:6], in1=coef[:, 0:1], op=ALU.mult),
                       reads=[B_zt, B_coef], writes=[B_zt])
                    op("dve", lambda: DVE.scalar_tensor_tensor(out=R2[:], in0=accO[:, :].rearrange("p (a b) -> p a b", a=H),
                                                               scalar=zt[:, 6:7], in1=oneh[:], op0=ALU.mult, op1=ALU.mult),
                       reads=[B_accO, B_zt, B_oneh], writes=[B_R2])
                    op("dve", lambda: DVE.tensor_copy(out=R2h[:], in_=R2[:]), reads=[B_R2], writes=[B_R2h])
                    op("dve", lambda: DVE.tensor_tensor(out=R2l[:], in0=R2[:], in1=R2h[:], op=ALU.subtract),
                       reads=[B_R2, B_R2h], writes=[B_R2l])
                    op("pe", lambda: TE.matmul(PS[O_BANK][0:16, :], lhsT=selb[:, sidx, :],
                                               rhs=R2h[:].rearrange("p a b -> p (a b)"),
                                               start=True, stop=False, skip_group_check=True),
                       reads=[B_R2h, B_selb], writes=[PSB[O_BANK]], inc=False)
                    op("pe", lambda: TE.matmul(PS[O_BANK][0:16, :], lhsT=selb[:, sidx, :],
                                               rhs=R2l[:].rearrange("p a b -> p (a b)"),
                                               start=False, stop=True, skip_group_check=True),
                       reads=[B_R2l, B_selb], writes=[PSB[O_BANK]], inc=True)
                    if sidx == 0:
                        op("dve", lambda: DVE.tensor_copy(out=atts[:].rearrange("p a b -> p (a b)"), in_=PS[O_BANK][0:16, :]),
                           reads=[PSB[O_BANK]], writes=[B_atts])
                    else:
                        op("dve", lambda: DVE.tensor_tensor(out=atts[:].rearrange("p a b -> p (a b)"),
                                                            in0=PS[O_BANK][0:16, :],
                                                            in1=atts[:].rearrange("p a b -> p (a b)"), op=ALU.add),
                           reads=[PSB[O_BANK], B_atts], writes=[B_atts])

                def decode_gen():
                    for g_ in range(NBt - 1):
                        issue(g_)
                    for g_ in range(n_tiles):
                        do_Th(g_, 0)
                        if g_ % tiles_per_seq != 0:
                            do_AV(g_ - 1)
                        issue(g_ + NBt - 1)
                        yield
                        do_Th(g_, 1)
                        do_S(g_, 0)
                        yield
                        do_S(g_, 1)
                        do_E(g_)
                        if (g_ + 1) % tiles_per_seq == 0:
                            yield
                            do_AV(g_)
                            seq_tail(g_ // tiles_per_seq)
                        yield

                dgen = decode_gen()
                n_ticks_total = n_tiles * 3 + 4
                ticks_done = [0]

                def decode_ticks(k):
                    for _ in range(k):
                        if ticks_done[0] >= n_ticks_total:
                            return
                        try:
                            next(dgen)
                        except StopIteration:
                            ticks_done[0] = n_ticks_total
                            return
                        ticks_done[0] += 1

            if stop_after >= 2:
                LOOK = 1
                steps = []
                grp_cnt = 0
                for j in range(4):
                    for h in range(H):
                        for c in range(2):
                            gs = grp_cnt % 2
                            grp_cnt += 1
                            na = 4 * j + 4
                            for a in range(na):
                                steps.append(dict(j=j, h=h, c=c, a=a, gs=gs, last=(a == na - 1),
                                                  lastgrp=(a == na - 1 and h == H - 1 and c == 1)))

                def emit_S(t, st):
                    j, h, c, a = st["j"], st["h"], st["c"], st["a"]
                    r0 = max(0, a - 4 * j)
                    ncols = 512 - 128 * r0
                    sbk = t % 2
                    eb = t % 4
                    st.update(r0=r0, ncols=ncols, sbk=sbk, eb=eb)
                    op("pe", lambda: TE.matmul(
                        PS[sbk][:, 0:ncols], lhsT=kT[:, h, a * 128:(a + 1) * 128],
                        rhs=qTz[:, c, h, j * 512 + r0 * 128:(j + 1) * 512], start=True, stop=True),
                       reads=[B_kT[a]] + B_qT[4 * j:4 * j + 4], writes=[PSB[sbk]], inc=True)
                    op("act", lambda: ACT.activation(out=eT[eb][:, 0:ncols], in_=PS[sbk][:, 0:ncols],
                                                     func=AF.Exp, scale=0.125),
                       reads=[PSB[sbk]], writes=[B_eT[eb]])
                    if a >= 4 * j:
                        op("dve", lambda: DVE.tensor_tensor(out=eT[eb][:, 0:128], in0=eT[eb][:, 0:128], in1=tri01[:, :],
                                                            op=ALU.mult),
                           reads=[B_eT[eb], B_tri], writes=[B_eT[eb]])

                def emit_AV(st):
                    j, h, c, a, gs = st["j"], st["h"], st["c"], st["a"], st["gs"]
                    r0, eb = st["r0"], st["eb"]
                    accs = (PS[2], PS[3])
                    Baccs = (PSB[2], PSB[3])
                    for r in range(r0, 4):
                        bank = accs[r // 2]
                        off = (r % 2) * 130
                        op("pe", lambda: TE.matmul(
                            bank[:, off:off + 130], lhsT=eT[eb][:, (r - r0) * 128:(r - r0 + 1) * 128],
                            rhs=V1[:, a, h, :], start=(a == 0 and r % 2 == 0), stop=(a == 4 * j + r),
                            skip_group_check=True),
                           reads=[B_eT[eb], B_V1[a]], writes=[Baccs[r // 2]], inc=(r == 3))
                    if not st["last"]:
                        return
                    for bi in range(2):
                        zv = accs[bi][:, 0:260].rearrange("p (r x) -> p r x", x=130)[:, :, 128]
                        op("dve", lambda: DVE.reciprocal(out=rz[gs][:, 2 * bi:2 * bi + 2], in_=zv),
                           reads=[Baccs[bi]], writes=[B_rz[gs]])
                    if c == 1:
                        op("dve", lambda: DVE.tensor_scalar(out=rz[gs][:, :], in0=rz[gs][:, :], scalar1=lam[:, 1:2],
                                                            scalar2=None, op0=ALU.mult),
                           reads=[B_rz[gs], B_lam], writes=[B_rz[gs]])
                    for r in range(4):
                        bank = accs[r // 2]
                        off = (r % 2) * 130
                        if c == 0:
                            op("act", lambda: ACT.activation(out=att[:, r, h, :], in_=bank[:, off:off + 128],
                                                             func=AF.Copy, scale=rz[gs][:, r:r + 1]),
                               reads=[Baccs[r // 2], B_rz[gs]], writes=[B_att[r]])
                        else:
                            op("dve", lambda: DVE.scalar_tensor_tensor(
                                out=att[:, r, h, :], in0=bank[:, off:off + 128], scalar=rz[gs][:, r:r + 1],
                                in1=att[:, r, h, :], op0=ALU.mult, op1=ALU.add),
                               reads=[Baccs[r // 2], B_rz[gs], B_att[r]], writes=[B_att[r]])
                    if st["lastgrp"]:
                        for r in range(4):
                            epilogue(4 * j + r, 128, att[:, r], B_att[r])

                for t in range(len(steps) + LOOK):
                    if t < len(steps):
                        emit_S(t, steps[t])
                    if t - LOOK >= 0:
                        emit_AV(steps[t - LOOK])
                    want = (min(t + 1, len(steps)) * n_ticks_total) // len(steps)
                    decode_ticks(want - ticks_done[0])
                decode_ticks(n_ticks_total)
                for _ in dgen:
                    pass
                epilogue(NT, NS, atts[:], B_atts)

            Sy.barrier()
        a_stack.close()
        cur[0] = es

        p3 = contextlib.ExitStack()
        with p3:
            if stop_after >= 4:
                gffn = sb("gffn", [128, D], F32, p3); B_gffn = Buf("gffn")
                g_sem = Sy.dma_sem("g_sem")
                dma("sp", g_sem, gffn[:], bc(norm_ffn, [[0, 128], [1, D]]), writes=[B_gffn])
                hres = sb("hres", [128, NT + 1, D], F32, p3); B_h = [Buf(f"h{i}") for i in range(NT + 1)]
                hl_sem = [Sy.dma_sem(f"hl_sem{i}") for i in range(2)]
                hn2T = sb("hn2T", [128, 8, NTOK], BF16, p3); B_hn2T = [Buf(f"hn2T{i}") for i in range(NT + 1)]
                wu = [sb(f"wu{i}", [128, 8, 1024], BF16, p3) for i in range(2)]; B_wu = [Buf(f"wu{i}") for i in range(2)]
                wd = [sb(f"wd{i}", [128, 8, 1024], BF16, p3) for i in range(2)]; B_wd = [Buf(f"wd{i}") for i in range(2)]
                wu_sem = [Sy.dma_sem(f"wu_sem{i}") for i in range(2)]
                wd_sem = [Sy.dma_sem(f"wd_sem{i}") for i in range(2)]
                hT = [sb(f"hT{i}", [128, 8, 512], BF16, p3) for i in range(2)]; B_hT = [Buf(f"hT{i}") for i in range(2)]
                rl = [sb(f"rl{i}", [128, 512], F32, p3) for i in range(2)]; B_rl = [Buf(f"rl{i}") for i in range(2)]
                ss3 = [sb(f"ss3{i}", [128, 1], F32, p3) for i in range(5)]; B_ss3 = [Buf(f"ss3{i}") for i in range(5)]
                rs3 = [sb(f"rs3{i}", [128, 1], F32, p3) for i in range(5)]; B_rs3 = [Buf(f"rs3{i}") for i in range(5)]
                hn3 = [sb(f"hn3{i}", [128, D], BF16, p3) for i in range(5)]; B_hn3 = [Buf(f"hn3{i}") for i in range(5)]
                junk3 = sb("junk3", [128, D], BF16, p3)
                y_sem = [Sy.dma_sem(f"y_sem{i}") for i in range(2)]

                def load_wu(qq):
                    s_ = qq % 2
                    for k in range(8):
                        dma("pool", wu_sem[s_], wu[s_][:, k, :], w_up[k * 128:(k + 1) * 128, qq * 1024:(qq + 1) * 1024],
                            writes=[B_wu[s_]])

                def load_wd(qq):
                    s_ = qq % 2
                    for f_ in range(8):
                        dma("pool", wd_sem[s_], wd[s_][:, f_, :],
                            w_down[qq * 1024 + f_ * 128:qq * 1024 + (f_ + 1) * 128, :], writes=[B_wd[s_]])

                load_wu(0)
                for i in range(NT + 1):
                    dma("sp", hl_sem[i % 2], hres[0:tile_rows(i), i, :], y_dst(i), reads=[B_hd[i]], writes=[B_h[i]])
                NB3 = 5

                def pro_pre(i):
                    n = tile_rows(i)
                    b = i % NB3
                    op("act", lambda: ACT.activation(out=junk3[0:n, :], in_=hres[0:n, i, :], func=AF.Square,
                                                     accum_out=ss3[b][0:n, :]), reads=[B_h[i]], writes=[B_ss3[b]])
                    rstd_from_ss(ss3[b][0:n, :], rs3[b][0:n, :], n, float(D), B_ss3[b], B_rs3[b], 1)
                    op("dve", lambda: DVE.scalar_tensor_tensor(out=hn3[b][0:n, :], in0=hres[0:n, i, :],
                                                               scalar=rs3[b][0:n, 0:1], in1=gffn[0:n, :],
                                                               op0=ALU.mult, op1=ALU.mult),
                       reads=[B_h[i], B_rs3[b], B_gffn], writes=[B_hn3[b]])

                def pro_T(i):
                    n = tile_rows(i)
                    b = i % NB3
                    pbk = 6 + i % 2
                    ptr = ps_bf(pbk).rearrange("p (k t) -> p k t", k=8)
                    for k in range(8):
                        op("pe", lambda k=k: TE.transpose(out=ptr[:, k, 0:n], in_=hn3[b][0:n, k * 128:(k + 1) * 128],
                                                          identity=identb[0:n, 0:n]),
                           reads=[B_hn3[b], B_idb], writes=[PSB[pbk]], inc=(k == 7))
                    op("act", lambda: ACT.copy(out=hn2T[:, :, i * 128:i * 128 + n], in_=ptr[:, :, 0:n]),
                       reads=[PSB[pbk]], writes=[B_hn2T[i]])

                def blk_tiles(tb):
                    return [4 * tb + r for r in range(4)] if tb < 4 else [NT]

                for i in blk_tiles(0):
                    pro_pre(i)
                load_wd(0)
                for i in blk_tiles(0):
                    pro_T(i)
                pd_cnt = 0
                hb_cnt = 0
                for qq in range(4):
                    s_ = qq % 2
                    for tb in range(5):
                        ntok = 512 if tb < 4 else NS
                        col0 = tb * 512
                        tiles = [4 * tb + r for r in range(4)] if tb < 4 else [NT]
                        if qq == 0 and tb + 1 < 5:
                            for i_ in blk_tiles(tb + 1):
                                pro_pre(i_)
                        hb = hb_cnt % 2
                        hb_cnt += 1
                        for f_ in range(8):
                            pb = f_ % 2
                            for k in range(8):
                                op("pe", lambda k=k: TE.matmul(PS[pb][:, 0:ntok], lhsT=wu[s_][:, k, f_ * 128:(f_ + 1) * 128],
                                                               rhs=hn2T[:, k, col0:col0 + ntok], start=(k == 0), stop=(k == 7)),
                                   reads=[B_wu[s_]] + [B_hn2T[t_] for t_ in tiles], writes=[PSB[pb]], inc=(k == 7))
                            op("act", lambda: ACT.activation(out=rl[pb][:, 0:ntok], in_=PS[pb][:, 0:ntok], func=AF.Relu),
                               reads=[PSB[pb]], writes=[B_rl[pb]])
                            op("dve", lambda: DVE.tensor_tensor(out=hT[hb][:, f_, 0:ntok], in0=rl[pb][:, 0:ntok],
                                                                in1=rl[pb][:, 0:ntok], op=ALU.mult),
                               reads=[B_rl[pb]], writes=[B_hT[hb]])
                        for ti, i in enumerate(tiles):
                            n = tile_rows(i)
                            for cb in range(2):
                                pd = 2 + pd_cnt % 4
                                pd_cnt += 1
                                for f_ in range(8):
                                    op("pe", lambda f_=f_: TE.matmul(PS[pd][0:n, :], lhsT=hT[hb][:, f_, ti * 128:ti * 128 + n],
                                                                     rhs=wd[s_][:, f_, cb * 512:(cb + 1) * 512],
                                                                     start=(f_ == 0), stop=(f_ == 7)),
                                       reads=[B_hT[hb], B_wd[s_]], writes=[PSB[pd]], inc=(f_ == 7))
                                op("dve", lambda: DVE.tensor_tensor(out=hres[0:n, i, cb * 512:(cb + 1) * 512],
                                                                    in0=PS[pd][0:n, :],
                                                                    in1=hres[0:n, i, cb * 512:(cb + 1) * 512], op=ALU.add),
                                   reads=[PSB[pd], B_h[i]], writes=[B_h[i]])
                            if qq == 3:
                                dma("sp", y_sem[i % 2], y_dst(i), hres[0:n, i, :], reads=[B_h[i]], writes=[B_hd[i]])
                        if qq == 0 and tb + 1 < 5:
                            for i_ in blk_tiles(tb + 1):
                                pro_T(i_)
                        if qq == 0 and tb == 0:
                            load_wu(1)
                        if qq == 0 and tb == 1:
                            load_wd(1)
                    if qq + 2 < 4:
                        load_wu(qq + 2)
                        load_wd(qq + 2)

        Sy.finish()
    return nc, Sy


_NC_CACHE = {}


def kernel(**inputs):
    n_cores = 8
    n_phys = inputs["cache_k"].shape[1]
    n_pages = inputs["page_table"].shape[1]
    key = (n_phys, n_pages)
    if key not in _NC_CACHE:
        _NC_CACHE[key] = build_nc(n_phys, n_pages)[0]
    nc = _NC_CACHE[key]
    f = lambda a: np.ascontiguousarray(a)
    ck = f(inputs["cache_k"]).reshape(n_phys, PAGE, 512)
    cv = f(inputs["cache_v"]).reshape(n_phys, PAGE, 512)
    shared = {
        "cache_k": ck, "cache_v": cv,
        "norm_mix": f(inputs["norm_mix"]).reshape(1, D),
        "w_in": f(inputs["w_in"]).reshape(D, INC),
        "q_gain": f(inputs["q_gain"]).reshape(1, 64), "k_gain": f(inputs["k_gain"]).reshape(1, 64),
        "lambda_q1": f(inputs["lambda_q1"]).reshape(1, 64), "lambda_k1": f(inputs["lambda_k1"]).reshape(1, 64),
        "lambda_q2": f(inputs["lambda_q2"]).reshape(1, 64), "lambda_k2": f(inputs["lambda_k2"]).reshape(1, 64),
        "subln_gain": f(inputs["subln_gain"]).reshape(1, 128),
        "gv_gain": f(inputs["gv_gain"]).reshape(1, 512),
        "w_spatial": f(inputs["w_spatial"]).reshape(4, 128, 128),
        "b_spatial": f(inputs["b_spatial"]).reshape(4, 128),
        "w_out": f(inputs["w_out"]).reshape(D, D),
        "norm_ffn": f(inputs["norm_ffn"]).reshape(1, D),
        "w_up": f(inputs["w_up"]).reshape(D, DFF),
        "w_down": f(inputs["w_down"]).reshape(DFF, D),
    }
    in_maps = []
    for c in range(n_cores):
        m = dict(shared)
        m["x_prompt"] = f(inputs["x_prompt"][c])
        m["x_sample"] = f(inputs["x_sample"][4 * c:4 * c + 4]).reshape(NS, D)
        m["page_table"] = f(inputs["page_table"][4 * c:4 * c + 4]).reshape(1, 4 * n_pages).astype(np.int32)
        in_maps.append(m)
    res = run_bass_kernel_spmd(nc, in_maps, core_ids=list(range(n_cores)))
    R = res.results
    cat = lambda name: np.stack([np.asarray(r[name]) for r in R], axis=0)
    y_p = cat("y_prompt").reshape(8, S, D)
    y_s = cat("y_sample").reshape(32, 4, D)
    k_p = cat("k_prompt").reshape(1, 8, S, 4, 2, 64)
    v_p = cat("v_prompt").reshape(1, 8, S, 4, 128)
    k_s = cat("k_sample").reshape(1, 32, 4, 4, 2, 64)
    v_s = cat("v_sample").reshape(1, 32, 4, 4, 128)
    g_s = cat("gv_sample").reshape(1, 32, 4, 512)
    return (y_p, y_s, k_p, v_p, k_s, v_s, g_s)
```

```python
import contextlib
import math
import numpy as np
import concourse.bass as bass
import concourse.mybir as mybir
from concourse.bass_utils import run_bass_kernel_spmd

F32 = mybir.dt.float32
BF16 = mybir.dt.bfloat16
I32 = mybir.dt.int32
AF = mybir.ActivationFunctionType
ALU = mybir.AluOpType
AX = mybir.AxisListType

D = 1024
S = 2048
NT = 16
NS = 16
NTOK = S + NS
H = 4
INC = 2560
DFF = 4096
EPS = 1e-6
PAGE = 128
LAM_INIT = 0.8 - 0.6 * math.exp(-0.3 * 0)


class Buf:
    __slots__ = ("name", "w", "r")

    def __init__(self, name):
        self.name = name
        self.w = None
        self.r = []


class Sync:
    def __init__(self, nc, es):
        self.nc = nc
        self.es = es
        self.eng = {"pe": nc.tensor, "act": nc.scalar, "dve": nc.vector, "pool": nc.gpsimd, "sp": nc.sync}
        self.sem = {k: es.enter_context(nc.semaphore("s_" + k)) for k in self.eng}
        self.cnt = {k: 0 for k in self.eng}
        self.pending = {k: False for k in self.eng}
        self.seen = {k: {} for k in self.eng}
        self.dsems = []
        self.nwaits = 0
        self.defer = False
        self.queue = []

    def flush(self):
        q, self.queue = self.queue, []
        for kind, args, kw in q:
            getattr(self, kind)(*args, **kw)

    def dma_sem(self, name):
        s = self.es.enter_context(self.nc.semaphore(name))
        ent = [s, 0]
        self.dsems.append(ent)
        return ent

    def _wait(self, e, dep):
        kind, key, val = dep
        if kind == "e":
            if key == e and e == "pe":
                return
            sem = self.sem[key]
            sk = "e_" + key
            assert val <= self.cnt[key], (e, key, val, self.cnt[key])
        else:
            sem = key[0]
            sk = id(key)
            val = key[1]
        if self.seen[e].get(sk, 0) >= val:
            return
        self.seen[e][sk] = val
        self.eng[e].wait_ge(sem, val)
        self.nwaits += 1

    def _deps(self, e, reads, writes, same_war=False):
        for b in reads:
            if b.w is not None:
                self._wait(e, b.w)
        for b in writes:
            if b.w is not None:
                self._wait(e, b.w)
            for d in b.r:
                if d[0] == "e" and d[1] == e:
                    continue
                self._wait(e, d)

    def op(self, e, fn, reads=(), writes=(), inc=True):
        if self.defer:
            self.queue.append(("op", (e, fn, list(reads), list(writes), inc), {}))
            return None
        self._deps(e, reads, writes)
        ins = fn()
        if inc:
            self.cnt[e] += 1
            ins.then_inc(self.sem[e], 1)
            idx = self.cnt[e]
            self.pending[e] = False
        else:
            idx = self.cnt[e] + 1
            self.pending[e] = True
        tok = ("e", e, idx)
        for b in reads:
            b.r = [d for d in b.r if not (d[0] == "e" and d[1] == e)] + [tok]
        for b in writes:
            b.w = tok
            b.r = []
        return ins

    def dma(self, q, ent, out, in_, reads=(), writes=(), **kw):
        if self.defer:
            self.queue.append(("dma", (q, ent, out, in_, list(reads), list(writes)), kw))
            return None
        self._deps(q, reads, writes)
        ins = self.eng[q].dma_start(out=out, in_=in_, **kw)
        ent[1] += 16
        ins.then_inc(ent[0], 16)
        tok = ("d", ent, ent[1])
        for b in reads:
            b.r = b.r + [tok]
        for b in writes:
            b.w = tok
            b.r = []
        return ins

    def dma_fn(self, q, ent, fn, reads=(), writes=()):
        self._deps(q, reads, writes)
        ins = fn()
        ent[1] += 16
        ins.then_inc(ent[0], 16)
        tok = ("d", ent, ent[1])
        for b in reads:
            b.r = b.r + [tok]
        for b in writes:
            b.w = tok
            b.r = []
        return ins

    def barrier(self):
        for e in self.eng:
            assert not self.pending[e], e
        for e in self.eng:
            for k in self.eng:
                if k != e and self.cnt[k] > 0:
                    self._wait(e, ("e", k, self.cnt[k]))
            for ent in self.dsems:
                if ent[1] > 0:
                    self._wait(e, ("d", ent, ent[1]))

    def finish(self):
        for ent in self.dsems:
            if ent[1] > 0:
                self._wait("sp", ("d", ent, ent[1]))
        for k in self.eng:
            if k != "sp" and self.cnt[k] > 0:
                self._wait("sp", ("e", k, self.cnt[k]))


def build_nc(n_phys=5120, n_pages=128, stop_after=99):
    nc = bass.Bass("TRN2", target_bir_lowering=False)
    dt = nc.dram_tensor
    xp = dt("x_prompt", [S, D], F32, kind="ExternalInput").ap()
    xs = dt("x_sample", [NS, D], F32, kind="ExternalInput").ap()
    cache_k = dt("cache_k", [n_phys, PAGE, 512], F32, kind="ExternalInput").ap()
    cache_v = dt("cache_v", [n_phys, PAGE, 512], F32, kind="ExternalInput").ap()
    ptab = dt("page_table", [1, 4 * n_pages], I32, kind="ExternalInput").ap()
    norm_mix = dt("norm_mix", [1, D], F32, kind="ExternalInput").ap()
    w_in = dt("w_in", [D, INC], F32, kind="ExternalInput").ap()
    q_gain = dt("q_gain", [1, 64], F32, kind="ExternalInput").ap()
    k_gain = dt("k_gain", [1, 64], F32, kind="ExternalInput").ap()
    lq1 = dt("lambda_q1", [1, 64], F32, kind="ExternalInput").ap()
    lk1 = dt("lambda_k1", [1, 64], F32, kind="ExternalInput").ap()
    lq2 = dt("lambda_q2", [1, 64], F32, kind="ExternalInput").ap()
    lk2 = dt("lambda_k2", [1, 64], F32, kind="ExternalInput").ap()
    subln = dt("subln_gain", [1, 128], F32, kind="ExternalInput").ap()
    gv_gain = dt("gv_gain", [1, 512], F32, kind="ExternalInput").ap()
    w_sp = dt("w_spatial", [4, 128, 128], F32, kind="ExternalInput").ap()
    b_sp = dt("b_spatial", [4, 128], F32, kind="ExternalInput").ap()
    w_out = dt("w_out", [D, D], F32, kind="ExternalInput").ap()
    norm_ffn = dt("norm_ffn", [1, D], F32, kind="ExternalInput").ap()
    w_up = dt("w_up", [D, DFF], F32, kind="ExternalInput").ap()
    w_down = dt("w_down", [DFF, D], F32, kind="ExternalInput").ap()

    y_p = dt("y_prompt", [S, D], F32, kind="ExternalOutput").ap()
    y_s = dt("y_sample", [NS, D], F32, kind="ExternalOutput").ap()
    k_p = dt("k_prompt", [S, 512], F32, kind="ExternalOutput").ap()
    v_p = dt("v_prompt", [S, 512], F32, kind="ExternalOutput").ap()
    k_s = dt("k_sample", [NS, 512], F32, kind="ExternalOutput").ap()
    v_s = dt("v_sample", [NS, 512], F32, kind="ExternalOutput").ap()
    g_s = dt("gv_sample", [NS, 512], F32, kind="ExternalOutput").ap()

    def bc(ap1, shape_steps):
        return bass.AP(ap1.tensor, 0, shape_steps)

    es = contextlib.ExitStack()
    with es:
        Sy = Sync(nc, es)
        op, dma = Sy.op, Sy.dma
        TE, ACT, DVE, POOL = nc.tensor, nc.scalar, nc.vector, nc.gpsimd

        cur = [es]

        def sb(name, shape, dtype, stack=None):
            return (stack or cur[0]).enter_context(nc.sbuf_tensor(name, shape, dtype))

        PS = [es.enter_context(nc.psum_tensor(f"ps{i}", [128, 512], F32)) for i in range(8)]
        PSB = [Buf(f"ps{i}") for i in range(8)]

        def ps_bf(i):
            return PS[i][:].bitcast(BF16)

        identf = sb("identf", [128, 128], F32); B_idf = Buf("identf")
        identb = sb("identb", [128, 128], BF16); B_idb = Buf("identb")
        nhalf = sb("nhalf", [128, 16], F32); B_nh = Buf("nhalf")
        a_stack = contextlib.ExitStack()
        cur[0] = a_stack
        c_sem = Sy.dma_sem("c_sem")
        gmix = sb("gmix", [128, D], F32, a_stack); B_gmix = Buf("gmix")
        qkg = sb("qkg", [128, 2, 8, 64], F32, a_stack); B_qkg = Buf("qkg")
        gvg = sb("gvg", [128, 512], F32, a_stack); B_gvg = Buf("gvg")
        sbl = sb("sbl", [128, 4, 128], F32, a_stack); B_sbl = Buf("sbl")
        lam4 = sb("lam4", [128, 4, 64], F32, a_stack); B_lam4 = Buf("lam4")
        wsp = sb("wsp", [128, 4, 128], F32, a_stack); B_wsp = Buf("wsp")
        bsp = sb("bsp", [128, 4], F32); B_bsp = Buf("bsp")
        bspn = sb("bspn", [4, 128], F32); B_bspn = Buf("bspn")
        c2_sem = Sy.dma_sem("c2_sem")
        bsp_s = sb("bsp_s", [16, 4], F32); B_bsps = Buf("bsp_s")
        wblk_f = sb("wblk_f", [16, 4, 16], F32, a_stack); B_wblkf = Buf("wblk_f")
        ptb = sb("ptb", [1, 4 * n_pages], I32); B_ptb = Buf("ptb")
        wT = sb("wT", [128, 4, 128], BF16); B_wT = Buf("wT")
        wblk = sb("wblk", [16, 4, 16], BF16); B_wblk = Buf("wblk")
        lam = sb("lam", [128, 4], F32); B_lam = Buf("lam")
        ljunk = sb("ljunk", [128, 64], F32)
        msk_s = sb("msk_s", [16, 4, 8, 4], F32, a_stack); B_msk = Buf("msk_s")
        coef = sb("coef", [32, 4], F32); B_coef = Buf("coef")
        oneh = sb("oneh", [32, 4, 128], F32, a_stack); B_oneh = Buf("oneh")
        sel = sb("sel", [32, 4, 16], F32); B_sel = Buf("sel")

        Sy.defer = True
        with nc.allow_non_contiguous_dma(reason="tiny constant loads"):
            dma("sp", c_sem, gmix[:], bc(norm_mix, [[0, 128], [1, D]]), writes=[B_gmix])
            dma("sp", c_sem, qkg[:, 0, :, :], bc(q_gain, [[0, 128], [0, 8], [1, 64]]), writes=[B_qkg])
            dma("sp", c_sem, qkg[:, 1, :, :], bc(k_gain, [[0, 128], [0, 8], [1, 64]]), writes=[B_qkg])
            dma("sp", c_sem, gvg[:], bc(gv_gain, [[0, 128], [1, 512]]), writes=[B_gvg])
            dma("sp", c_sem, sbl[:], bc(subln, [[0, 128], [0, 4], [1, 128]]), writes=[B_sbl])
            for i_, l_ in enumerate((lq1, lk1, lq2, lk2)):
                dma("sp", c_sem, lam4[:, i_, :], bc(l_, [[0, 128], [1, 64]]), writes=[B_lam4])
            dma("sp", c_sem, wsp[:], w_sp.rearrange("g t s -> t g s"), writes=[B_wsp])
            dma("sp", c_sem, bspn[:], b_sp, writes=[B_bspn])

        op("pool", lambda: POOL.memset(identf[:], 0.0), writes=[B_idf])
        op("pool", lambda: POOL.affine_select(out=identf[:], in_=identf[:], pattern=[[-1, 128]],
                                              compare_op=ALU.not_equal, fill=1.0, base=0, channel_multiplier=1),
           reads=[B_idf], writes=[B_idf])
        op("pool", lambda: POOL.tensor_copy(out=identb[:], in_=identf[:]), reads=[B_idf], writes=[B_idb])
        op("pool", lambda: POOL.memset(nhalf[:], -0.5), writes=[B_nh])
        op("pe", lambda: TE.transpose(out=PS[1][:, 0:4], in_=bspn[0:4, :], identity=identf[0:4, 0:4]),
           reads=[B_bspn, B_idf], writes=[PSB[1]], inc=True)
        op("act", lambda: ACT.copy(out=bsp[:], in_=PS[1][:, 0:4]), reads=[PSB[1]], writes=[B_bsp])
        for j in range(4):
            dma("sp", c2_sem, bsp_s[4 * j:4 * j + 4, :], bsp[0:4, :], reads=[B_bsp], writes=[B_bsps])

        ldot = sb("ldot", [128, 2], F32); B_ldot = Buf("ldot")
        op("dve", lambda: DVE.tensor_tensor(out=lam4[:, 0, :], in0=lam4[:, 0, :], in1=lam4[:, 1, :], op=ALU.mult),
           reads=[B_lam4], writes=[B_lam4])
        op("dve", lambda: DVE.tensor_tensor(out=lam4[:, 2, :], in0=lam4[:, 2, :], in1=lam4[:, 3, :], op=ALU.mult),
           reads=[B_lam4], writes=[B_lam4])
        op("dve", lambda: DVE.tensor_reduce(out=ldot[:, 0:1], in_=lam4[:, 0, :], axis=AX.X, op=ALU.add),
           reads=[B_lam4], writes=[B_ldot])
        op("dve", lambda: DVE.tensor_reduce(out=ldot[:, 1:2], in_=lam4[:, 2, :], axis=AX.X, op=ALU.add),
           reads=[B_lam4], writes=[B_ldot])
        op("act", lambda: ACT.activation(out=ldot[:], in_=ldot[:], func=AF.Exp), reads=[B_ldot], writes=[B_ldot])
        op("dve", lambda: DVE.tensor_tensor(out=lam[:, 0:1], in0=ldot[:, 0:1], in1=ldot[:, 1:2], op=ALU.subtract),
           reads=[B_ldot], writes=[B_lam])
        op("dve", lambda: DVE.tensor_scalar(out=lam[:, 0:1], in0=lam[:, 0:1], scalar1=float(LAM_INIT), scalar2=None,
                                            op0=ALU.add), reads=[B_lam], writes=[B_lam])
        op("dve", lambda: DVE.tensor_scalar(out=lam[:, 1:2], in0=lam[:, 0:1], scalar1=-1.0, scalar2=None,
                                            op0=ALU.mult), reads=[B_lam], writes=[B_lam])

        op("dve", lambda: DVE.tensor_scalar(out=sbl[:], in0=sbl[:], scalar1=float(1.0 - LAM_INIT), scalar2=None,
                                            op0=ALU.mult), reads=[B_sbl], writes=[B_sbl])

        op("pool", lambda: POOL.affine_select(out=wsp[:], in_=wsp[:], pattern=[[0, 4], [-1, 128]],
                                              compare_op=ALU.is_ge, fill=0.0, base=0, channel_multiplier=1),
           reads=[B_wsp], writes=[B_wsp])
        for g in range(4):
            op("pe", lambda g=g: TE.transpose(out=PS[0][:, g * 128:(g + 1) * 128], in_=wsp[:, g, :], identity=identf[:]),
               reads=[B_wsp, B_idf], writes=[PSB[0]], inc=(g == 3))
        op("act", lambda: ACT.copy(out=wT[:], in_=PS[0][:].rearrange("p (g t) -> p g t", g=4)),
           reads=[PSB[0]], writes=[B_wT])
        op("pool", lambda: POOL.memset(wblk_f[:], 0.0), writes=[B_wblkf])
        wT4 = sb("wT4", [4, 4, 4], F32); B_wT4 = Buf("wT4")
        op("act", lambda: ACT.copy(out=wT4[:], in_=PS[0][0:4, :].rearrange("p (g t) -> p g t", g=4)[:, :, 0:4]),
           reads=[PSB[0]], writes=[B_wT4])
        for j in range(4):
            dma("sp", c2_sem, wblk_f[4 * j:4 * j + 4, :, 4 * j:4 * j + 4], wT4[:], reads=[B_wT4], writes=[B_wblkf])
        op("pool", lambda: POOL.tensor_copy(out=wblk[:], in_=wblk_f[:]), reads=[B_wblkf], writes=[B_wblk])

        op("pool", lambda: POOL.memset(msk_s[:], 1.0), writes=[B_msk])
        op("pool", lambda: POOL.affine_select(out=msk_s[:], in_=msk_s[:], pattern=[[-4, 4], [0, 8], [0, 4]],
                                              compare_op=ALU.is_ge, fill=0.0, base=0, channel_multiplier=1),
           reads=[B_msk], writes=[B_msk])
        op("pool", lambda: POOL.affine_select(out=msk_s[:], in_=msk_s[:], pattern=[[4, 4], [0, 8], [1, 4]],
                                              compare_op=ALU.is_ge, fill=0.0, base=0, channel_multiplier=-1),
           reads=[B_msk], writes=[B_msk])
        csel = sb("csel", [32, 4, 2], F32); B_csel = Buf("csel")
        op("pool", lambda: POOL.memset(csel[:], 1.0), writes=[B_csel])
        op("pool", lambda: POOL.affine_select(out=csel[:], in_=csel[:], pattern=[[-8, 4], [-4, 2]],
                                              compare_op=ALU.is_ge, fill=0.0, base=0, channel_multiplier=1),
           reads=[B_csel], writes=[B_csel])
        op("pool", lambda: POOL.affine_select(out=csel[:], in_=csel[:], pattern=[[8, 4], [4, 2]],
                                              compare_op=ALU.is_ge, fill=0.0, base=3, channel_multiplier=-1),
           reads=[B_csel], writes=[B_csel])
        cvec = sb("cvec", [32, 2], F32); B_cvec = Buf("cvec")
        op("dve", lambda: DVE.tensor_reduce(out=cvec[:], in_=csel[:].rearrange("p h c -> p c h"), axis=AX.X, op=ALU.add),
           reads=[B_csel], writes=[B_cvec])
        op("dve", lambda: DVE.scalar_tensor_tensor(out=coef[:, 0:1], in0=cvec[:, 1:2], scalar=lam[0:32, 1:2],
                                                   in1=cvec[:, 0:1], op0=ALU.mult, op1=ALU.add),
           reads=[B_cvec, B_lam], writes=[B_coef])
        ohs = sb("ohs", [32, 4], F32); B_ohs = Buf("ohs")
        op("dve", lambda: DVE.tensor_reduce(out=ohs[:], in_=csel[:], axis=AX.X, op=ALU.add), reads=[B_csel], writes=[B_ohs])
        op("dve", lambda: DVE.tensor_copy(out=oneh[:], in_=ohs[:].unsqueeze(2).to_broadcast([32, 4, 128])),
           reads=[B_ohs], writes=[B_oneh])
        op("pool", lambda: POOL.memset(sel[:], 0.0), writes=[B_sel])
        selq = sb("selq", [32, 8, 4, 16], F32, a_stack); B_selq = Buf("selq")
        op("pool", lambda: POOL.memset(selq[:], 1.0), writes=[B_selq])
        op("pool", lambda: POOL.affine_select(out=selq[:], in_=selq[:], pattern=[[4, 8], [-4, 4], [1, 16]],
                                              compare_op=ALU.is_equal, fill=0.0, base=0, channel_multiplier=-1),
           reads=[B_selq], writes=[B_selq])
        op("pool", lambda: POOL.affine_select(out=selq[:], in_=selq[:], pattern=[[-4, 8], [0, 4], [0, 16]],
                                              compare_op=ALU.is_ge, fill=0.0, base=0, channel_multiplier=1),
           reads=[B_selq], writes=[B_selq])
        op("pool", lambda: POOL.affine_select(out=selq[:], in_=selq[:], pattern=[[4, 8], [0, 4], [0, 16]],
                                              compare_op=ALU.is_ge, fill=0.0, base=3, channel_multiplier=-1),
           reads=[B_selq], writes=[B_selq])
        op("dve", lambda: DVE.tensor_reduce(out=sel[:].rearrange("p j m -> p (j m)"),
                                            in_=selq[:].rearrange("p q j m -> p (j m) q"), axis=AX.X, op=ALU.add),
           reads=[B_selq], writes=[B_sel])

        qTz = sb("qTz", [128, 2, H, S], BF16, a_stack); B_qT = [Buf(f"qT{i}") for i in range(NT)]
        op("dve", lambda: DVE.memset(qTz[64:128, 0], 0.0), writes=B_qT)
        op("dve", lambda: DVE.memset(qTz[0:64, 1], 0.0), writes=B_qT)
        kT = sb("kT", [128, H, S], BF16, a_stack); B_kT = [Buf(f"kT{i}") for i in range(NT)]
        qTs = sb("qTs", [128, H, NS], BF16, a_stack); B_qTs = Buf("qTs")
        kTs = sb("kTs", [128, H, NS], BF16, a_stack); B_kTs = Buf("kTs")
        V1 = sb("V1", [128, NT, H, 130], BF16, a_stack); B_V1 = [Buf(f"V1{i}") for i in range(NT)]
        Vs = sb("Vs", [16, H, 130], BF16, a_stack); B_Vs = Buf("Vs")
        mixb = sb("mixb", [128, NT + 1, 512], BF16, a_stack); B_mix = [Buf(f"mix{i}") for i in range(NT + 1)]
        op("pool", lambda: POOL.memset(V1[:, :, :, 128:130], 1.0), writes=B_V1)
        op("pool", lambda: POOL.memset(Vs[:, :, 128:130], 1.0), writes=[B_Vs])

        def tile_rows(i):
            return 128 if i < NT else NS

        def x_src(i):
            return xp[i * 128:(i + 1) * 128, :] if i < NT else xs[:, :]

        def rstd_from_ss(ss_ap, rs_ap, n, width, B_ss, B_rs, ncol):
            op("pool", lambda: POOL.tensor_scalar(out=rs_ap, in0=ss_ap, scalar1=1.0 / width, scalar2=EPS,
                                                  op0=ALU.mult, op1=ALU.add), reads=[B_ss], writes=[B_rs])
            op("pool", lambda: POOL.tensor_tensor(out=rs_ap, in0=rs_ap, in1=nhalf[0:n, 0:ncol], op=ALU.pow),
               reads=[B_rs, B_nh], writes=[B_rs])

        p1 = contextlib.ExitStack()
        with p1:
            win = sb("win", [128, 8, INC], BF16, p1); B_win = [Buf(f"win{k}") for k in range(8)]
            w_sem = Sy.dma_sem("w_sem")
            Sy.defer = False
            wstg = [sb(f"wstg{i}", [128, 1280], F32, p1) for i in range(2)]; B_wstg = [Buf(f"wstg{i}") for i in range(2)]
            ws_sem = [Sy.dma_sem(f"ws_sem{i}") for i in range(2)]
            hw_cnt = 0
            for k in range(8):
                for hh in range(2):
                    if hh == 0:
                        Sy.dma("pool", w_sem, win[:, k, 0:1280], w_in[k * 128:(k + 1) * 128, 0:1280], writes=[B_win[k]])
                    else:
                        sl = hw_cnt % 2
                        Sy.dma("sp", ws_sem[sl], wstg[sl][:, :], w_in[k * 128:(k + 1) * 128, 1280:2560],
                               writes=[B_wstg[sl]])
                        if hw_cnt % 2 == 0:
                            Sy.op("dve", lambda: DVE.tensor_copy(out=win[:, k, 1280:2560], in_=wstg[sl][:, :]),
                                  reads=[B_wstg[sl]], writes=[B_win[k]])
                        else:
                            Sy.op("act", lambda: ACT.copy(out=win[:, k, 1280:2560], in_=wstg[sl][:, :]),
                                  reads=[B_wstg[sl]], writes=[B_win[k]])
                        hw_cnt += 1
            with nc.allow_non_contiguous_dma(reason="tiny constant loads"):
                Sy.flush()

            def dbl(name, shape, dtype):
                return ([sb(f"{name}{i}", shape, dtype, p1) for i in range(2)], [Buf(f"{name}{i}") for i in range(2)])

            xt, B_xt = dbl("xt", [128, D], F32)
            x_sem = [Sy.dma_sem(f"x_sem{i}") for i in range(2)]
            junk = sb("junk", [128, D], BF16, p1)
            ss, B_ss = dbl("ss", [128, 1], F32)
            rs, B_rs = dbl("rs", [128, 1], F32)
            hn, B_hn = dbl("hn", [128, D], BF16)
            hnT, B_hnT = dbl("hnT", [128, 8, 128], BF16)
            sq = sb("sq", [128, 2, 512], F32, p1); B_sq = Buf("sq")
            ssq, B_ssq = dbl("ssq", [128, 16], F32)
            rq, B_rq = dbl("rq", [128, 16], F32)
            qkt, B_qkt = dbl("qkt", [128, 2, 8, 64], F32)
            qkb, B_qkb = dbl("qkb", [128, D], BF16)
            vst, B_vst = dbl("vst", [128, 512], F32)
            o_sem = [Sy.dma_sem(f"o_sem{i}") for i in range(2)]
            ug, B_ug = dbl("ug", [128, 512], F32)
            gg, B_gg = dbl("gg", [128, 4, 128], F32)
            gss, B_gss = dbl("gss", [128, 4], F32)
            grs, B_grs = dbl("grs", [128, 4], F32)
            gvb, B_gvb = dbl("gvb", [128, 512], BF16)

            def pre(i):
                n = tile_rows(i); b = i % 2
                if i + 1 <= NT:
                    dma("sp", x_sem[1 - b], xt[1 - b][0:tile_rows(i + 1), :], x_src(i + 1), writes=[B_xt[1 - b]])
                op("act", lambda: ACT.activation(out=junk[0:n, :], in_=xt[b][0:n, :], func=AF.Square,
                                                 accum_out=ss[b][0:n, :]), reads=[B_xt[b]], writes=[B_ss[b]])
                rstd_from_ss(ss[b][0:n, :], rs[b][0:n, :], n, float(D), B_ss[b], B_rs[b], 1)
                op("dve", lambda: DVE.scalar_tensor_tensor(out=hn[b][0:n, :], in0=xt[b][0:n, :], scalar=rs[b][0:n, 0:1],
                                                           in1=gmix[0:n, :], op0=ALU.mult, op1=ALU.mult),
                   reads=[B_xt[b], B_rs[b], B_gmix], writes=[B_hn[b]])

            def TH(i):
                n = tile_rows(i); b = i % 2
                ptr = ps_bf(5).rearrange("p (k t) -> p k t", k=8)
                for k in range(8):
                    op("pe", lambda k=k: TE.transpose(out=ptr[:, k, 0:n], in_=hn[b][0:n, k * 128:(k + 1) * 128],
                                                      identity=identb[0:n, 0:n]),
                       reads=[B_hn[b], B_idb], writes=[PSB[5]], inc=(k == 7))
                op("act", lambda: ACT.copy(out=hnT[b][:, :, 0:n], in_=ptr[:, :, 0:n]), reads=[PSB[5]], writes=[B_hnT[b]])

            def MM(i):
                n = tile_rows(i); b = i % 2
                for cb in (2, 3, 4, 0, 1):
                    for k in range(8):
                        op("pe", lambda cb=cb, k=k: TE.matmul(PS[cb][0:n, :], lhsT=hnT[b][:, k, 0:n],
                                                              rhs=win[:, k, cb * 512:(cb + 1) * 512],
                                                              start=(k == 0), stop=(k == 7)),
                           reads=[B_hnT[b], B_win[k]], writes=[PSB[cb]], inc=(k == 7))

            def back_a(i):
                n = tile_rows(i); b = i % 2
                op("act", lambda: ACT.copy(out=vst[b][0:n, :], in_=PS[2][0:n, :]), reads=[PSB[2]], writes=[B_vst[b]])
                op("act", lambda: ACT.activation(out=ug[b][0:n, :], in_=PS[3][0:n, :], func=AF.Gelu_apprx_tanh),
                   reads=[PSB[3]], writes=[B_ug[b]])
                op("act", lambda: ACT.activation(out=gg[b][0:n].rearrange("p g d -> p (g d)"), in_=PS[4][0:n, :],
                                                 func=AF.Gelu_apprx_tanh), reads=[PSB[4]], writes=[B_gg[b]])
                for cb in range(2):
                    op("act", lambda cb=cb: ACT.activation(out=sq[0:n, cb, :], in_=PS[cb][0:n, :], func=AF.Square),
                       reads=[PSB[cb]], writes=[B_sq])
                for g in range(4):
                    op("act", lambda g=g: ACT.activation(out=junk[0:n, g * 128:(g + 1) * 128], in_=gg[b][0:n, g, :],
                                                         func=AF.Square, accum_out=gss[b][0:n, g:g + 1]),
                       reads=[B_gg[b]], writes=[B_gss[b]])
                op("dve", lambda: DVE.tensor_reduce(out=ssq[b][0:n, :], in_=sq[0:n].rearrange("p a (g d) -> p (a g) d", d=64),
                                                    axis=AX.X, op=ALU.add), reads=[B_sq], writes=[B_ssq[b]])
                rstd_from_ss(ssq[b][0:n, :], rq[b][0:n, :], n, 64.0, B_ssq[b], B_rq[b], 16)
                rstd_from_ss(gss[b][0:n, :], grs[b][0:n, :], n, 128.0, B_gss[b], B_grs[b], 4)
                vdst = v_p[i * 128:(i + 1) * 128, :] if i < NT else v_s[:, :]
                dma("sp", o_sem[b], vdst, vst[b][0:n, :], reads=[B_vst[b]])
                if i < NT:
                    op("dve", lambda: DVE.tensor_copy(out=V1[:, i, :, 0:128],
                                                      in_=vst[b][:, :].rearrange("p (h d) -> p h d", h=4)),
                       reads=[B_vst[b]], writes=[B_V1[i]])
                else:
                    op("dve", lambda: DVE.tensor_copy(out=Vs[:, :, 0:128],
                                                      in_=vst[b][0:n, :].rearrange("p (h d) -> p h d", h=4)),
                       reads=[B_vst[b]], writes=[B_Vs])
                for cb in range(2):
                    op("dve", lambda cb=cb: DVE.tensor_tensor(
                        out=qkt[b][0:n, cb], in0=PS[cb][0:n, :].rearrange("p (g d) -> p g d", d=64),
                        in1=rq[b][0:n, cb * 8:(cb + 1) * 8].unsqueeze(2).to_broadcast([n, 8, 64]), op=ALU.mult),
                       reads=[PSB[cb], B_rq[b]], writes=[B_qkt[b]])
                op("dve", lambda: DVE.tensor_tensor(out=gg[b][0:n], in0=gg[b][0:n],
                                                    in1=grs[b][0:n, :].unsqueeze(2).to_broadcast([n, 4, 128]), op=ALU.mult),
                   reads=[B_gg[b], B_grs[b]], writes=[B_gg[b]])

            def back_b(i):
                n = tile_rows(i); b = i % 2
                op("dve", lambda: DVE.tensor_tensor(out=qkt[b][0:n], in0=qkt[b][0:n], in1=qkg[0:n], op=ALU.mult),
                   reads=[B_qkt[b], B_qkg], writes=[B_qkt[b]])
                op("act", lambda: ACT.copy(out=qkb[b][0:n, :], in_=qkt[b][0:n].rearrange("p a g d -> p (a g d)")),
                   reads=[B_qkt[b]], writes=[B_qkb[b]])
                kdst = k_p[i * 128:(i + 1) * 128, :] if i < NT else k_s[:, :]
                dma("sp", o_sem[b], kdst, qkt[b][0:n, 1].rearrange("p g d -> p (g d)"), reads=[B_qkt[b]])
                op("pool", lambda: POOL.tensor_tensor(out=gg[b][0:n].rearrange("p g d -> p (g d)"),
                                                      in0=gg[b][0:n].rearrange("p g d -> p (g d)"), in1=gvg[0:n, :],
                                                      op=ALU.mult), reads=[B_gg[b], B_gvg], writes=[B_gg[b]])
                if i == NT:
                    dma("sp", o_sem[b], g_s[:, :], gg[b][0:n].rearrange("p g d -> p (g d)"), reads=[B_gg[b]])
                op("act", lambda: ACT.copy(out=gvb[b][0:n, :], in_=gg[b][0:n].rearrange("p g d -> p (g d)")),
                   reads=[B_gg[b]], writes=[B_gvb[b]])
                for g in range(4):
                    lw = wT[:, g, :] if i < NT else wblk[:, g, :]
                    op("pe", lambda g=g, lw=lw: TE.matmul(PS[6][0:n, g * 128:(g + 1) * 128], lhsT=lw,
                                                          rhs=gvb[b][0:n, g * 128:(g + 1) * 128], start=True, stop=True),
                       reads=[B_gvb[b], B_wT, B_wblk], writes=[PSB[6]], inc=(g == 3))
                bb = bsp if i < NT else bsp_s
                for g in range(4):
                    op("dve", lambda g=g: DVE.scalar_tensor_tensor(
                        out=mixb[0:n, i, g * 128:(g + 1) * 128], in0=PS[6][0:n, g * 128:(g + 1) * 128],
                        scalar=bb[0:n, g:g + 1], in1=ug[b][0:n, g * 128:(g + 1) * 128], op0=ALU.add, op1=ALU.mult),
                       reads=[PSB[6], B_bsp, B_bsps, B_ug[b]], writes=[B_mix[i]])
                ptq = ps_bf(7).rearrange("p (k t) -> p k t", k=8)
                for k in range(8):
                    op("pe", lambda k=k: TE.transpose(out=ptq[:, k, 0:n], in_=qkb[b][0:n, k * 128:(k + 1) * 128],
                                                      identity=identb[0:n, 0:n]),
                       reads=[B_qkb[b], B_idb], writes=[PSB[7]], inc=(k == 7))
                if i < NT:
                    op("dve", lambda: DVE.tensor_copy(out=qTz[0:64, 0, :, i * 128:(i + 1) * 128], in_=ptq[0:64, 0:4, :]),
                       reads=[PSB[7]], writes=[B_qT[i]])
                    op("dve", lambda: DVE.tensor_copy(out=qTz[64:128, 1, :, i * 128:(i + 1) * 128], in_=ptq[64:128, 0:4, :]),
                       reads=[PSB[7]], writes=[B_qT[i]])
                    op("act", lambda: ACT.copy(out=kT[:, :, i * 128:(i + 1) * 128], in_=ptq[:, 4:8, :]),
                       reads=[PSB[7]], writes=[B_kT[i]])
                else:
                    op("dve", lambda: DVE.tensor_copy(out=qTs[:, :, :], in_=ptq[:, 0:4, 0:n]),
                       reads=[PSB[7]], writes=[B_qTs])
                    op("act", lambda: ACT.copy(out=kTs[:, :, :], in_=ptq[:, 4:8, 0:n]),
                       reads=[PSB[7]], writes=[B_kTs])

            dma("sp", x_sem[0], xt[0][:, :], x_src(0), writes=[B_xt[0]])
            pre(0)
            TH(0)
            for i in range(NT + 1):
                if i + 1 <= NT:
                    pre(i + 1)
                MM(i)
                if i + 1 <= NT:
                    TH(i + 1)
                if i >= 1:
                    back_b(i - 1)
                back_a(i)
            back_b(NT)
            Sy.barrier()
        B_hd = [Buf(f"hd{i}") for i in range(NT + 1)]

        def y_dst(i):
            return y_p[i * 128:(i + 1) * 128, :] if i < NT else y_s[:, :]

        p2 = contextlib.ExitStack()
        with p2:
            wout = sb("wout", [128, 8, D], BF16, p2); B_wout = Buf("wout")
            wo_sem = Sy.dma_sem("wo_sem")
            att = sb("att", [128, 4, H, 128], F32, p2); B_att = [Buf(f"att{r}") for r in range(4)]
            eT = [sb(f"eT{i}", [128, 512], BF16, p2) for i in range(4)]; B_eT = [Buf(f"eT{i}") for i in range(4)]
            rz = [sb(f"rz{i}", [128, 4], F32, p2) for i in range(2)]; B_rz = [Buf(f"rz{i}") for i in range(2)]
            junk2 = sb("junk2", [128, 512], BF16, p2)
            tri01 = sb("tri01", [128, 128], BF16, p2); B_tri = Buf("tri01")
            op("pool", lambda: POOL.memset(tri01[:], 1.0), writes=[B_tri])
            op("pool", lambda: POOL.affine_select(out=tri01[:], in_=tri01[:], pattern=[[1, 128]], compare_op=ALU.is_ge,
                                                  fill=0.0, base=0, channel_multiplier=-1), reads=[B_tri], writes=[B_tri])
            ass = sb("ass", [128, 4], F32, p2); B_ass = Buf("ass")
            ars = sb("ars", [128, 4], F32, p2); B_ars = Buf("ars")
            at1 = sb("at1", [128, H, 128], F32, p2); B_at1 = Buf("at1")
            catb = sb("catb", [128, H, 128], BF16, p2); B_catb = Buf("catb")
            catT = sb("catT", [128, 8, 128], BF16, p2); B_catT = Buf("catT")
            xt2 = [sb(f"xt2{i}", [128, D], F32, p2) for i in range(2)]; B_xt2 = [Buf(f"xt2{i}") for i in range(2)]
            x2_sem = [Sy.dma_sem(f"x2_sem{i}") for i in range(2)]
            hst = [sb(f"hst{i}", [128, D], F32, p2) for i in range(2)]; B_hst = [Buf(f"hst{i}") for i in range(2)]
            h_sem = [Sy.dma_sem(f"h_sem{i}") for i in range(2)]

            def epilogue(i, n, att_ap, B_a):
                b = i % 2
                dma("sp", x2_sem[b], xt2[b][0:n, :], x_src(i), writes=[B_xt2[b]])
                for h in range(H):
                    op("act", lambda h=h: ACT.activation(out=junk2[0:n, h * 128:(h + 1) * 128], in_=att_ap[:, h, :],
                                                         func=AF.Square, accum_out=ass[0:n, h:h + 1]),
                       reads=[B_a], writes=[B_ass])
                rstd_from_ss(ass[0:n, :], ars[0:n, :], n, 128.0, B_ass, B_ars, 4)
                op("dve", lambda: DVE.tensor_tensor(out=at1[0:n], in0=att_ap,
                                                    in1=ars[0:n, :].unsqueeze(2).to_broadcast([n, H, 128]), op=ALU.mult),
                   reads=[B_a, B_ars], writes=[B_at1])
                op("pool", lambda: POOL.tensor_tensor(out=catb[0:n], in0=at1[0:n], in1=sbl[0:n], op=ALU.mult),
                   reads=[B_at1, B_sbl], writes=[B_catb])
                ptr = ps_bf(4).rearrange("p (k t) -> p k t", k=8)
                for k in range(4):
                    op("pe", lambda k=k: TE.transpose(out=ptr[:, k, 0:n], in_=catb[0:n, k, :], identity=identb[0:n, 0:n]),
                       reads=[B_catb, B_idb], writes=[PSB[4]], inc=False)
                for k in range(4):
                    op("pe", lambda k=k: TE.transpose(out=ptr[:, 4 + k, 0:n], in_=mixb[0:n, i, k * 128:(k + 1) * 128],
                                                      identity=identb[0:n, 0:n]),
                       reads=[B_mix[i], B_idb], writes=[PSB[4]], inc=(k == 3))
                op("act", lambda: ACT.copy(out=catT[:, :, 0:n], in_=ptr[:, :, 0:n]), reads=[PSB[4]], writes=[B_catT])
                for cb in range(2):
                    for k in range(8):
                        op("pe", lambda cb=cb, k=k: TE.matmul(PS[cb][0:n, :], lhsT=catT[:, k, 0:n],
                                                              rhs=wout[:, k, cb * 512:(cb + 1) * 512],
                                                              start=(k == 0), stop=(k == 7)),
                           reads=[B_catT, B_wout], writes=[PSB[cb]], inc=(k == 7))
                for cb in range(2):
                    op("dve", lambda cb=cb: DVE.tensor_tensor(out=hst[b][0:n, cb * 512:(cb + 1) * 512], in0=PS[cb][0:n, :],
                                                              in1=xt2[b][0:n, cb * 512:(cb + 1) * 512], op=ALU.add),
                       reads=[PSB[cb], B_xt2[b]], writes=[B_hst[b]])
                dma("sp", h_sem[b], y_dst(i), hst[b][0:n, :], reads=[B_hst[b]], writes=[B_hd[i]])

            if True:
                NBt = 3
                n_tot = 4 * n_pages
                n_tiles = n_tot // 4
                tiles_per_seq = n_pages // 4
                Kb = [sb(f"Kb{i}", [128, 4, 512], BF16, p2) for i in range(NBt)]; B_Kb = [Buf(f"Kb{i}") for i in range(NBt)]
                Vb = [sb(f"Vb{i}", [128, 4, 512], BF16, p2) for i in range(NBt)]; B_Vb = [Buf(f"Vb{i}") for i in range(NBt)]
                kd_sem = [Sy.dma_sem(f"kd_sem{i}") for i in range(NBt)]
                vd_sem = [Sy.dma_sem(f"vd_sem{i}") for i in range(NBt)]
                KTb = [sb(f"KTb{i}", [128, 2, H, 128], BF16, p2) for i in range(2)]; B_KTb = [Buf(f"KTb{i}") for i in range(2)]
                eTd = [sb(f"eTd{i}", [128, 4, 32], BF16, p2) for i in range(2)]; B_eTd = [Buf(f"eTd{i}") for i in range(2)]
                qblk = sb("qblk", [128, 4, H, 8], BF16, p2); B_qblk = Buf("qblk")
                ones2 = sb("ones2", [128, 2], BF16, p2); B_ones2 = Buf("ones2")
                Vsb = sb("Vsb", [16, 512], BF16, p2); B_Vsb = Buf("Vsb")
                en = sb("en", [16, 32], F32, p2); B_en = Buf("en")
                enb = sb("enb", [16, 32], BF16, p2); B_enb = Buf("enb")
                Rr = sb("Rr", [32, H, 128], F32, p2); B_Rr = Buf("Rr")
                zt = sb("zt", [32, 8], F32, p2); B_zt = Buf("zt")
                R2 = sb("R2", [32, H, 128], F32, p2); B_R2 = Buf("R2")
                atts = sb("atts", [16, H, 128], F32, p2); B_atts = Buf("atts")
                R2h = sb("R2h", [32, H, 128], BF16, p2); B_R2h = Buf("R2h")
                R2l = sb("R2l", [32, H, 128], BF16, p2); B_R2l = Buf("R2l")
                selb = sb("selb", [32, 4, 16], BF16, p2); B_selb = Buf("selb")
                op("dve", lambda: DVE.tensor_copy(out=selb[:], in_=sel[:]), reads=[B_sel], writes=[B_selb])
                op("pool", lambda: POOL.memset(ones2[:], 1.0), writes=[B_ones2])
                op("dve", lambda: DVE.tensor_copy(out=Vsb[:, :].rearrange("p (h d) -> p h d", h=H), in_=Vs[:, :, 0:128]),
                   reads=[B_Vs], writes=[B_Vsb])
                op("pool", lambda: POOL.memset(qblk[:], 0.0), writes=[B_qblk])
                for c in range(2):
                    op("dve", lambda c=c: DVE.tensor_copy(
                        out=qblk[c * 64:(c + 1) * 64, :, :, c * 4:(c + 1) * 4],
                        in_=qTs[c * 64:(c + 1) * 64, :, :].rearrange("p h (s t) -> p s h t", s=4)),
                       reads=[B_qTs, B_qblk], writes=[B_qblk])
                ptbc = sb("ptbc", [128, n_tot], I32, p2); B_ptbc = Buf("ptbc")
                ptf = sb("ptf", [128, n_tiles, 4], F32, p2); B_ptf = Buf("ptf")
                m4 = sb("m4", [128, 4], F32, p2); B_m4 = Buf("m4")
                pv = sb("pv", [128, 4], F32, p2); B_pv = Buf("pv")
                pm = sb("pm", [128, 1], F32, p2); B_pm = Buf("pm")
                selp = sb("selp", [128, n_tiles], F32, p2); B_selp = Buf("selp")
                idx4 = sb("idx4", [128, n_tiles], I32, p2); B_idx = Buf("idx4")
                pt_sem = Sy.dma_sem("pt_sem")
                dma("sp", pt_sem, ptbc[:], bc(ptab, [[0, 128], [1, n_tot]]), writes=[B_ptbc])
                op("pool", lambda: POOL.memset(m4[:], 1.0), writes=[B_m4])
                op("pool", lambda: POOL.affine_select(out=m4[:], in_=m4[:], pattern=[[-32, 4]], compare_op=ALU.is_ge,
                                                      fill=0.0, base=0, channel_multiplier=1), reads=[B_m4], writes=[B_m4])
                op("pool", lambda: POOL.affine_select(out=m4[:], in_=m4[:], pattern=[[32, 4]], compare_op=ALU.is_ge,
                                                      fill=0.0, base=31, channel_multiplier=-1), reads=[B_m4], writes=[B_m4])
                op("pool", lambda: POOL.iota(pv[:], pattern=[[-32, 4]], base=0, channel_multiplier=1,
                                             allow_small_or_imprecise_dtypes=True), writes=[B_pv])
                op("dve", lambda: DVE.tensor_tensor(out=pv[:], in0=pv[:], in1=m4[:], op=ALU.mult),
                   reads=[B_pv, B_m4], writes=[B_pv])
                op("dve", lambda: DVE.tensor_reduce(out=pm[:], in_=pv[:], axis=AX.X, op=ALU.add), reads=[B_pv], writes=[B_pm])
                op("dve", lambda: DVE.tensor_copy(out=ptf[:].rearrange("p g q -> p (g q)"), in_=ptbc[:]),
                   reads=[B_ptbc], writes=[B_ptf])
                op("dve", lambda: DVE.tensor_tensor(out=ptf[:], in0=ptf[:],
                                                    in1=m4[:].unsqueeze(1).to_broadcast([128, n_tiles, 4]), op=ALU.mult),
                   reads=[B_ptf, B_m4], writes=[B_ptf])
                op("dve", lambda: DVE.tensor_reduce(out=selp[:], in_=ptf[:], axis=AX.X, op=ALU.add),
                   reads=[B_ptf], writes=[B_selp])
                op("dve", lambda: DVE.tensor_scalar(out=idx4[:], in0=selp[:], scalar1=32.0, scalar2=pm[:, 0:1],
                                                    op0=ALU.mult, op1=ALU.add), reads=[B_selp, B_pm], writes=[B_idx])
                ck_rows = cache_k.rearrange("n (r q) c -> (n r) (q c)", q=4)
                cv_rows = cache_v.rearrange("n (r q) c -> (n r) (q c)", q=4)

                def issue(g_):
                    if g_ >= n_tiles:
                        return
                    sl = g_ % NBt
                    Sy.dma_fn("pool", kd_sem[sl], lambda: POOL.indirect_dma_start(
                        out=Kb[sl][:].rearrange("p s c -> p (s c)"), out_offset=None, in_=ck_rows,
                        in_offset=bass.IndirectOffsetOnAxis(ap=idx4[:, g_:g_ + 1], axis=0)),
                        reads=[B_idx], writes=[B_Kb[sl]])
                    Sy.dma_fn("pool", vd_sem[sl], lambda: POOL.indirect_dma_start(
                        out=Vb[sl][:].rearrange("p s c -> p (s c)"), out_offset=None, in_=cv_rows,
                        in_offset=bass.IndirectOffsetOnAxis(ap=idx4[:, g_:g_ + 1], axis=0)),
                        reads=[B_idx], writes=[B_Vb[sl]])

                accO = sb("accO", [32, 512], F32, p2); B_accO = Buf("accO")
                accZs = sb("accZs", [32, 2], F32, p2); B_accZs = Buf("accZs")
                T_BANK, S_BANK, O_BANK = 5, 6, 7

                def do_Th(g_, half):
                    sl = g_ % NBt
                    ptk = ps_bf(T_BANK).rearrange("p (s h t) -> p s h t", s=2, h=H)
                    for s2 in range(2):
                        for h in range(H):
                            op("pe", lambda: TE.transpose(out=ptk[:, s2, h, :],
                                                          in_=Kb[sl][:, 2 * half + s2, h * 128:(h + 1) * 128],
                                                          identity=identb[:]),
                               reads=[B_Kb[sl], B_idb], writes=[PSB[T_BANK]], inc=(s2 == 1 and h == H - 1))
                    if half == 0:
                        op("act", lambda: ACT.copy(out=KTb[half][:], in_=ptk), reads=[PSB[T_BANK]], writes=[B_KTb[half]])
                    else:
                        op("dve", lambda: DVE.tensor_copy(out=KTb[half][:], in_=ptk), reads=[PSB[T_BANK]],
                           writes=[B_KTb[half]])

                def do_S(g_, half):
                    sidx = g_ // tiles_per_seq
                    for s2 in range(2):
                        s_ = 2 * half + s2
                        for h in range(H):
                            op("pe", lambda: TE.matmul(PS[S_BANK][:, s_ * 32 + h * 8:s_ * 32 + h * 8 + 8],
                                                       lhsT=KTb[half][:, s2, h, :], rhs=qblk[:, sidx, h, :],
                                                       start=True, stop=True),
                               reads=[B_KTb[half], B_qblk], writes=[PSB[S_BANK]], inc=(s2 == 1 and h == H - 1))

                def do_E(g_):
                    e = g_ % 2
                    op("act", lambda: ACT.activation(out=eTd[e][:].rearrange("p g x -> p (g x)"),
                                                     in_=PS[S_BANK][:, 0:128], func=AF.Exp, scale=0.125),
                       reads=[PSB[S_BANK]], writes=[B_eTd[e]])

                def acc_add(first):
                    if first:
                        op("dve", lambda: DVE.tensor_copy(out=accO[:, :], in_=PS[O_BANK][0:32, :]),
                           reads=[PSB[O_BANK]], writes=[B_accO])
                        op("dve", lambda: DVE.tensor_copy(out=accZs[:, :], in_=PS[O_BANK + 0][0:32, 0:2]) if False else
                           DVE.tensor_copy(out=accZs[:, :], in_=PS[S_BANK][0:32, 128:130]),
                           reads=[PSB[S_BANK]], writes=[B_accZs])
                    else:
                        op("dve", lambda: DVE.tensor_tensor(out=accO[:, :], in0=PS[O_BANK][0:32, :], in1=accO[:, :],
                                                            op=ALU.add), reads=[PSB[O_BANK], B_accO], writes=[B_accO])
                        op("dve", lambda: DVE.tensor_tensor(out=accZs[:, :], in0=PS[S_BANK][0:32, 128:130],
                                                            in1=accZs[:, :], op=ALU.add),
                           reads=[PSB[S_BANK], B_accZs], writes=[B_accZs])

                def do_AV(g_):
                    sl = g_ % NBt
                    e = g_ % 2
                    for s_ in range(4):
                        op("pe", lambda: TE.matmul(PS[O_BANK][0:32, :], lhsT=eTd[e][:, s_, :], rhs=Vb[sl][:, s_, :],
                                                   start=(s_ == 0), stop=(s_ == 3), skip_group_check=True),
                           reads=[B_eTd[e], B_Vb[sl]], writes=[PSB[O_BANK]], inc=False)
                        op("pe", lambda: TE.matmul(PS[S_BANK][0:32, 128:130], lhsT=eTd[e][:, s_, :], rhs=ones2[:, :],
                                                   start=False, stop=(s_ == 3), skip_group_check=True),
                           reads=[B_eTd[e], B_ones2], writes=[PSB[S_BANK]], inc=(s_ == 3))
                    acc_add(g_ % tiles_per_seq == 0)

                def seq_tail(sidx):
                    for h in range(H):
                        op("pe", lambda h=h: TE.matmul(PS[S_BANK][0:16, h * 8:(h + 1) * 8], lhsT=kTs[:, h, :],
                                                       rhs=qblk[:, sidx, h, :], start=True, stop=True),
                           reads=[B_kTs, B_qblk], writes=[PSB[S_BANK]], inc=(h == 3))
                    op("act", lambda: ACT.activation(out=en[:, :], in_=PS[S_BANK][0:16, 0:32], func=AF.Exp, scale=0.125),
                       reads=[PSB[S_BANK]], writes=[B_en])
                    op("dve", lambda: DVE.tensor_tensor(out=enb[:, :], in0=en[:, :],
                                                        in1=msk_s[:, sidx].rearrange("p a b -> p (a b)"), op=ALU.mult),
                       reads=[B_en, B_msk], writes=[B_enb])
                    op("pe", lambda: TE.matmul(PS[O_BANK][0:32, :], lhsT=enb[:, :], rhs=Vsb[:, :], start=True, stop=True,
                                               skip_group_check=True), reads=[B_enb, B_Vsb], writes=[PSB[O_BANK]], inc=False)
                    op("pe", lambda: TE.matmul(PS[S_BANK][0:32, 128:130], lhsT=enb[:, :], rhs=ones2[0:16, :],
                                               start=False, stop=True, skip_group_check=True),
                       reads=[B_enb, B_ones2], writes=[PSB[S_BANK]], inc=True)
                    acc_add(False)
                    op("dve", lambda: DVE.reciprocal(out=zt[:, 5:6], in_=accZs[:, 0:1]), reads=[B_accZs], writes=[B_zt])
                    op("dve", lambda: DVE.tensor_tensor(out=zt[:, 6:7], in0=zt[:, 5:6], in1=coef[:, 0:1], op=ALU.mult),
                       reads=[B_zt, B_coef], writes=[B_zt])
                    op("dve", lambda: DVE.scalar_tensor_tensor(out=R2[:], in0=accO[:, :].rearrange("p (a b) -> p a b", a=H),
                                                               scalar=zt[:, 6:7], in1=oneh[:], op0=ALU.mult, op1=ALU.mult),
                       reads=[B_accO, B_zt, B_oneh], writes=[B_R2])
                    op("dve", lambda: DVE.tensor_copy(out=R2h[:], in_=R2[:]), reads=[B_R2], writes=[B_R2h])
                    op("dve", lambda: DVE.tensor_tensor(out=R2l[:], in0=R2[:], in1=R2h[:], op=ALU.subtract),
                       reads=[B_R2, B_R2h], writes=[B_R2l])
                    op("pe", lambda: TE.matmul(PS[O_BANK][0:16, :], lhsT=selb[:, sidx, :],
                                               rhs=R2h[:].rearrange("p a b -> p (a b)"),
                                               start=True, stop=False, skip_group_check=True),
                       reads=[B_R2h, B_selb], writes=[PSB[O_BANK]], inc=False)
                    op("pe", lambda: TE.matmul(PS[O_BANK][0:16, :], lhsT=selb[:, sidx, :],
                                               rhs=R2l[:].rearrange("p a b -> p (a b)"),
                                               start=False, stop=True, skip_group_check=True),
                       reads=[B_R2l, B_selb], writes=[PSB[O_BANK]], inc=True)
                    if sidx == 0:
                        op("dve", lambda: DVE.tensor_copy(out=atts[:].rearrange("p a b -> p (a b)"), in_=PS[O_BANK][0:16, :]),
                           reads=[PSB[O_BANK]], writes=[B_atts])
                    else:
                        op("dve", lambda: DVE.tensor_tensor(out=atts[:].rearrange("p a b -> p (a b)"),
                                                            in0=PS[O_BANK][0:16, :],
                                                            in1=atts[:].rearrange("p a b -> p (a b)"), op=ALU.add),
                           reads=[PSB[O_BANK], B_atts], writes=[B_atts])

                def decode_gen():
                    for g_ in range(NBt - 1):
                        issue(g_)
                    for g_ in range(n_tiles):
                        do_Th(g_, 0)
                        if g_ % tiles_per_seq != 0:
                            do_AV(g_ - 1)
                        issue(g_ + NBt - 1)
                        yield
                        do_Th(g_, 1)
                        do_S(g_, 0)
                        yield
                        do_S(g_, 1)
                        do_E(g_)
                        if (g_ + 1) % tiles_per_seq == 0:
                            yield
                            do_AV(g_)
                            seq_tail(g_ // tiles_per_seq)
                        yield

                for k in range(8):
                    dma("pool", wo_sem, wout[:, k, :], w_out[k * 128:(k + 1) * 128, :], writes=[B_wout])
                dgen = decode_gen()
                n_ticks_total = n_tiles * 3 + 4
                ticks_done = [0]

                def decode_ticks(k):
                    for _ in range(k):
                        if ticks_done[0] >= n_ticks_total:
                            return
                        try:
                            next(dgen)
                        except StopIteration:
                            ticks_done[0] = n_ticks_total
                            return
                        ticks_done[0] += 1

            if stop_after >= 2:
                LOOK = 1
                steps = []
                grp_cnt = 0
                for j in range(4):
                    for h in range(H):
                        for c in range(2):
                            gs = grp_cnt % 2
                            grp_cnt += 1
                            na = 4 * j + 4
                            for a in range(na):
                                steps.append(dict(j=j, h=h, c=c, a=a, gs=gs, last=(a == na - 1),
                                                  lastgrp=(a == na - 1 and h == H - 1 and c == 1)))

                def emit_S(t, st):
                    j, h, c, a = st["j"], st["h"], st["c"], st["a"]
                    r0 = max(0, a - 4 * j)
                    ncols = 512 - 128 * r0
                    sbk = t % 2
                    eb = t % 4
                    st.update(r0=r0, ncols=ncols, sbk=sbk, eb=eb)
                    op("pe", lambda: TE.matmul(
                        PS[sbk][:, 0:ncols], lhsT=kT[:, h, a * 128:(a + 1) * 128],
                        rhs=qTz[:, c, h, j * 512 + r0 * 128:(j + 1) * 512], start=True, stop=True),
                       reads=[B_kT[a]] + B_qT[4 * j:4 * j + 4], writes=[PSB[sbk]], inc=True)
                    op("act", lambda: ACT.activation(out=eT[eb][:, 0:ncols], in_=PS[sbk][:, 0:ncols],
                                                     func=AF.Exp, scale=0.125),
                       reads=[PSB[sbk]], writes=[B_eT[eb]])
                    if a >= 4 * j:
                        op("dve", lambda: DVE.tensor_tensor(out=eT[eb][:, 0:128], in0=eT[eb][:, 0:128], in1=tri01[:, :],
                                                            op=ALU.mult),
                           reads=[B_eT[eb], B_tri], writes=[B_eT[eb]])

                def emit_AV(st):
                    j, h, c, a, gs = st["j"], st["h"], st["c"], st["a"], st["gs"]
                    r0, eb = st["r0"], st["eb"]
                    accs = (PS[2], PS[3])
                    Baccs = (PSB[2], PSB[3])
                    for r in range(r0, 4):
                        bank = accs[r // 2]
                        off = (r % 2) * 130
                        op("pe", lambda: TE.matmul(
                            bank[:, off:off + 130], lhsT=eT[eb][:, (r - r0) * 128:(r - r0 + 1) * 128],
                            rhs=V1[:, a, h, :], start=(a == 0 and r % 2 == 0), stop=(a == 4 * j + r),
                            skip_group_check=True),
                           reads=[B_eT[eb], B_V1[a]], writes=[Baccs[r // 2]], inc=(r == 3))
                    if not st["last"]:
                        return
                    for bi in range(2):
                        zv = accs[bi][:, 0:260].rearrange("p (r x) -> p r x", x=130)[:, :, 128]
                        op("dve", lambda: DVE.reciprocal(out=rz[gs][:, 2 * bi:2 * bi + 2], in_=zv),
                           reads=[Baccs[bi]], writes=[B_rz[gs]])
                    if c == 1:
                        op("dve", lambda: DVE.tensor_scalar(out=rz[gs][:, :], in0=rz[gs][:, :], scalar1=lam[:, 1:2],
                                                            scalar2=None, op0=ALU.mult),
                           reads=[B_rz[gs], B_lam], writes=[B_rz[gs]])
                    for r in range(4):
                        bank = accs[r // 2]
                        off = (r % 2) * 130
                        if c == 0:
                            op("act", lambda: ACT.activation(out=att[:, r, h, :], in_=bank[:, off:off + 128],
                                                             func=AF.Copy, scale=rz[gs][:, r:r + 1]),
                               reads=[Baccs[r // 2], B_rz[gs]], writes=[B_att[r]])
                        else:
                            op("dve", lambda: DVE.scalar_tensor_tensor(
                                out=att[:, r, h, :], in0=bank[:, off:off + 128], scalar=rz[gs][:, r:r + 1],
                                in1=att[:, r, h, :], op0=ALU.mult, op1=ALU.add),
                               reads=[Baccs[r // 2], B_rz[gs], B_att[r]], writes=[B_att[r]])
                    if st["lastgrp"]:
                        for r in range(4):
                            epilogue(4 * j + r, 128, att[:, r], B_att[r])

                for t in range(len(steps) + LOOK):
                    if t < len(steps):
                        emit_S(t, steps[t])
                    if t - LOOK >= 0:
                        emit_AV(steps[t - LOOK])
                    want = (min(t + 1, len(steps)) * n_ticks_total) // len(steps)
                    decode_ticks(want - ticks_done[0])
                decode_ticks(n_ticks_total)
                for _ in dgen:
                    pass
                epilogue(NT, NS, atts[:], B_atts)

            Sy.barrier()
        a_stack.close()
        cur[0] = es

        p3 = contextlib.ExitStack()
        with p3:
            if stop_after >= 4:
                gffn = sb("gffn", [128, D], F32, p3); B_gffn = Buf("gffn")
                g_sem = Sy.dma_sem("g_sem")
                dma("sp", g_sem, gffn[:], bc(norm_ffn, [[0, 128], [1, D]]), writes=[B_gffn])
                hres = sb("hres", [128, NT + 1, D], F32, p3); B_h = [Buf(f"h{i}") for i in range(NT + 1)]
                hl_sem = [Sy.dma_sem(f"hl_sem{i}") for i in range(2)]
                hn2T = sb("hn2T", [128, 8, NTOK], BF16, p3); B_hn2T = [Buf(f"hn2T{i}") for i in range(NT + 1)]
                wu = [sb(f"wu{i}", [128, 8, 1024], BF16, p3) for i in range(2)]; B_wu = [Buf(f"wu{i}") for i in range(2)]
                wd = [sb(f"wd{i}", [128, 8, 1024], BF16, p3) for i in range(2)]; B_wd = [Buf(f"wd{i}") for i in range(2)]
                wu_sem = [Sy.dma_sem(f"wu_sem{i}") for i in range(2)]
                wd_sem = [Sy.dma_sem(f"wd_sem{i}") for i in range(2)]
                hT = [sb(f"hT{i}", [128, 8, 512], BF16, p3) for i in range(2)]; B_hT = [Buf(f"hT{i}") for i in range(2)]
                rl = [sb(f"rl{i}", [128, 512], F32, p3) for i in range(2)]; B_rl = [Buf(f"rl{i}") for i in range(2)]
                ss3 = [sb(f"ss3{i}", [128, 1], F32, p3) for i in range(5)]; B_ss3 = [Buf(f"ss3{i}") for i in range(5)]
                rs3 = [sb(f"rs3{i}", [128, 1], F32, p3) for i in range(5)]; B_rs3 = [Buf(f"rs3{i}") for i in range(5)]
                hn3 = [sb(f"hn3{i}", [128, D], BF16, p3) for i in range(5)]; B_hn3 = [Buf(f"hn3{i}") for i in range(5)]
                junk3 = sb("junk3", [128, D], BF16, p3)
                y_sem = [Sy.dma_sem(f"y_sem{i}") for i in range(2)]

                def load_wu(qq):
                    s_ = qq % 2
                    for k in range(8):
                        dma("pool", wu_sem[s_], wu[s_][:, k, :], w_up[k * 128:(k + 1) * 128, qq * 1024:(qq + 1) * 1024],
                            writes=[B_wu[s_]])

                def load_wd(qq):
                    s_ = qq % 2
                    for f_ in range(8):
                        dma("pool", wd_sem[s_], wd[s_][:, f_, :],
                            w_down[qq * 1024 + f_ * 128:qq * 1024 + (f_ + 1) * 128, :], writes=[B_wd[s_]])

                load_wu(0)
                for i in range(NT + 1):
                    dma("sp", hl_sem[i % 2], hres[0:tile_rows(i), i, :], y_dst(i), reads=[B_hd[i]], writes=[B_h[i]])
                NB3 = 5

                def pro_pre(i):
                    n = tile_rows(i)
                    b = i % NB3
                    op("act", lambda: ACT.activation(out=junk3[0:n, :], in_=hres[0:n, i, :], func=AF.Square,
                                                     accum_out=ss3[b][0:n, :]), reads=[B_h[i]], writes=[B_ss3[b]])
                    rstd_from_ss(ss3[b][0:n, :], rs3[b][0:n, :], n, float(D), B_ss3[b], B_rs3[b], 1)
                    op("dve", lambda: DVE.scalar_tensor_tensor(out=hn3[b][0:n, :], in0=hres[0:n, i, :],
                                                               scalar=rs3[b][0:n, 0:1], in1=gffn[0:n, :],
                                                               op0=ALU.mult, op1=ALU.mult),
                       reads=[B_h[i], B_rs3[b], B_gffn], writes=[B_hn3[b]])

                def pro_T(i):
                    n = tile_rows(i)
                    b = i % NB3
                    pbk = 6 + i % 2
                    ptr = ps_bf(pbk).rearrange("p (k t) -> p k t", k=8)
                    for k in range(8):
                        op("pe", lambda k=k: TE.transpose(out=ptr[:, k, 0:n], in_=hn3[b][0:n, k * 128:(k + 1) * 128],
                                                          identity=identb[0:n, 0:n]),
                           reads=[B_hn3[b], B_idb], writes=[PSB[pbk]], inc=(k == 7))
                    op("act", lambda: ACT.copy(out=hn2T[:, :, i * 128:i * 128 + n], in_=ptr[:, :, 0:n]),
                       reads=[PSB[pbk]], writes=[B_hn2T[i]])

                def blk_tiles(tb):
                    return [4 * tb + r for r in range(4)] if tb < 4 else [NT]

                for i in blk_tiles(0):
                    pro_pre(i)
                load_wd(0)
                for i in blk_tiles(0):
                    pro_T(i)
                pd_cnt = 0
                hb_cnt = 0
                for qq in range(4):
                    s_ = qq % 2
                    for tb in range(5):
                        ntok = 512 if tb < 4 else NS
                        col0 = tb * 512
                        tiles = [4 * tb + r for r in range(4)] if tb < 4 else [NT]
                        if qq == 0 and tb + 1 < 5:
                            for i_ in blk_tiles(tb + 1):
                                pro_pre(i_)
                        hb = hb_cnt % 2
                        hb_cnt += 1
                        for f_ in range(8):
                            pb = f_ % 2
                            for k in range(8):
                                op("pe", lambda k=k: TE.matmul(PS[pb][:, 0:ntok], lhsT=wu[s_][:, k, f_ * 128:(f_ + 1) * 128],
                                                               rhs=hn2T[:, k, col0:col0 + ntok], start=(k == 0), stop=(k == 7)),
                                   reads=[B_wu[s_]] + [B_hn2T[t_] for t_ in tiles], writes=[PSB[pb]], inc=(k == 7))
                            op("act", lambda: ACT.activation(out=rl[pb][:, 0:ntok], in_=PS[pb][:, 0:ntok], func=AF.Relu),
                               reads=[PSB[pb]], writes=[B_rl[pb]])
                            op("dve", lambda: DVE.tensor_tensor(out=hT[hb][:, f_, 0:ntok], in0=rl[pb][:, 0:ntok],
                                                                in1=rl[pb][:, 0:ntok], op=ALU.mult),
                               reads=[B_rl[pb]], writes=[B_hT[hb]])
                        for ti, i in enumerate(tiles):
                            n = tile_rows(i)
                            for cb in range(2):
                                pd = 2 + pd_cnt % 4
                                pd_cnt += 1
                                for f_ in range(8):
                                    op("pe", lambda f_=f_: TE.matmul(PS[pd][0:n, :], lhsT=hT[hb][:, f_, ti * 128:ti * 128 + n],
                                                                     rhs=wd[s_][:, f_, cb * 512:(cb + 1) * 512],
                                                                     start=(f_ == 0), stop=(f_ == 7)),
                                       reads=[B_hT[hb], B_wd[s_]], writes=[PSB[pd]], inc=(f_ == 7))
                                op("dve", lambda: DVE.tensor_tensor(out=hres[0:n, i, cb * 512:(cb + 1) * 512],
                                                                    in0=PS[pd][0:n, :],
                                                                    in1=hres[0:n, i, cb * 512:(cb + 1) * 512], op=ALU.add),
                                   reads=[PSB[pd], B_h[i]], writes=[B_h[i]])
                            if qq == 3:
                                dma("sp", y_sem[i % 2], y_dst(i), hres[0:n, i, :], reads=[B_h[i]], writes=[B_hd[i]])
                        if qq == 0 and tb + 1 < 5:
                            for i_ in blk_tiles(tb + 1):
                                pro_T(i_)
                        if qq == 0 and tb == 0:
                            load_wu(1)
                        if qq == 0 and tb == 1:
                            load_wd(1)
                    if qq + 2 < 4:
                        load_wu(qq + 2)
                        load_wd(qq + 2)

        Sy.finish()
    return nc, Sy


_NC_CACHE = {}


def kernel(**inputs):
    n_cores = 8
    n_phys = inputs["cache_k"].shape[1]
    n_pages = inputs["page_table"].shape[1]
    key = (n_phys, n_pages)
    if key not in _NC_CACHE:
        _NC_CACHE[key] = build_nc(n_phys, n_pages)[0]
    nc = _NC_CACHE[key]
    f = lambda a: np.ascontiguousarray(a)
    ck = f(inputs["cache_k"]).reshape(n_phys, PAGE, 512)
    cv = f(inputs["cache_v"]).reshape(n_phys, PAGE, 512)
    shared = {
        "cache_k": ck, "cache_v": cv,
        "norm_mix": f(inputs["norm_mix"]).reshape(1, D),
        "w_in": f(inputs["w_in"]).reshape(D, INC),
        "q_gain": f(inputs["q_gain"]).reshape(1, 64), "k_gain": f(inputs["k_gain"]).reshape(1, 64),
        "lambda_q1": f(inputs["lambda_q1"]).reshape(1, 64), "lambda_k1": f(inputs["lambda_k1"]).reshape(1, 64),
        "lambda_q2": f(inputs["lambda_q2"]).reshape(1, 64), "lambda_k2": f(inputs["lambda_k2"]).reshape(1, 64),
        "subln_gain": f(inputs["subln_gain"]).reshape(1, 128),
        "gv_gain": f(inputs["gv_gain"]).reshape(1, 512),
        "w_spatial": f(inputs["w_spatial"]).reshape(4, 128, 128),
        "b_spatial": f(inputs["b_spatial"]).reshape(4, 128),
        "w_out": f(inputs["w_out"]).reshape(D, D),
        "norm_ffn": f(inputs["norm_ffn"]).reshape(1, D),
        "w_up": f(inputs["w_up"]).reshape(D, DFF),
        "w_down": f(inputs["w_down"]).reshape(DFF, D),
    }
    in_maps = []
    for c in range(n_cores):
        m = dict(shared)
        m["x_prompt"] = f(inputs["x_prompt"][c])
        m["x_sample"] = f(inputs["x_sample"][4 * c:4 * c + 4]).reshape(NS, D)
        m["page_table"] = f(inputs["page_table"][4 * c:4 * c + 4]).reshape(1, 4 * n_pages).astype(np.int32)
        in_maps.append(m)
    res = run_bass_kernel_spmd(nc, in_maps, core_ids=list(range(n_cores)))
    R = res.results
    cat = lambda name: np.stack([np.asarray(r[name]) for r in R], axis=0)
    y_p = cat("y_prompt").reshape(8, S, D)
    y_s = cat("y_sample").reshape(32, 4, D)
    k_p = cat("k_prompt").reshape(1, 8, S, 4, 2, 64)
    v_p = cat("v_prompt").reshape(1, 8, S, 4, 128)
    k_s = cat("k_sample").reshape(1, 32, 4, 4, 2, 64)
    v_s = cat("v_sample").reshape(1, 32, 4, 4, 128)
    g_s = cat("gv_sample").reshape(1, 32, 4, 512)
    return (y_p, y_s, k_p, v_p, k_s, v_s, g_s)
```

```python
import contextlib
import math
import numpy as np
import concourse.bass as bass
import concourse.mybir as mybir
from concourse.bass_utils import run_bass_kernel_spmd

F32 = mybir.dt.float32
BF16 = mybir.dt.bfloat16
I32 = mybir.dt.int32
AF = mybir.ActivationFunctionType
ALU = mybir.AluOpType
AX = mybir.AxisListType

D = 1024
S = 2048
NT = 16
NS = 16
NTOK = S + NS
H = 4
INC = 2560
DFF = 4096
EPS = 1e-6
PAGE = 128
LAM_INIT = 0.8 - 0.6 * math.exp(-0.3 * 0)


class Buf:
    __slots__ = ("name", "w", "r")

    def __init__(self, name):
        self.name = name
        self.w = None
        self.r = []


class Sync:
    def __init__(self, nc, es):
        self.nc = nc
        self.es = es
        self.eng = {"pe": nc.tensor, "act": nc.scalar, "dve": nc.vector, "pool": nc.gpsimd, "sp": nc.sync}
        self.sem = {k: es.enter_context(nc.semaphore("s_" + k)) for k in self.eng}
        self.cnt = {k: 0 for k in self.eng}
        self.pending = {k: False for k in self.eng}
        self.seen = {k: {} for k in self.eng}
        self.dsems = []
        self.nwaits = 0
        self.defer = False
        self.queue = []

    def flush(self):
        q, self.queue = self.queue, []
        for kind, args, kw in q:
            getattr(self, kind)(*args, **kw)

    def dma_sem(self, name):
        s = self.es.enter_context(self.nc.semaphore(name))
        ent = [s, 0]
        self.dsems.append(ent)
        return ent

    def _wait(self, e, dep):
        kind, key, val = dep
        if kind == "e":
            if key == e and e == "pe":
                return
            sem = self.sem[key]
            sk = "e_" + key
            assert val <= self.cnt[key], (e, key, val, self.cnt[key])
        else:
            sem = key[0]
            sk = id(key)
            val = key[1]
        if self.seen[e].get(sk, 0) >= val:
            return
        self.seen[e][sk] = val
        self.eng[e].wait_ge(sem, val)
        self.nwaits += 1

    def _deps(self, e, reads, writes, same_war=False):
        for b in reads:
            if b.w is not None:
                self._wait(e, b.w)
        for b in writes:
            if b.w is not None:
                self._wait(e, b.w)
            for d in b.r:
                if d[0] == "e" and d[1] == e:
                    continue
                self._wait(e, d)

    def op(self, e, fn, reads=(), writes=(), inc=True):
        if self.defer:
            self.queue.append(("op", (e, fn, list(reads), list(writes), inc), {}))
            return None
        self._deps(e, reads, writes)
        ins = fn()
        if inc:
            self.cnt[e] += 1
            ins.then_inc(self.sem[e], 1)
            idx = self.cnt[e]
            self.pending[e] = False
        else:
            idx = self.cnt[e] + 1
            self.pending[e] = True
        tok = ("e", e, idx)
        for b in reads:
            b.r = [d for d in b.r if not (d[0] == "e" and d[1] == e)] + [tok]
        for b in writes:
            b.w = tok
            b.r = []
        return ins

    def dma(self, q, ent, out, in_, reads=(), writes=(), **kw):
        if self.defer:
            self.queue.append(("dma", (q, ent, out, in_, list(reads), list(writes)), kw))
            return None
        self._deps(q, reads, writes)
        ins = self.eng[q].dma_start(out=out, in_=in_, **kw)
        ent[1] += 16
        ins.then_inc(ent[0], 16)
        tok = ("d", ent, ent[1])
        for b in reads:
            b.r = b.r + [tok]
        for b in writes:
            b.w = tok
            b.r = []
        return ins

    def dma_fn(self, q, ent, fn, reads=(), writes=()):
        self._deps(q, reads, writes)
        ins = fn()
        ent[1] += 16
        ins.then_inc(ent[0], 16)
        tok = ("d", ent, ent[1])
        for b in reads:
            b.r = b.r + [tok]
        for b in writes:
            b.w = tok
            b.r = []
        return ins

    def barrier(self):
        for e in self.eng:
            assert not self.pending[e], e
        for e in self.eng:
            for k in self.eng:
                if k != e and self.cnt[k] > 0:
                    self._wait(e, ("e", k, self.cnt[k]))
            for ent in self.dsems:
                if ent[1] > 0:
                    self._wait(e, ("d", ent, ent[1]))

    def finish(self):
        for ent in self.dsems:
            if ent[1] > 0:
                self._wait("sp", ("d", ent, ent[1]))
        for k in self.eng:
            if k != "sp" and self.cnt[k] > 0:
                self._wait("sp", ("e", k, self.cnt[k]))


def build_nc(n_phys=5120, n_pages=128, stop_after=99):
    nc = bass.Bass("TRN2", target_bir_lowering=False)
    dt = nc.dram_tensor
    xp = dt("x_prompt", [S, D], F32, kind="ExternalInput").ap()
    xs = dt("x_sample", [NS, D], F32, kind="ExternalInput").ap()
    cache_k = dt("cache_k", [n_phys, PAGE, 512], F32, kind="ExternalInput").ap()
    cache_v = dt("cache_v", [n_phys, PAGE, 512], F32, kind="ExternalInput").ap()
    ptab = dt("page_table", [1, 4 * n_pages], I32, kind="ExternalInput").ap()
    norm_mix = dt("norm_mix", [1, D], F32, kind="ExternalInput").ap()
    w_in = dt("w_in", [D, INC], F32, kind="ExternalInput").ap()
    q_gain = dt("q_gain", [1, 64], F32, kind="ExternalInput").ap()
    k_gain = dt("k_gain", [1, 64], F32, kind="ExternalInput").ap()
    lq1 = dt("lambda_q1", [1, 64], F32, kind="ExternalInput").ap()
    lk1 = dt("lambda_k1", [1, 64], F32, kind="ExternalInput").ap()
    lq2 = dt("lambda_q2", [1, 64], F32, kind="ExternalInput").ap()
    lk2 = dt("lambda_k2", [1, 64], F32, kind="ExternalInput").ap()
    subln = dt("subln_gain", [1, 128], F32, kind="ExternalInput").ap()
    gv_gain = dt("gv_gain", [1, 512], F32, kind="ExternalInput").ap()
    w_sp = dt("w_spatial", [4, 128, 128], F32, kind="ExternalInput").ap()
    b_sp = dt("b_spatial", [4, 128], F32, kind="ExternalInput").ap()
    w_out = dt("w_out", [D, D], F32, kind="ExternalInput").ap()
    norm_ffn = dt("norm_ffn", [1, D], F32, kind="ExternalInput").ap()
    w_up = dt("w_up", [D, DFF], F32, kind="ExternalInput").ap()
    w_down = dt("w_down", [DFF, D], F32, kind="ExternalInput").ap()

    y_p = dt("y_prompt", [S, D], F32, kind="ExternalOutput").ap()
    y_s = dt("y_sample", [NS, D], F32, kind="ExternalOutput").ap()
    k_p = dt("k_prompt", [S, 512], F32, kind="ExternalOutput").ap()
    v_p = dt("v_prompt", [S, 512], F32, kind="ExternalOutput").ap()
    k_s = dt("k_sample", [NS, 512], F32, kind="ExternalOutput").ap()
    v_s = dt("v_sample", [NS, 512], F32, kind="ExternalOutput").ap()
    g_s = dt("gv_sample", [NS, 512], F32, kind="ExternalOutput").ap()

    def bc(ap1, shape_steps):
        return bass.AP(ap1.tensor, 0, shape_steps)

    es = contextlib.ExitStack()
    with es:
        Sy = Sync(nc, es)
        op, dma = Sy.op, Sy.dma
        TE, ACT, DVE, POOL = nc.tensor, nc.scalar, nc.vector, nc.gpsimd

        cur = [es]

        def sb(name, shape, dtype, stack=None):
            return (stack or cur[0]).enter_context(nc.sbuf_tensor(name, shape, dtype))

        PS = [es.enter_context(nc.psum_tensor(f"ps{i}", [128, 512], F32)) for i in range(8)]
        PSB = [Buf(f"ps{i}") for i in range(8)]

        def ps_bf(i):
            return PS[i][:].bitcast(BF16)

        identf = sb("identf", [128, 128], F32); B_idf = Buf("identf")
        identb = sb("identb", [128, 128], BF16); B_idb = Buf("identb")
        nhalf = sb("nhalf", [128, 16], F32); B_nh = Buf("nhalf")
        a_stack = contextlib.ExitStack()
        cur[0] = a_stack
        c_sem = Sy.dma_sem("c_sem")
        gmix = sb("gmix", [128, D], F32, a_stack); B_gmix = Buf("gmix")
        qkg = sb("qkg", [128, 2, 8, 64], F32, a_stack); B_qkg = Buf("qkg")
        gvg = sb("gvg", [128, 512], F32, a_stack); B_gvg = Buf("gvg")
        sbl = sb("sbl", [128, 4, 128], F32, a_stack); B_sbl = Buf("sbl")
        lam4 = sb("lam4", [128, 4, 64], F32, a_stack); B_lam4 = Buf("lam4")
        wsp = sb("wsp", [128, 4, 128], F32, a_stack); B_wsp = Buf("wsp")
        bsp = sb("bsp", [128, 4], F32); B_bsp = Buf("bsp")
        bspn = sb("bspn", [4, 128], F32); B_bspn = Buf("bspn")
        c2_sem = Sy.dma_sem("c2_sem")
        bsp_s = sb("bsp_s", [16, 4], F32); B_bsps = Buf("bsp_s")
        wblk_f = sb("wblk_f", [16, 4, 16], F32, a_stack); B_wblkf = Buf("wblk_f")
        ptb = sb("ptb", [1, 4 * n_pages], I32); B_ptb = Buf("ptb")
        wT = sb("wT", [128, 4, 128], BF16); B_wT = Buf("wT")
        wblk = sb("wblk", [16, 4, 16], BF16); B_wblk = Buf("wblk")
        lam = sb("lam", [128, 4], F32); B_lam = Buf("lam")
        ljunk = sb("ljunk", [128, 64], F32)
        msk_s = sb("msk_s", [16, 4, 8, 4], F32, a_stack); B_msk = Buf("msk_s")
        coef = sb("coef", [32, 4], F32); B_coef = Buf("coef")
        oneh = sb("oneh", [32, 4, 128], F32, a_stack); B_oneh = Buf("oneh")
        sel = sb("sel", [32, 4, 16], F32); B_sel = Buf("sel")

        Sy.defer = True
        with nc.allow_non_contiguous_dma(reason="tiny constant loads"):
            dma("sp", c_sem, gmix[:], bc(norm_mix, [[0, 128], [1, D]]), writes=[B_gmix])
            dma("sp", c_sem, qkg[:, 0, :, :], bc(q_gain, [[0, 128], [0, 8], [1, 64]]), writes=[B_qkg])
            dma("sp", c_sem, qkg[:, 1, :, :], bc(k_gain, [[0, 128], [0, 8], [1, 64]]), writes=[B_qkg])
            dma("sp", c_sem, gvg[:], bc(gv_gain, [[0, 128], [1, 512]]), writes=[B_gvg])
            dma("sp", c_sem, sbl[:], bc(subln, [[0, 128], [0, 4], [1, 128]]), writes=[B_sbl])
            for i_, l_ in enumerate((lq1, lk1, lq2, lk2)):
                dma("sp", c_sem, lam4[:, i_, :], bc(l_, [[0, 128], [1, 64]]), writes=[B_lam4])
            dma("sp", c_sem, wsp[:], w_sp.rearrange("g t s -> t g s"), writes=[B_wsp])
            dma("sp", c_sem, bspn[:], b_sp, writes=[B_bspn])

        op("pool", lambda: POOL.memset(identf[:], 0.0), writes=[B_idf])
        op("pool", lambda: POOL.affine_select(out=identf[:], in_=identf[:], pattern=[[-1, 128]],
                                              compare_op=ALU.not_equal, fill=1.0, base=0, channel_multiplier=1),
           reads=[B_idf], writes=[B_idf])
        op("pool", lambda: POOL.tensor_copy(out=identb[:], in_=identf[:]), reads=[B_idf], writes=[B_idb])
        op("pool", lambda: POOL.memset(nhalf[:], -0.5), writes=[B_nh])
        op("pe", lambda: TE.transpose(out=PS[1][:, 0:4], in_=bspn[0:4, :], identity=identf[0:4, 0:4]),
           reads=[B_bspn, B_idf], writes=[PSB[1]], inc=True)
        op("act", lambda: ACT.copy(out=bsp[:], in_=PS[1][:, 0:4]), reads=[PSB[1]], writes=[B_bsp])
        for j in range(4):
            dma("sp", c2_sem, bsp_s[4 * j:4 * j + 4, :], bsp[0:4, :], reads=[B_bsp], writes=[B_bsps])

        ldot = sb("ldot", [128, 2], F32); B_ldot = Buf("ldot")
        op("dve", lambda: DVE.tensor_tensor(out=lam4[:, 0, :], in0=lam4[:, 0, :], in1=lam4[:, 1, :], op=ALU.mult),
           reads=[B_lam4], writes=[B_lam4])
        op("dve", lambda: DVE.tensor_tensor(out=lam4[:, 2, :], in0=lam4[:, 2, :], in1=lam4[:, 3, :], op=ALU.mult),
           reads=[B_lam4], writes=[B_lam4])
        op("dve", lambda: DVE.tensor_reduce(out=ldot[:, 0:1], in_=lam4[:, 0, :], axis=AX.X, op=ALU.add),
           reads=[B_lam4], writes=[B_ldot])
        op("dve", lambda: DVE.tensor_reduce(out=ldot[:, 1:2], in_=lam4[:, 2, :], axis=AX.X, op=ALU.add),
           reads=[B_lam4], writes=[B_ldot])
        op("act", lambda: ACT.activation(out=ldot[:], in_=ldot[:], func=AF.Exp), reads=[B_ldot], writes=[B_ldot])
        op("dve", lambda: DVE.tensor_tensor(out=lam[:, 0:1], in0=ldot[:, 0:1], in1=ldot[:, 1:2], op=ALU.subtract),
           reads=[B_ldot], writes=[B_lam])
        op("dve", lambda: DVE.tensor_scalar(out=lam[:, 0:1], in0=lam[:, 0:1], scalar1=float(LAM_INIT), scalar2=None,
                                            op0=ALU.add), reads=[B_lam], writes=[B_lam])
        op("dve", lambda: DVE.tensor_scalar(out=lam[:, 1:2], in0=lam[:, 0:1], scalar1=-1.0, scalar2=None,
                                            op0=ALU.mult), reads=[B_lam], writes=[B_lam])

        op("dve", lambda: DVE.tensor_scalar(out=sbl[:], in0=sbl[:], scalar1=float(1.0 - LAM_INIT), scalar2=None,
                                            op0=ALU.mult), reads=[B_sbl], writes=[B_sbl])

        op("pool", lambda: POOL.affine_select(out=wsp[:], in_=wsp[:], pattern=[[0, 4], [-1, 128]],
                                              compare_op=ALU.is_ge, fill=0.0, base=0, channel_multiplier=1),
           reads=[B_wsp], writes=[B_wsp])
        for g in range(4):
            op("pe", lambda g=g: TE.transpose(out=PS[0][:, g * 128:(g + 1) * 128], in_=wsp[:, g, :], identity=identf[:]),
               reads=[B_wsp, B_idf], writes=[PSB[0]], inc=(g == 3))
        op("act", lambda: ACT.copy(out=wT[:], in_=PS[0][:].rearrange("p (g t) -> p g t", g=4)),
           reads=[PSB[0]], writes=[B_wT])
        op("pool", lambda: POOL.memset(wblk_f[:], 0.0), writes=[B_wblkf])
        wT4 = sb("wT4", [4, 4, 4], F32); B_wT4 = Buf("wT4")
        op("act", lambda: ACT.copy(out=wT4[:], in_=PS[0][0:4, :].rearrange("p (g t) -> p g t", g=4)[:, :, 0:4]),
           reads=[PSB[0]], writes=[B_wT4])
        for j in range(4):
            dma("sp", c2_sem, wblk_f[4 * j:4 * j + 4, :, 4 * j:4 * j + 4], wT4[:], reads=[B_wT4], writes=[B_wblkf])
        op("pool", lambda: POOL.tensor_copy(out=wblk[:], in_=wblk_f[:]), reads=[B_wblkf], writes=[B_wblk])

        op("pool", lambda: POOL.memset(msk_s[:], 1.0), writes=[B_msk])
        op("pool", lambda: POOL.affine_select(out=msk_s[:], in_=msk_s[:], pattern=[[-4, 4], [0, 8], [0, 4]],
                                              compare_op=ALU.is_ge, fill=0.0, base=0, channel_multiplier=1),
           reads=[B_msk], writes=[B_msk])
        op("pool", lambda: POOL.affine_select(out=msk_s[:], in_=msk_s[:], pattern=[[4, 4], [0, 8], [1, 4]],
                                              compare_op=ALU.is_ge, fill=0.0, base=0, channel_multiplier=-1),
           reads=[B_msk], writes=[B_msk])
        csel = sb("csel", [32, 4, 2], F32); B_csel = Buf("csel")
        op("pool", lambda: POOL.memset(csel[:], 1.0), writes=[B_csel])
        op("pool", lambda: POOL.affine_select(out=csel[:], in_=csel[:], pattern=[[-8, 4], [-4, 2]],
                                              compare_op=ALU.is_ge, fill=0.0, base=0, channel_multiplier=1),
           reads=[B_csel], writes=[B_csel])
        op("pool", lambda: POOL.affine_select(out=csel[:], in_=csel[:], pattern=[[8, 4], [4, 2]],
                                              compare_op=ALU.is_ge, fill=0.0, base=3, channel_multiplier=-1),
           reads=[B_csel], writes=[B_csel])
        cvec = sb("cvec", [32, 2], F32); B_cvec = Buf("cvec")
        op("dve", lambda: DVE.tensor_reduce(out=cvec[:], in_=csel[:].rearrange("p h c -> p c h"), axis=AX.X, op=ALU.add),
           reads=[B_csel], writes=[B_cvec])
        op("dve", lambda: DVE.scalar_tensor_tensor(out=coef[:, 0:1], in0=cvec[:, 1:2], scalar=lam[0:32, 1:2],
                                                   in1=cvec[:, 0:1], op0=ALU.mult, op1=ALU.add),
           reads=[B_cvec, B_lam], writes=[B_coef])
        ohs = sb("ohs", [32, 4], F32); B_ohs = Buf("ohs")
        op("dve", lambda: DVE.tensor_reduce(out=ohs[:], in_=csel[:], axis=AX.X, op=ALU.add), reads=[B_csel], writes=[B_ohs])
        op("dve", lambda: DVE.tensor_copy(out=oneh[:], in_=ohs[:].unsqueeze(2).to_broadcast([32, 4, 128])),
           reads=[B_ohs], writes=[B_oneh])
        op("pool", lambda: POOL.memset(sel[:], 0.0), writes=[B_sel])
        selq = sb("selq", [32, 8, 4, 16], F32, a_stack); B_selq = Buf("selq")
        op("pool", lambda: POOL.memset(selq[:], 1.0), writes=[B_selq])
        op("pool", lambda: POOL.affine_select(out=selq[:], in_=selq[:], pattern=[[4, 8], [-4, 4], [1, 16]],
                                              compare_op=ALU.is_equal, fill=0.0, base=0, channel_multiplier=-1),
           reads=[B_selq], writes=[B_selq])
        op("pool", lambda: POOL.affine_select(out=selq[:], in_=selq[:], pattern=[[-4, 8], [0, 4], [0, 16]],
                                              compare_op=ALU.is_ge, fill=0.0, base=0, channel_multiplier=1),
           reads=[B_selq], writes=[B_selq])
        op("pool", lambda: POOL.affine_select(out=selq[:], in_=selq[:], pattern=[[4, 8], [0, 4], [0, 16]],
                                              compare_op=ALU.is_ge, fill=0.0, base=3, channel_multiplier=-1),
           reads=[B_selq], writes=[B_selq])
        op("dve", lambda: DVE.tensor_reduce(out=sel[:].rearrange("p j m -> p (j m)"),
                                            in_=selq[:].rearrange("p q j m -> p (j m) q"), axis=AX.X, op=ALU.add),
           reads=[B_selq], writes=[B_sel])

        qTz = sb("qTz", [128, 2, H, S], BF16, a_stack); B_qT = [Buf(f"qT{i}") for i in range(NT)]
        op("dve", lambda: DVE.memset(qTz[64:128, 0], 0.0), writes=B_qT)
        op("dve", lambda: DVE.memset(qTz[0:64, 1], 0.0), writes=B_qT)
        kT = sb("kT", [128, H, S], BF16, a_stack); B_kT = [Buf(f"kT{i}") for i in range(NT)]
        qTs = sb("qTs", [128, H, NS], BF16, a_stack); B_qTs = Buf("qTs")
        kTs = sb("kTs", [128, H, NS], BF16, a_stack); B_kTs = Buf("kTs")
        V1 = sb("V1", [128, NT, H, 130], BF16, a_stack); B_V1 = [Buf(f"V1{i}") for i in range(NT)]
        Vs = sb("Vs", [16, H, 130], BF16, a_stack); B_Vs = Buf("Vs")
        mixb = sb("mixb", [128, NT + 1, 512], BF16, a_stack); B_mix = [Buf(f"mix{i}") for i in range(NT + 1)]
        op("pool", lambda: POOL.memset(V1[:, :, :, 128:130], 1.0), writes=B_V1)
        op("pool", lambda: POOL.memset(Vs[:, :, 128:130], 1.0), writes=[B_Vs])

        def tile_rows(i):
            return 128 if i < NT else NS

        def x_src(i):
            return xp[i * 128:(i + 1) * 128, :] if i < NT else xs[:, :]

        def rstd_from_ss(ss_ap, rs_ap, n, width, B_ss, B_rs, ncol):
            op("pool", lambda: POOL.tensor_scalar(out=rs_ap, in0=ss_ap, scalar1=1.0 / width, scalar2=EPS,
                                                  op0=ALU.mult, op1=ALU.add), reads=[B_ss], writes=[B_rs])
            op("pool", lambda: POOL.tensor_tensor(out=rs_ap, in0=rs_ap, in1=nhalf[0:n, 0:ncol], op=ALU.pow),
               reads=[B_rs, B_nh], writes=[B_rs])

        p1 = contextlib.ExitStack()
        with p1:
            win = sb("win", [128, 8, INC], BF16, p1); B_win = [Buf(f"win{k}") for k in range(8)]
            w_sem = Sy.dma_sem("w_sem")
            Sy.defer = False
            wstg = [sb(f"wstg{i}", [128, 1280], F32, p1) for i in range(2)]; B_wstg = [Buf(f"wstg{i}") for i in range(2)]
            ws_sem = [Sy.dma_sem(f"ws_sem{i}") for i in range(2)]
            hw_cnt = 0
            for k in range(8):
                for hh in range(2):
                    if hh == 0:
                        Sy.dma("pool", w_sem, win[:, k, 0:1280], w_in[k * 128:(k + 1) * 128, 0:1280], writes=[B_win[k]])
                    else:
                        sl = hw_cnt % 2
                        Sy.dma("sp", ws_sem[sl], wstg[sl][:, :], w_in[k * 128:(k + 1) * 128, 1280:2560],
                               writes=[B_wstg[sl]])
                        if hw_cnt % 2 == 0:
                            Sy.op("dve", lambda: DVE.tensor_copy(out=win[:, k, 1280:2560], in_=wstg[sl][:, :]),
                                  reads=[B_wstg[sl]], writes=[B_win[k]])
                        else:
                            Sy.op("act", lambda: ACT.copy(out=win[:, k, 1280:2560], in_=wstg[sl][:, :]),
                                  reads=[B_wstg[sl]], writes=[B_win[k]])
                        hw_cnt += 1
            with nc.allow_non_contiguous_dma(reason="tiny constant loads"):
                Sy.flush()

            def dbl(name, shape, dtype):
                return ([sb(f"{name}{i}", shape, dtype, p1) for i in range(2)], [Buf(f"{name}{i}") for i in range(2)])

            xt, B_xt = dbl("xt", [128, D], F32)
            x_sem = [Sy.dma_sem(f"x_sem{i}") for i in range(2)]
            junk = sb("junk", [128, D], BF16, p1)
            ss, B_ss = dbl("ss", [128, 1], F32)
            rs, B_rs = dbl("rs", [128, 1], F32)
            hn, B_hn = dbl("hn", [128, D], BF16)
            hnT, B_hnT = dbl("hnT", [128, 8, 128], BF16)
            sq = sb("sq", [128, 2, 512], F32, p1); B_sq = Buf("sq")
            ssq, B_ssq = dbl("ssq", [128, 16], F32)
            rq, B_rq = dbl("rq", [128, 16], F32)
            qkt, B_qkt = dbl("qkt", [128, 2, 8, 64], F32)
            qkb, B_qkb = dbl("qkb", [128, D], BF16)
            vst, B_vst = dbl("vst", [128, 512], F32)
            o_sem = [Sy.dma_sem(f"o_sem{i}") for i in range(2)]
            ug, B_ug = dbl("ug", [128, 512], F32)
            gg, B_gg = dbl("gg", [128, 4, 128], F32)
            gss, B_gss = dbl("gss", [128, 4], F32)
            grs, B_grs = dbl("grs", [128, 4], F32)
            gvb, B_gvb = dbl("gvb", [128, 512], BF16)

            def pre(i):
                n = tile_rows(i); b = i % 2
                if i + 1 <= NT:
                    dma("sp", x_sem[1 - b], xt[1 - b][0:tile_rows(i + 1), :], x_src(i + 1), writes=[B_xt[1 - b]])
                op("act", lambda: ACT.activation(out=junk[0:n, :], in_=xt[b][0:n, :], func=AF.Square,
                                                 accum_out=ss[b][0:n, :]), reads=[B_xt[b]], writes=[B_ss[b]])
                rstd_from_ss(ss[b][0:n, :], rs[b][0:n, :], n, float(D), B_ss[b], B_rs[b], 1)
                op("dve", lambda: DVE.scalar_tensor_tensor(out=hn[b][0:n, :], in0=xt[b][0:n, :], scalar=rs[b][0:n, 0:1],
                                                           in1=gmix[0:n, :], op0=ALU.mult, op1=ALU.mult),
                   reads=[B_xt[b], B_rs[b], B_gmix], writes=[B_hn[b]])

            def TH(i):
                n = tile_rows(i); b = i % 2
                ptr = ps_bf(5).rearrange("p (k t) -> p k t", k=8)
                for k in range(8):
                    op("pe", lambda k=k: TE.transpose(out=ptr[:, k, 0:n], in_=hn[b][0:n, k * 128:(k + 1) * 128],
                                                      identity=identb[0:n, 0:n]),
                       reads=[B_hn[b], B_idb], writes=[PSB[5]], inc=(k == 7))
                op("act", lambda: ACT.copy(out=hnT[b][:, :, 0:n], in_=ptr[:, :, 0:n]), reads=[PSB[5]], writes=[B_hnT[b]])

            def MM(i):
                n = tile_rows(i); b = i % 2
                for cb in (2, 3, 4, 0, 1):
                    for k in range(8):
                        op("pe", lambda cb=cb, k=k: TE.matmul(PS[cb][0:n, :], lhsT=hnT[b][:, k, 0:n],
                                                              rhs=win[:, k, cb * 512:(cb + 1) * 512],
                                                              start=(k == 0), stop=(k == 7)),
                           reads=[B_hnT[b], B_win[k]], writes=[PSB[cb]], inc=(k == 7))

            def back_a(i):
                n = tile_rows(i); b = i % 2
                op("act", lambda: ACT.copy(out=vst[b][0:n, :], in_=PS[2][0:n, :]), reads=[PSB[2]], writes=[B_vst[b]])
                op("act", lambda: ACT.activation(out=ug[b][0:n, :], in_=PS[3][0:n, :], func=AF.Gelu_apprx_tanh),
                   reads=[PSB[3]], writes=[B_ug[b]])
                op("act", lambda: ACT.activation(out=gg[b][0:n].rearrange("p g d -> p (g d)"), in_=PS[4][0:n, :],
                                                 func=AF.Gelu_apprx_tanh), reads=[PSB[4]], writes=[B_gg[b]])
                for cb in range(2):
                    op("act", lambda cb=cb: ACT.activation(out=sq[0:n, cb, :], in_=PS[cb][0:n, :], func=AF.Square),
                       reads=[PSB[cb]], writes=[B_sq])
                for g in range(4):
                    op("act", lambda g=g: ACT.activation(out=junk[0:n, g * 128:(g + 1) * 128], in_=gg[b][0:n, g, :],
                                                         func=AF.Square, accum_out=gss[b][0:n, g:g + 1]),
                       reads=[B_gg[b]], writes=[B_gss[b]])
                op("dve", lambda: DVE.tensor_reduce(out=ssq[b][0:n, :], in_=sq[0:n].rearrange("p a (g d) -> p (a g) d", d=64),
                                                    axis=AX.X, op=ALU.add), reads=[B_sq], writes=[B_ssq[b]])
                rstd_from_ss(ssq[b][0:n, :], rq[b][0:n, :], n, 64.0, B_ssq[b], B_rq[b], 16)
                rstd_from_ss(gss[b][0:n, :], grs[b][0:n, :], n, 128.0, B_gss[b], B_grs[b], 4)
                vdst = v_p[i * 128:(i + 1) * 128, :] if i < NT else v_s[:, :]
                dma("sp", o_sem[b], vdst, vst[b][0:n, :], reads=[B_vst[b]])
                if i < NT:
                    op("dve", lambda: DVE.tensor_copy(out=V1[:, i, :, 0:128],
                                                      in_=vst[b][:, :].rearrange("p (h d) -> p h d", h=4)),
                       reads=[B_vst[b]], writes=[B_V1[i]])
                else:
                    op("dve", lambda: DVE.tensor_copy(out=Vs[:, :, 0:128],
                                                      in_=vst[b][0:n, :].rearrange("p (h d) -> p h d", h=4)),
                       reads=[B_vst[b]], writes=[B_Vs])
                for cb in range(2):
                    op("dve", lambda cb=cb: DVE.tensor_tensor(
                        out=qkt[b][0:n, cb], in0=PS[cb][0:n, :].rearrange("p (g d) -> p g d", d=64),
                        in1=rq[b][0:n, cb * 8:(cb + 1) * 8].unsqueeze(2).to_broadcast([n, 8, 64]), op=ALU.mult),
                       reads=[PSB[cb], B_rq[b]], writes=[B_qkt[b]])
                op("dve", lambda: DVE.tensor_tensor(out=gg[b][0:n], in0=gg[b][0:n],
                                                    in1=grs[b][0:n, :].unsqueeze(2).to_broadcast([n, 4, 128]), op=ALU.mult),
                   reads=[B_gg[b], B_grs[b]], writes=[B_gg[b]])

            def back_b(i):
                n = tile_rows(i); b = i % 2
                op("dve", lambda: DVE.tensor_tensor(out=qkt[b][0:n], in0=qkt[b][0:n], in1=qkg[0:n], op=ALU.mult),
                   reads=[B_qkt[b], B_qkg], writes=[B_qkt[b]])
                op("act", lambda: ACT.copy(out=qkb[b][0:n, :], in_=qkt[b][0:n].rearrange("p a g d -> p (a g d)")),
                   reads=[B_qkt[b]], writes=[B_qkb[b]])
                kdst = k_p[i * 128:(i + 1) * 128, :] if i < NT else k_s[:, :]
                dma("sp", o_sem[b], kdst, qkt[b][0:n, 1].rearrange("p g d -> p (g d)"), reads=[B_qkt[b]])
                op("pool", lambda: POOL.tensor_tensor(out=gg[b][0:n].rearrange("p g d -> p (g d)"),
                                                      in0=gg[b][0:n].rearrange("p g d -> p (g d)"), in1=gvg[0:n, :],
                                                      op=ALU.mult), reads=[B_gg[b], B_gvg], writes=[B_gg[b]])
                if i == NT:
                    dma("sp", o_sem[b], g_s[:, :], gg[b][0:n].rearrange("p g d -> p (g d)"), reads=[B_gg[b]])
                op("act", lambda: ACT.copy(out=gvb[b][0:n, :], in_=gg[b][0:n].rearrange("p g d -> p (g d)")),
                   reads=[B_gg[b]], writes=[B_gvb[b]])
                for g in range(4):
                    lw = wT[:, g, :] if i < NT else wblk[:, g, :]
                    op("pe", lambda g=g, lw=lw: TE.matmul(PS[6][0:n, g * 128:(g + 1) * 128], lhsT=lw,
                                                          rhs=gvb[b][0:n, g * 128:(g + 1) * 128], start=True, stop=True),
                       reads=[B_gvb[b], B_wT, B_wblk], writes=[PSB[6]], inc=(g == 3))
                bb = bsp if i < NT else bsp_s
                for g in range(4):
                    op("dve", lambda g=g: DVE.scalar_tensor_tensor(
                        out=mixb[0:n, i, g * 128:(g + 1) * 128], in0=PS[6][0:n, g * 128:(g + 1) * 128],
                        scalar=bb[0:n, g:g + 1], in1=ug[b][0:n, g * 128:(g + 1) * 128], op0=ALU.add, op1=ALU.mult),
                       reads=[PSB[6], B_bsp, B_bsps, B_ug[b]], writes=[B_mix[i]])
                ptq = ps_bf(7).rearrange("p (k t) -> p k t", k=8)
                for k in range(8):
                    op("pe", lambda k=k: TE.transpose(out=ptq[:, k, 0:n], in_=qkb[b][0:n, k * 128:(k + 1) * 128],
                                                      identity=identb[0:n, 0:n]),
                       reads=[B_qkb[b], B_idb], writes=[PSB[7]], inc=(k == 7))
                if i < NT:
                    op("dve", lambda: DVE.tensor_copy(out=qTz[0:64, 0, :, i * 128:(i + 1) * 128], in_=ptq[0:64, 0:4, :]),
                       reads=[PSB[7]], writes=[B_qT[i]])
                    op("dve", lambda: DVE.tensor_copy(out=qTz[64:128, 1, :, i * 128:(i + 1) * 128], in_=ptq[64:128, 0:4, :]),
                       reads=[PSB[7]], writes=[B_qT[i]])
                    op("act", lambda: ACT.copy(out=kT[:, :, i * 128:(i + 1) * 128], in_=ptq[:, 4:8, :]),
                       reads=[PSB[7]], writes=[B_kT[i]])
                else:
                    op("dve", lambda: DVE.tensor_copy(out=qTs[:, :, :], in_=ptq[:, 0:4, 0:n]),
                       reads=[PSB[7]], writes=[B_qTs])
                    op("act", lambda: ACT.copy(out=kTs[:, :, :], in_=ptq[:, 4:8, 0:n]),
                       reads=[PSB[7]], writes=[B_kTs])

            dma("sp", x_sem[0], xt[0][:, :], x_src(0), writes=[B_xt[0]])
            pre(0)
            TH(0)
            for i in range(NT + 1):
                if i + 1 <= NT:
                    pre(i + 1)
                MM(i)
                if i + 1 <= NT:
                    TH(i + 1)
                if i >= 1:
                    back_b(i - 1)
                back_a(i)
            back_b(NT)
            Sy.barrier()
        B_hd = [Buf(f"hd{i}") for i in range(NT + 1)]

        def y_dst(i):
            return y_p[i * 128:(i + 1) * 128, :] if i < NT else y_s[:, :]

        p2 = contextlib.ExitStack()
        with p2:
            wout = sb("wout", [128, 8, D], BF16, p2); B_wout = Buf("wout")
            wo_sem = Sy.dma_sem("wo_sem")
            att = sb("att", [128, 4, H, 128], F32, p2); B_att = [Buf(f"att{r}") for r in range(4)]
            eT = [sb(f"eT{i}", [128, 512], BF16, p2) for i in range(4)]; B_eT = [Buf(f"eT{i}") for i in range(4)]
            rz = [sb(f"rz{i}", [128, 4], F32, p2) for i in range(2)]; B_rz = [Buf(f"rz{i}") for i in range(2)]
            junk2 = sb("junk2", [128, 512], BF16, p2)
            tri01 = sb("tri01", [128, 128], BF16, p2); B_tri = Buf("tri01")
            op("pool", lambda: POOL.memset(tri01[:], 1.0), writes=[B_tri])
            op("pool", lambda: POOL.affine_select(out=tri01[:], in_=tri01[:], pattern=[[1, 128]], compare_op=ALU.is_ge,
                                                  fill=0.0, base=0, channel_multiplier=-1), reads=[B_tri], writes=[B_tri])
            ass = sb("ass", [128, 4], F32, p2); B_ass = Buf("ass")
            ars = sb("ars", [128, 4], F32, p2); B_ars = Buf("ars")
            at1 = sb("at1", [128, H, 128], F32, p2); B_at1 = Buf("at1")
            catb = sb("catb", [128, H, 128], BF16, p2); B_catb = Buf("catb")
            catT = sb("catT", [128, 8, 128], BF16, p2); B_catT = Buf("catT")
            _xt2 = sb("xt2", [128, D], F32, p2); _Bxt2 = Buf("xt2"); _x2s = Sy.dma_sem("x2_sem")
            xt2 = [_xt2, _xt2]; B_xt2 = [_Bxt2, _Bxt2]
            x2_sem = [_x2s, _x2s]
            hst = [sb(f"hst{i}", [128, D], F32, p2) for i in range(2)]; B_hst = [Buf(f"hst{i}") for i in range(2)]
            h_sem = [Sy.dma_sem(f"h_sem{i}") for i in range(2)]

            def epilogue(i, n, att_ap, B_a):
                b = i % 2
                dma("sp", x2_sem[b], xt2[b][0:n, :], x_src(i), writes=[B_xt2[b]])
                for h in range(H):
                    op("act", lambda h=h: ACT.activation(out=junk2[0:n, h * 128:(h + 1) * 128], in_=att_ap[:, h, :],
                                                         func=AF.Square, accum_out=ass[0:n, h:h + 1]),
                       reads=[B_a], writes=[B_ass])
                rstd_from_ss(ass[0:n, :], ars[0:n, :], n, 128.0, B_ass, B_ars, 4)
                op("dve", lambda: DVE.tensor_tensor(out=at1[0:n], in0=att_ap,
                                                    in1=ars[0:n, :].unsqueeze(2).to_broadcast([n, H, 128]), op=ALU.mult),
                   reads=[B_a, B_ars], writes=[B_at1])
                op("pool", lambda: POOL.tensor_tensor(out=catb[0:n], in0=at1[0:n], in1=sbl[0:n], op=ALU.mult),
                   reads=[B_at1, B_sbl], writes=[B_catb])
                ptr = ps_bf(4).rearrange("p (k t) -> p k t", k=8)
                for k in range(4):
                    op("pe", lambda k=k: TE.transpose(out=ptr[:, k, 0:n], in_=catb[0:n, k, :], identity=identb[0:n, 0:n]),
                       reads=[B_catb, B_idb], writes=[PSB[4]], inc=False)
                for k in range(4):
                    op("pe", lambda k=k: TE.transpose(out=ptr[:, 4 + k, 0:n], in_=mixb[0:n, i, k * 128:(k + 1) * 128],
                                                      identity=identb[0:n, 0:n]),
                       reads=[B_mix[i], B_idb], writes=[PSB[4]], inc=(k == 3))
                op("act", lambda: ACT.copy(out=catT[:, :, 0:n], in_=ptr[:, :, 0:n]), reads=[PSB[4]], writes=[B_catT])
                for cb in range(2):
                    for k in range(8):
                        op("pe", lambda cb=cb, k=k: TE.matmul(PS[cb][0:n, :], lhsT=catT[:, k, 0:n],
                                                              rhs=wout[:, k, cb * 512:(cb + 1) * 512],
                                                              start=(k == 0), stop=(k == 7)),
                           reads=[B_catT, B_wout], writes=[PSB[cb]], inc=(k == 7))
                for cb in range(2):
                    op("dve", lambda cb=cb: DVE.tensor_tensor(out=hst[b][0:n, cb * 512:(cb + 1) * 512], in0=PS[cb][0:n, :],
                                                              in1=xt2[b][0:n, cb * 512:(cb + 1) * 512], op=ALU.add),
                       reads=[PSB[cb], B_xt2[b]], writes=[B_hst[b]])
                dma("sp", h_sem[b], y_dst(i), hst[b][0:n, :], reads=[B_hst[b]], writes=[B_hd[i]])

            if True:
                NBt = 4
                n_tot = 4 * n_pages
                n_tiles = n_tot // 4
                tiles_per_seq = n_pages // 4
                Kb = [sb(f"Kb{i}", [128, 4, 512], BF16, p2) for i in range(NBt)]; B_Kb = [Buf(f"Kb{i}") for i in range(NBt)]
                Vb = [sb(f"Vb{i}", [128, 4, 512], BF16, p2) for i in range(NBt)]; B_Vb = [Buf(f"Vb{i}") for i in range(NBt)]
                kd_sem = [Sy.dma_sem(f"kd_sem{i}") for i in range(NBt)]
                vd_sem = [Sy.dma_sem(f"vd_sem{i}") for i in range(NBt)]
                KTb = [sb(f"KTb{i}", [128, 2, H, 128], BF16, p2) for i in range(2)]; B_KTb = [Buf(f"KTb{i}") for i in range(2)]
                eTd = [sb(f"eTd{i}", [128, 4, 32], BF16, p2) for i in range(2)]; B_eTd = [Buf(f"eTd{i}") for i in range(2)]
                qblk = sb("qblk", [128, 4, H, 8], BF16, p2); B_qblk = Buf("qblk")
                ones2 = sb("ones2", [128, 2], BF16, p2); B_ones2 = Buf("ones2")
                Vsb = sb("Vsb", [16, 512], BF16, p2); B_Vsb = Buf("Vsb")
                en = sb("en", [16, 32], F32, p2); B_en = Buf("en")
                enb = sb("enb", [16, 32], BF16, p2); B_enb = Buf("enb")
                Rr = sb("Rr", [32, H, 128], F32, p2); B_Rr = Buf("Rr")
                zt = sb("zt", [32, 8], F32, p2); B_zt = Buf("zt")
                R2 = sb("R2", [32, H, 128], F32, p2); B_R2 = Buf("R2")
                atts = sb("atts", [16, H, 128], F32, p2); B_atts = Buf("atts")
                R2h = sb("R2h", [32, H, 128], BF16, p2); B_R2h = Buf("R2h")
                R2l = sb("R2l", [32, H, 128], BF16, p2); B_R2l = Buf("R2l")
                selb = sb("selb", [32, 4, 16], BF16, p2); B_selb = Buf("selb")
                op("dve", lambda: DVE.tensor_copy(out=selb[:], in_=sel[:]), reads=[B_sel], writes=[B_selb])
                op("pool", lambda: POOL.memset(ones2[:], 1.0), writes=[B_ones2])
                op("dve", lambda: DVE.tensor_copy(out=Vsb[:, :].rearrange("p (h d) -> p h d", h=H), in_=Vs[:, :, 0:128]),
                   reads=[B_Vs], writes=[B_Vsb])
                op("pool", lambda: POOL.memset(qblk[:], 0.0), writes=[B_qblk])
                for c in range(2):
                    op("dve", lambda c=c: DVE.tensor_copy(
                        out=qblk[c * 64:(c + 1) * 64, :, :, c * 4:(c + 1) * 4],
                        in_=qTs[c * 64:(c + 1) * 64, :, :].rearrange("p h (s t) -> p s h t", s=4)),
                       reads=[B_qTs, B_qblk], writes=[B_qblk])
                ptbc = sb("ptbc", [128, n_tot], I32, p2); B_ptbc = Buf("ptbc")
                ptf = sb("ptf", [128, n_tiles, 4], F32, p2); B_ptf = Buf("ptf")
                m4 = sb("m4", [128, 4], F32, p2); B_m4 = Buf("m4")
                pv = sb("pv", [128, 4], F32, p2); B_pv = Buf("pv")
                pm = sb("pm", [128, 1], F32, p2); B_pm = Buf("pm")
                selp = sb("selp", [128, n_tiles], F32, p2); B_selp = Buf("selp")
                idx4 = sb("idx4", [128, n_tiles], I32, p2); B_idx = Buf("idx4")
                pt_sem = Sy.dma_sem("pt_sem")
                dma("sp", pt_sem, ptbc[:], bc(ptab, [[0, 128], [1, n_tot]]), writes=[B_ptbc])
                op("pool", lambda: POOL.memset(m4[:], 1.0), writes=[B_m4])
                op("pool", lambda: POOL.affine_select(out=m4[:], in_=m4[:], pattern=[[-32, 4]], compare_op=ALU.is_ge,
                                                      fill=0.0, base=0, channel_multiplier=1), reads=[B_m4], writes=[B_m4])
                op("pool", lambda: POOL.affine_select(out=m4[:], in_=m4[:], pattern=[[32, 4]], compare_op=ALU.is_ge,
                                                      fill=0.0, base=31, channel_multiplier=-1), reads=[B_m4], writes=[B_m4])
                op("pool", lambda: POOL.iota(pv[:], pattern=[[-32, 4]], base=0, channel_multiplier=1,
                                             allow_small_or_imprecise_dtypes=True), writes=[B_pv])
                op("dve", lambda: DVE.tensor_tensor(out=pv[:], in0=pv[:], in1=m4[:], op=ALU.mult),
                   reads=[B_pv, B_m4], writes=[B_pv])
                op("dve", lambda: DVE.tensor_reduce(out=pm[:], in_=pv[:], axis=AX.X, op=ALU.add), reads=[B_pv], writes=[B_pm])
                op("dve", lambda: DVE.tensor_copy(out=ptf[:].rearrange("p g q -> p (g q)"), in_=ptbc[:]),
                   reads=[B_ptbc], writes=[B_ptf])
                op("dve", lambda: DVE.tensor_tensor(out=ptf[:], in0=ptf[:],
                                                    in1=m4[:].unsqueeze(1).to_broadcast([128, n_tiles, 4]), op=ALU.mult),
                   reads=[B_ptf, B_m4], writes=[B_ptf])
                op("dve", lambda: DVE.tensor_reduce(out=selp[:], in_=ptf[:], axis=AX.X, op=ALU.add),
                   reads=[B_ptf], writes=[B_selp])
                op("dve", lambda: DVE.tensor_scalar(out=idx4[:], in0=selp[:], scalar1=32.0, scalar2=pm[:, 0:1],
                                                    op0=ALU.mult, op1=ALU.add), reads=[B_selp, B_pm], writes=[B_idx])
                ck_rows = cache_k.rearrange("n (r q) c -> (n r) (q c)", q=4)
                cv_rows = cache_v.rearrange("n (r q) c -> (n r) (q c)", q=4)

                def issue(g_):
                    if g_ >= n_tiles:
                        return
                    sl = g_ % NBt
                    Sy.dma_fn("pool", kd_sem[sl], lambda: POOL.indirect_dma_start(
                        out=Kb[sl][:].rearrange("p s c -> p (s c)"), out_offset=None, in_=ck_rows,
                        in_offset=bass.IndirectOffsetOnAxis(ap=idx4[:, g_:g_ + 1], axis=0)),
                        reads=[B_idx], writes=[B_Kb[sl]])
                    Sy.dma_fn("pool", vd_sem[sl], lambda: POOL.indirect_dma_start(
                        out=Vb[sl][:].rearrange("p s c -> p (s c)"), out_offset=None, in_=cv_rows,
                        in_offset=bass.IndirectOffsetOnAxis(ap=idx4[:, g_:g_ + 1], axis=0)),
                        reads=[B_idx], writes=[B_Vb[sl]])

                accO = sb("accO", [32, 512], F32, p2); B_accO = Buf("accO")
                accZs = sb("accZs", [32, 2], F32, p2); B_accZs = Buf("accZs")
                T_BANK, S_BANK, O_BANK = 5, 6, 7

                def do_Th(g_, half):
                    sl = g_ % NBt
                    ptk = ps_bf(T_BANK).rearrange("p (s h t) -> p s h t", s=2, h=H)
                    for s2 in range(2):
                        for h in range(H):
                            op("pe", lambda: TE.transpose(out=ptk[:, s2, h, :],
                                                          in_=Kb[sl][:, 2 * half + s2, h * 128:(h + 1) * 128],
                                                          identity=identb[:]),
                               reads=[B_Kb[sl], B_idb], writes=[PSB[T_BANK]], inc=(s2 == 1 and h == H - 1))
                    if half == 0:
                        op("act", lambda: ACT.copy(out=KTb[half][:], in_=ptk), reads=[PSB[T_BANK]], writes=[B_KTb[half]])
                    else:
                        op("dve", lambda: DVE.tensor_copy(out=KTb[half][:], in_=ptk), reads=[PSB[T_BANK]],
                           writes=[B_KTb[half]])

                def do_S(g_, half):
                    sidx = g_ // tiles_per_seq
                    for s2 in range(2):
                        s_ = 2 * half + s2
                        for h in range(H):
                            op("pe", lambda: TE.matmul(PS[S_BANK][:, s_ * 32 + h * 8:s_ * 32 + h * 8 + 8],
                                                       lhsT=KTb[half][:, s2, h, :], rhs=qblk[:, sidx, h, :],
                                                       start=True, stop=True),
                               reads=[B_KTb[half], B_qblk], writes=[PSB[S_BANK]], inc=(s2 == 1 and h == H - 1))

                def do_E(g_):
                    e = g_ % 2
                    op("act", lambda: ACT.activation(out=eTd[e][:].rearrange("p g x -> p (g x)"),
                                                     in_=PS[S_BANK][:, 0:128], func=AF.Exp, scale=0.125),
                       reads=[PSB[S_BANK]], writes=[B_eTd[e]])

                def acc_add(first):
                    if first:
                        op("dve", lambda: DVE.tensor_copy(out=accO[:, :], in_=PS[O_BANK][0:32, :]),
                           reads=[PSB[O_BANK]], writes=[B_accO])
                        op("dve", lambda: DVE.tensor_copy(out=accZs[:, :], in_=PS[O_BANK + 0][0:32, 0:2]) if False else
                           DVE.tensor_copy(out=accZs[:, :], in_=PS[S_BANK][0:32, 128:130]),
                           reads=[PSB[S_BANK]], writes=[B_accZs])
                    else:
                        op("dve", lambda: DVE.tensor_tensor(out=accO[:, :], in0=PS[O_BANK][0:32, :], in1=accO[:, :],
                                                            op=ALU.add), reads=[PSB[O_BANK], B_accO], writes=[B_accO])
                        op("dve", lambda: DVE.tensor_tensor(out=accZs[:, :], in0=PS[S_BANK][0:32, 128:130],
                                                            in1=accZs[:, :], op=ALU.add),
                           reads=[PSB[S_BANK], B_accZs], writes=[B_accZs])

                def do_AV(g_):
                    sl = g_ % NBt
                    e = g_ % 2
                    for s_ in range(4):
                        op("pe", lambda: TE.matmul(PS[O_BANK][0:32, :], lhsT=eTd[e][:, s_, :], rhs=Vb[sl][:, s_, :],
                                                   start=(s_ == 0), stop=(s_ == 3), skip_group_check=True),
                           reads=[B_eTd[e], B_Vb[sl]], writes=[PSB[O_BANK]], inc=False)
                        op("pe", lambda: TE.matmul(PS[S_BANK][0:32, 128:130], lhsT=eTd[e][:, s_, :], rhs=ones2[:, :],
                                                   start=False, stop=(s_ == 3), skip_group_check=True),
                           reads=[B_eTd[e], B_ones2], writes=[PSB[S_BANK]], inc=(s_ == 3))
                    acc_add(g_ % tiles_per_seq == 0)

                def seq_tail(sidx):
                    for h in range(H):
                        op("pe", lambda h=h: TE.matmul(PS[S_BANK][0:16, h * 8:(h + 1) * 8], lhsT=kTs[:, h, :],
                                                       rhs=qblk[:, sidx, h, :], start=True, stop=True),
                           reads=[B_kTs, B_qblk], writes=[PSB[S_BANK]], inc=(h == 3))
                    op("act", lambda: ACT.activation(out=en[:, :], in_=PS[S_BANK][0:16, 0:32], func=AF.Exp, scale=0.125),
                       reads=[PSB[S_BANK]], writes=[B_en])
                    op("dve", lambda: DVE.tensor_tensor(out=enb[:, :], in0=en[:, :],
                                                        in1=msk_s[:, sidx].rearrange("p a b -> p (a b)"), op=ALU.mult),
                       reads=[B_en, B_msk], writes=[B_enb])
                    op("pe", lambda: TE.matmul(PS[O_BANK][0:32, :], lhsT=enb[:, :], rhs=Vsb[:, :], start=True, stop=True,
                                               skip_group_check=True), reads=[B_enb, B_Vsb], writes=[PSB[O_BANK]], inc=False)
                    op("pe", lambda: TE.matmul(PS[S_BANK][0:32, 128:130], lhsT=enb[:, :], rhs=ones2[0:16, :],
                                               start=False, stop=True, skip_group_check=True),
                       reads=[B_enb, B_ones2], writes=[PSB[S_BANK]], inc=True)
                    acc_add(False)
                    op("dve", lambda: DVE.reciprocal(out=zt[:, 5:6], in_=accZs[:, 0:1]), reads=[B_accZs], writes=[B_zt])
                    op("dve", lambda: DVE.tensor_tensor(out=zt[:, 6:7], in0=zt[:, 5:6], in1=coef[:, 0:1], op=ALU.mult),
                       reads=[B_zt, B_coef], writes=[B_zt])
                    op("dve", lambda: DVE.scalar_tensor_tensor(out=R2[:], in0=accO[:, :].rearrange("p (a b) -> p a b", a=H),
                                                               scalar=zt[:, 6:7], in1=oneh[:], op0=ALU.mult, op1=ALU.mult),
                       reads=[B_accO, B_zt, B_oneh], writes=[B_R2])
                    op("dve", lambda: DVE.tensor_copy(out=R2h[:], in_=R2[:]), reads=[B_R2], writes=[B_R2h])
                    op("dve", lambda: DVE.tensor_tensor(out=R2l[:], in0=R2[:], in1=R2h[:], op=ALU.subtract),
                       reads=[B_R2, B_R2h], writes=[B_R2l])
                    op("pe", lambda: TE.matmul(PS[O_BANK][0:16, :], lhsT=selb[:, sidx, :],
                                               rhs=R2h[:].rearrange("p a b -> p (a b)"),
                                               start=True, stop=False, skip_group_check=True),
                       reads=[B_R2h, B_selb], writes=[PSB[O_BANK]], inc=False)
                    op("pe", lambda: TE.matmul(PS[O_BANK][0:16, :], lhsT=selb[:, sidx, :],
                                               rhs=R2l[:].rearrange("p a b -> p (a b)"),
                                               start=False, stop=True, skip_group_check=True),
                       reads=[B_R2l, B_selb], writes=[PSB[O_BANK]], inc=True)
                    if sidx == 0:
                        op("dve", lambda: DVE.tensor_copy(out=atts[:].rearrange("p a b -> p (a b)"), in_=PS[O_BANK][0:16, :]),
                           reads=[PSB[O_BANK]], writes=[B_atts])
                    else:
                        op("dve", lambda: DVE.tensor_tensor(out=atts[:].rearrange("p a b -> p (a b)"),
                                                            in0=PS[O_BANK][0:16, :],
                                                            in1=atts[:].rearrange("p a b -> p (a b)"), op=ALU.add),
                           reads=[PSB[O_BANK], B_atts], writes=[B_atts])

                def decode_gen():
                    for g_ in range(NBt - 1):
                        issue(g_)
                    for g_ in range(n_tiles):
                        do_Th(g_, 0)
                        if g_ % tiles_per_seq != 0:
                            do_AV(g_ - 1)
                        issue(g_ + NBt - 1)
                        yield
                        do_Th(g_, 1)
                        do_S(g_, 0)
                        yield
                        do_S(g_, 1)
                        do_E(g_)
                        if (g_ + 1) % tiles_per_seq == 0:
                            yield
                            do_AV(g_)
                            seq_tail(g_ // tiles_per_seq)
                        yield

                for k in range(8):
                    dma("pool", wo_sem, wout[:, k, :], w_out[k * 128:(k + 1) * 128, :], writes=[B_wout])
                dgen = decode_gen()
                n_ticks_total = n_tiles * 3 + 4
                ticks_done = [0]

                def decode_ticks(k):
                    for _ in range(k):
                        if ticks_done[0] >= n_ticks_total:
                            return
                        try:
                            next(dgen)
                        except StopIteration:
                            ticks_done[0] = n_ticks_total
                            return
                        ticks_done[0] += 1

            if stop_after >= 2:
                LOOK = 1
                steps = []
                grp_cnt = 0
                for j in range(4):
                    for h in range(H):
                        for c in range(2):
                            gs = grp_cnt % 2
                            grp_cnt += 1
                            na = 4 * j + 4
                            for a in range(na):
                                steps.append(dict(j=j, h=h, c=c, a=a, gs=gs, last=(a == na - 1),
                                                  lastgrp=(a == na - 1 and h == H - 1 and c == 1)))

                def emit_S(t, st):
                    j, h, c, a = st["j"], st["h"], st["c"], st["a"]
                    r0 = max(0, a - 4 * j)
                    ncols = 512 - 128 * r0
                    sbk = t % 2
                    eb = t % 4
                    st.update(r0=r0, ncols=ncols, sbk=sbk, eb=eb)
                    op("pe", lambda: TE.matmul(
                        PS[sbk][:, 0:ncols], lhsT=kT[:, h, a * 128:(a + 1) * 128],
                        rhs=qTz[:, c, h, j * 512 + r0 * 128:(j + 1) * 512], start=True, stop=True),
                       reads=[B_kT[a]] + B_qT[4 * j:4 * j + 4], writes=[PSB[sbk]], inc=True)
                    op("act", lambda: ACT.activation(out=eT[eb][:, 0:ncols], in_=PS[sbk][:, 0:ncols],
                                                     func=AF.Exp, scale=0.125),
                       reads=[PSB[sbk]], writes=[B_eT[eb]])
                    if a >= 4 * j:
                        op("dve", lambda: DVE.tensor_tensor(out=eT[eb][:, 0:128], in0=eT[eb][:, 0:128], in1=tri01[:, :],
                                                            op=ALU.mult),
                           reads=[B_eT[eb], B_tri], writes=[B_eT[eb]])

                def emit_AV(st):
                    j, h, c, a, gs = st["j"], st["h"], st["c"], st["a"], st["gs"]
                    r0, eb = st["r0"], st["eb"]
                    accs = (PS[2], PS[3])
                    Baccs = (PSB[2], PSB[3])
                    for r in range(r0, 4):
                        bank = accs[r // 2]
                        off = (r % 2) * 130
                        op("pe", lambda: TE.matmul(
                            bank[:, off:off + 130], lhsT=eT[eb][:, (r - r0) * 128:(r - r0 + 1) * 128],
                            rhs=V1[:, a, h, :], start=(a == 0 and r % 2 == 0), stop=(a == 4 * j + r),
                            skip_group_check=True),
                           reads=[B_eT[eb], B_V1[a]], writes=[Baccs[r // 2]], inc=(r == 3))
                    if not st["last"]:
                        return
                    for bi in range(2):
                        zv = accs[bi][:, 0:260].rearrange("p (r x) -> p r x", x=130)[:, :, 128]
                        op("dve", lambda: DVE.reciprocal(out=rz[gs][:, 2 * bi:2 * bi + 2], in_=zv),
                           reads=[Baccs[bi]], writes=[B_rz[gs]])
                    if c == 1:
                        op("dve", lambda: DVE.tensor_scalar(out=rz[gs][:, :], in0=rz[gs][:, :], scalar1=lam[:, 1:2],
                                                            scalar2=None, op0=ALU.mult),
                           reads=[B_rz[gs], B_lam], writes=[B_rz[gs]])
                    for r in range(4):
                        bank = accs[r // 2]
                        off = (r % 2) * 130
                        if c == 0:
                            op("act", lambda: ACT.activation(out=att[:, r, h, :], in_=bank[:, off:off + 128],
                                                             func=AF.Copy, scale=rz[gs][:, r:r + 1]),
                               reads=[Baccs[r // 2], B_rz[gs]], writes=[B_att[r]])
                        else:
                            op("dve", lambda: DVE.scalar_tensor_tensor(
                                out=att[:, r, h, :], in0=bank[:, off:off + 128], scalar=rz[gs][:, r:r + 1],
                                in1=att[:, r, h, :], op0=ALU.mult, op1=ALU.add),
                               reads=[Baccs[r // 2], B_rz[gs], B_att[r]], writes=[B_att[r]])
                    if st["lastgrp"]:
                        for r in range(4):
                            epilogue(4 * j + r, 128, att[:, r], B_att[r])

                for t in range(len(steps) + LOOK):
                    if t < len(steps):
                        emit_S(t, steps[t])
                    if t - LOOK >= 0:
                        emit_AV(steps[t - LOOK])
                    want = (min(t + 1, len(steps)) * n_ticks_total) // len(steps)
                    decode_ticks(want - ticks_done[0])
                decode_ticks(n_ticks_total)
                for _ in dgen:
                    pass
                epilogue(NT, NS, atts[:], B_atts)

            Sy.barrier()
        a_stack.close()
        cur[0] = es

        p3 = contextlib.ExitStack()
        with p3:
            if stop_after >= 4:
                gffn = sb("gffn", [128, D], F32, p3); B_gffn = Buf("gffn")
                g_sem = Sy.dma_sem("g_sem")
                dma("sp", g_sem, gffn[:], bc(norm_ffn, [[0, 128], [1, D]]), writes=[B_gffn])
                hres = sb("hres", [128, NT + 1, D], F32, p3); B_h = [Buf(f"h{i}") for i in range(NT + 1)]
                hl_sem = [Sy.dma_sem(f"hl_sem{i}") for i in range(2)]
                hn2T = sb("hn2T", [128, 8, NTOK], BF16, p3); B_hn2T = [Buf(f"hn2T{i}") for i in range(NT + 1)]
                wu = [sb(f"wu{i}", [128, 8, 1024], BF16, p3) for i in range(2)]; B_wu = [Buf(f"wu{i}") for i in range(2)]
                wd = [sb(f"wd{i}", [128, 8, 1024], BF16, p3) for i in range(2)]; B_wd = [Buf(f"wd{i}") for i in range(2)]
                wu_sem = [Sy.dma_sem(f"wu_sem{i}") for i in range(2)]
                wd_sem = [Sy.dma_sem(f"wd_sem{i}") for i in range(2)]
                hT = [sb(f"hT{i}", [128, 8, 512], BF16, p3) for i in range(2)]; B_hT = [Buf(f"hT{i}") for i in range(2)]
                rl = [sb(f"rl{i}", [128, 512], F32, p3) for i in range(2)]; B_rl = [Buf(f"rl{i}") for i in range(2)]
                ss3 = [sb(f"ss3{i}", [128, 1], F32, p3) for i in range(5)]; B_ss3 = [Buf(f"ss3{i}") for i in range(5)]
                rs3 = [sb(f"rs3{i}", [128, 1], F32, p3) for i in range(5)]; B_rs3 = [Buf(f"rs3{i}") for i in range(5)]
                hn3 = [sb(f"hn3{i}", [128, D], BF16, p3) for i in range(5)]; B_hn3 = [Buf(f"hn3{i}") for i in range(5)]
                junk3 = sb("junk3", [128, D], BF16, p3)
                y_sem = [Sy.dma_sem(f"y_sem{i}") for i in range(2)]

                def load_wu(qq):
                    s_ = qq % 2
                    for k in range(8):
                        dma("pool", wu_sem[s_], wu[s_][:, k, :], w_up[k * 128:(k + 1) * 128, qq * 1024:(qq + 1) * 1024],
                            writes=[B_wu[s_]])

                def load_wd(qq):
                    s_ = qq % 2
                    for f_ in range(8):
                        dma("pool", wd_sem[s_], wd[s_][:, f_, :],
                            w_down[qq * 1024 + f_ * 128:qq * 1024 + (f_ + 1) * 128, :], writes=[B_wd[s_]])

                load_wu(0)
                for i in range(NT + 1):
                    dma("sp", hl_sem[i % 2], hres[0:tile_rows(i), i, :], y_dst(i), reads=[B_hd[i]], writes=[B_h[i]])
                NB3 = 5

                def pro_pre(i):
                    n = tile_rows(i)
                    b = i % NB3
                    op("act", lambda: ACT.activation(out=junk3[0:n, :], in_=hres[0:n, i, :], func=AF.Square,
                                                     accum_out=ss3[b][0:n, :]), reads=[B_h[i]], writes=[B_ss3[b]])
                    rstd_from_ss(ss3[b][0:n, :], rs3[b][0:n, :], n, float(D), B_ss3[b], B_rs3[b], 1)
                    op("dve", lambda: DVE.scalar_tensor_tensor(out=hn3[b][0:n, :], in0=hres[0:n, i, :],
                                                               scalar=rs3[b][0:n, 0:1], in1=gffn[0:n, :],
                                                               op0=ALU.mult, op1=ALU.mult),
                       reads=[B_h[i], B_rs3[b], B_gffn], writes=[B_hn3[b]])

                def pro_T(i):
                    n = tile_rows(i)
                    b = i % NB3
                    pbk = 6 + i % 2
                    ptr = ps_bf(pbk).rearrange("p (k t) -> p k t", k=8)
                    for k in range(8):
                        op("pe", lambda k=k: TE.transpose(out=ptr[:, k, 0:n], in_=hn3[b][0:n, k * 128:(k + 1) * 128],
                                                          identity=identb[0:n, 0:n]),
                           reads=[B_hn3[b], B_idb], writes=[PSB[pbk]], inc=(k == 7))
                    op("act", lambda: ACT.copy(out=hn2T[:, :, i * 128:i * 128 + n], in_=ptr[:, :, 0:n]),
                       reads=[PSB[pbk]], writes=[B_hn2T[i]])

                def blk_tiles(tb):
                    return [4 * tb + r for r in range(4)] if tb < 4 else [NT]

                for i in blk_tiles(0):
                    pro_pre(i)
                load_wd(0)
                for i in blk_tiles(0):
                    pro_T(i)
                pd_cnt = 0
                hb_cnt = 0
                for qq in range(4):
                    s_ = qq % 2
                    for tb in range(5):
                        ntok = 512 if tb < 4 else NS
                        col0 = tb * 512
                        tiles = [4 * tb + r for r in range(4)] if tb < 4 else [NT]
                        if qq == 0 and tb + 1 < 5:
                            for i_ in blk_tiles(tb + 1):
                                pro_pre(i_)
                        hb = hb_cnt % 2
                        hb_cnt += 1
                        for f_ in range(8):
                            pb = f_ % 2
                            for k in range(8):
                                op("pe", lambda k=k: TE.matmul(PS[pb][:, 0:ntok], lhsT=wu[s_][:, k, f_ * 128:(f_ + 1) * 128],
                                                               rhs=hn2T[:, k, col0:col0 + ntok], start=(k == 0), stop=(k == 7)),
                                   reads=[B_wu[s_]] + [B_hn2T[t_] for t_ in tiles], writes=[PSB[pb]], inc=(k == 7))
                            op("act", lambda: ACT.activation(out=rl[pb][:, 0:ntok], in_=PS[pb][:, 0:ntok], func=AF.Relu),
                               reads=[PSB[pb]], writes=[B_rl[pb]])
                            op("dve", lambda: DVE.tensor_tensor(out=hT[hb][:, f_, 0:ntok], in0=rl[pb][:, 0:ntok],
                                                                in1=rl[pb][:, 0:ntok], op=ALU.mult),
                               reads=[B_rl[pb]], writes=[B_hT[hb]])
                        for ti, i in enumerate(tiles):
                            n = tile_rows(i)
                            for cb in range(2):
                                pd = 2 + pd_cnt % 4
                                pd_cnt += 1
                                for f_ in range(8):
                                    op("pe", lambda f_=f_: TE.matmul(PS[pd][0:n, :], lhsT=hT[hb][:, f_, ti * 128:ti * 128 + n],
                                                                     rhs=wd[s_][:, f_, cb * 512:(cb + 1) * 512],
                                                                     start=(f_ == 0), stop=(f_ == 7)),
                                       reads=[B_hT[hb], B_wd[s_]], writes=[PSB[pd]], inc=(f_ == 7))
                                op("dve", lambda: DVE.tensor_tensor(out=hres[0:n, i, cb * 512:(cb + 1) * 512],
                                                                    in0=PS[pd][0:n, :],
                                                                    in1=hres[0:n, i, cb * 512:(cb + 1) * 512], op=ALU.add),
                                   reads=[PSB[pd], B_h[i]], writes=[B_h[i]])
                            if qq == 3:
                                dma("sp", y_sem[i % 2], y_dst(i), hres[0:n, i, :], reads=[B_h[i]], writes=[B_hd[i]])
                        if qq == 0 and tb + 1 < 5:
                            for i_ in blk_tiles(tb + 1):
                                pro_T(i_)
                        if qq == 0 and tb == 0:
                            load_wu(1)
                        if qq == 0 and tb == 1:
                            load_wd(1)
                    if qq + 2 < 4:
                        load_wu(qq + 2)
                        load_wd(qq + 2)

        Sy.finish()
    return nc, Sy


_NC_CACHE = {}


def kernel(**inputs):
    n_cores = 8
    n_phys = inputs["cache_k"].shape[1]
    n_pages = inputs["page_table"].shape[1]
    key = (n_phys, n_pages)
    if key not in _NC_CACHE:
        _NC_CACHE[key] = build_nc(n_phys, n_pages)[0]
    nc = _NC_CACHE[key]
    f = lambda a: np.ascontiguousarray(a)
    ck = f(inputs["cache_k"]).reshape(n_phys, PAGE, 512)
    cv = f(inputs["cache_v"]).reshape(n_phys, PAGE, 512)
    shared = {
        "cache_k": ck, "cache_v": cv,
        "norm_mix": f(inputs["norm_mix"]).reshape(1, D),
        "w_in": f(inputs["w_in"]).reshape(D, INC),
        "q_gain": f(inputs["q_gain"]).reshape(1, 64), "k_gain": f(inputs["k_gain"]).reshape(1, 64),
        "lambda_q1": f(inputs["lambda_q1"]).reshape(1, 64), "lambda_k1": f(inputs["lambda_k1"]).reshape(1, 64),
        "lambda_q2": f(inputs["lambda_q2"]).reshape(1, 64), "lambda_k2": f(inputs["lambda_k2"]).reshape(1, 64),
        "subln_gain": f(inputs["subln_gain"]).reshape(1, 128),
        "gv_gain": f(inputs["gv_gain"]).reshape(1, 512),
        "w_spatial": f(inputs["w_spatial"]).reshape(4, 128, 128),
        "b_spatial": f(inputs["b_spatial"]).reshape(4, 128),
        "w_out": f(inputs["w_out"]).reshape(D, D),
        "norm_ffn": f(inputs["norm_ffn"]).reshape(1, D),
        "w_up": f(inputs["w_up"]).reshape(D, DFF),
        "w_down": f(inputs["w_down"]).reshape(DFF, D),
    }
    in_maps = []
    for c in range(n_cores):
        m = dict(shared)
        m["x_prompt"] = f(inputs["x_prompt"][c])
        m["x_sample"] = f(inputs["x_sample"][4 * c:4 * c + 4]).reshape(NS, D)
        m["page_table"] = f(inputs["page_table"][4 * c:4 * c + 4]).reshape(1, 4 * n_pages).astype(np.int32)
        in_maps.append(m)
    res = run_bass_kernel_spmd(nc, in_maps, core_ids=list(range(n_cores)))
    R = res.results
    cat = lambda name: np.stack([np.asarray(r[name]) for r in R], axis=0)
    y_p = cat("y_prompt").reshape(8, S, D)
    y_s = cat("y_sample").reshape(32, 4, D)
    k_p = cat("k_prompt").reshape(1, 8, S, 4, 2, 64)
    v_p = cat("v_prompt").reshape(1, 8, S, 4, 128)
    k_s = cat("k_sample").reshape(1, 32, 4, 4, 2, 64)
    v_s = cat("v_sample").reshape(1, 32, 4, 4, 128)
    g_s = cat("gv_sample").reshape(1, 32, 4, 512)
    return (y_p, y_s, k_p, v_p, k_s, v_s, g_s)
```
